# Optimizing a Trainium2 kernel written in Bass

```python
import math
import jax, jax.numpy as jnp
from jax import lax
import numpy as np

D_MODEL = 1024
BATCH = 8
SEQ = 2048
DEPTH = 2
DEC_BATCH = 128
DEC_SEQ = 8
PAST_LEN = 8192
PAGE_SIZE = 128

N_A = DEPTH // 2
N_B = DEPTH - N_A
CONV_K = 31
FFN_CONV_K = 3
D_FF = 2816
N_HEADS = 16
N_KV_HEADS = 4
HEAD_DIM = 64
GROUP = N_HEADS // N_KV_HEADS
WINDOW = 128
N_BUCKETS = 32
MAX_DISTANCE = 128
PLE_DIM = 256
EPS = 1e-6
SCALE = HEAD_DIM ** -0.5
NEG = -1e30

kernel_name = "yoco_conformer_swa_sink_decoder_step"


def _rmsnorm(x, g):
    xf = x.astype(jnp.float32)
    y = xf * lax.rsqrt(jnp.mean(xf * xf, axis=-1, keepdims=True) + EPS)
    return (y * g.astype(jnp.float32)).astype(x.dtype)


def _layernorm(x, g, b):
    xf = x.astype(jnp.float32)
    mu = jnp.mean(xf, axis=-1, keepdims=True)
    var = jnp.mean(jnp.square(xf - mu), axis=-1, keepdims=True)
    y = (xf - mu) * lax.rsqrt(var + EPS)
    return (y * g.astype(jnp.float32) + b.astype(jnp.float32)).astype(x.dtype)


def _dwconv_valid(xh, w):
    return lax.conv_general_dilated(xh, w[:, None, :].astype(xh.dtype), window_strides=(1,), padding='VALID',
                                    dimension_numbers=('NWC', 'WIO', 'NWC'),
                                    feature_group_count=xh.shape[-1])


def _conv_module(xn, hist, w_pw1, b_pw1, w_dw, b_dw, ln_g, ln_b, w_pw2, b_pw2):
    u = xn @ w_pw1 + b_pw1
    a, g = jnp.split(u, 2, axis=-1)
    u = a * jax.nn.sigmoid(g)
    if hist is None:
        hist = jnp.zeros((u.shape[0], CONV_K - 1, u.shape[-1]), u.dtype)
    uh = jnp.concatenate([hist.astype(u.dtype), u], axis=1)
    c = _dwconv_valid(uh, w_dw) + b_dw
    c = jax.nn.silu(_layernorm(c, ln_g, ln_b))
    return c @ w_pw2 + b_pw2, uh[:, -(CONV_K - 1):]


def _conv_ffn(xn, hist, w_up, w_dw, b_dw, w_down):
    u = xn @ w_up
    if hist is None:
        hist = jnp.zeros((u.shape[0], FFN_CONV_K - 1, u.shape[-1]), u.dtype)
    uh = jnp.concatenate([hist.astype(u.dtype), u], axis=1)
    c = _dwconv_valid(uh, w_dw) + b_dw
    gate, val = jnp.split(c, 2, axis=-1)
    return (jax.nn.silu(gate) * val) @ w_down, uh[:, -(FFN_CONV_K - 1):]


def _ple(h, p, g, w_gate, w_proj):
    gate = jax.nn.sigmoid(_rmsnorm(h, g) @ w_gate)
    return h + gate * (p @ w_proj)


def _t5_bucket(dist):
    max_exact = N_BUCKETS // 2
    d = jnp.maximum(dist, 0)
    df = jnp.maximum(d, 1).astype(jnp.float32)
    large = max_exact + (jnp.log(df / max_exact) / math.log(MAX_DISTANCE / max_exact)
                         * (N_BUCKETS - max_exact)).astype(jnp.int32)
    large = jnp.minimum(large, N_BUCKETS - 1)
    return jnp.where(d < max_exact, d, large)


def _rel_bias(dist, table):
    vals = table[_t5_bucket(dist)].astype(jnp.float32)
    return jnp.moveaxis(vals, -1, 0).reshape(N_KV_HEADS, GROUP, *dist.shape)


def _sink_softmax(logits, sinks):
    s = sinks.astype(jnp.float32).reshape(N_KV_HEADS, GROUP)[:, :, None, None]
    m = jnp.maximum(jnp.max(logits, axis=-1, keepdims=True), s)
    e = jnp.exp(logits - m)
    return e / (jnp.sum(e, axis=-1, keepdims=True) + jnp.exp(s - m))


def _swa_prompt(q, k, v, table, sinks):
    Bq, T = q.shape[0], q.shape[1]
    W = WINDOW
    nb = T // W
    qb = q.reshape(Bq, nb, W, N_KV_HEADS, GROUP, HEAD_DIM)
    pad = jnp.zeros((Bq, W, N_KV_HEADS, HEAD_DIM), k.dtype)
    kb = jnp.concatenate([pad, k], axis=1).reshape(Bq, nb + 1, W, N_KV_HEADS, HEAD_DIM)
    vb = jnp.concatenate([pad.astype(v.dtype), v], axis=1).reshape(Bq, nb + 1, W, N_KV_HEADS, HEAD_DIM)
    k_band = jnp.concatenate([kb[:, :-1], kb[:, 1:]], axis=2)
    v_band = jnp.concatenate([vb[:, :-1], vb[:, 1:]], axis=2)
    scores = jnp.einsum('bnqkgd,bnskd->bnkgqs', qb, k_band,
                        preferred_element_type=jnp.float32) * SCALE
    qi = jnp.arange(W)[:, None]
    sj = jnp.arange(2 * W)[None, :]
    dist = qi + W - sj
    kpos = jnp.arange(nb)[:, None, None] * W + sj[None] - W
    valid = (dist >= 0) & (dist < WINDOW) & (kpos >= 0)
    logits = jnp.where(valid[None, :, None, None], scores + _rel_bias(dist, table), NEG)
    probs = _sink_softmax(logits, sinks)
    out = jnp.einsum('bnkgqs,bnskd->bnqkgd', probs.astype(v.dtype), v_band)
    return out.reshape(Bq, T, N_HEADS * HEAD_DIM)


def _swa_sample(q, k_all, v_all, table, sinks):
    Bq, S = q.shape[0], q.shape[1]
    L = k_all.shape[1]
    WB = L - S
    qg = q.reshape(Bq, S, N_KV_HEADS, GROUP, HEAD_DIM)
    scores = jnp.einsum('bqkgd,bskd->bkgqs', qg, k_all,
                        preferred_element_type=jnp.float32) * SCALE
    qpos = PAST_LEN + jnp.arange(S)
    kpos = PAST_LEN - WB + jnp.arange(L)
    dist = qpos[:, None] - kpos[None, :]
    valid = (dist >= 0) & (dist < WINDOW)
    logits = jnp.where(valid, scores + _rel_bias(dist, table), NEG)
    probs = _sink_softmax(logits, sinks)
    out = jnp.einsum('bkgqs,bskd->bqkgd', probs.astype(v_all.dtype), v_all)
    return out.reshape(Bq, S, N_HEADS * HEAD_DIM)


def _trunk(x, p, conv_state, ffn_state, kc, vc, prm):
    prompt = conv_state is None
    Bx, T, _ = x.shape
    h = x
    new_conv, new_ffn = [], []
    k_use = v_use = new_k = new_v = None
    for i in range(DEPTH):
        hn = _rmsnorm(h, prm['g_mix'][i])
        if i < N_A:
            o, ch = _conv_module(hn, None if prompt else conv_state[i],
                                 prm['cm_w_pw1'][i], prm['cm_b_pw1'][i], prm['cm_w_dw'][i],
                                 prm['cm_b_dw'][i], prm['cm_ln_g'][i], prm['cm_ln_b'][i],
                                 prm['cm_w_pw2'][i], prm['cm_b_pw2'][i])
            new_conv.append(ch)
        else:
            b = i - N_A
            q = (hn @ prm['at_w_q'][b]).reshape(Bx, T, N_HEADS, HEAD_DIM)
            q = _rmsnorm(q, prm['at_g_q'][b])
            if prompt:
                o = _swa_prompt(q, k_use, v_use, prm['rel_bias'], prm['at_sinks'][b])
            else:
                o = _swa_sample(q, k_use, v_use, prm['rel_bias'], prm['at_sinks'][b])
            o = o @ prm['at_w_o'][b]
        h = h + o
        f, fh = _conv_ffn(_rmsnorm(h, prm['g_ffn'][i]), None if prompt else ffn_state[i],
                          prm['ffn_w_up'][i], prm['ffn_w_dw'][i], prm['ffn_b_dw'][i],
                          prm['ffn_w_down'][i])
        new_ffn.append(fh)
        h = h + f
        h = _ple(h, p[i], prm['g_ple'][i], prm['ple_w_gate'][i], prm['ple_w_proj'][i])
        if i == N_A - 1:
            hk = _rmsnorm(h, prm['kv_g'])
            k_sh = _rmsnorm((hk @ prm['kv_w_k']).reshape(Bx, T, N_KV_HEADS, HEAD_DIM), prm['kv_g_k'])
            v_sh = (hk @ prm['kv_w_v']).reshape(Bx, T, N_KV_HEADS, HEAD_DIM)
            if prompt:
                k_use, v_use = k_sh, v_sh
                keep = min(WINDOW, T)
            else:
                k_use = jnp.concatenate([kc.astype(k_sh.dtype), k_sh], axis=1)
                v_use = jnp.concatenate([vc.astype(v_sh.dtype), v_sh], axis=1)
                keep = kc.shape[1]
            new_k = k_use[:, -keep:]
            new_v = v_use[:, -keep:]
    return h, jnp.stack(new_conv), jnp.stack(new_ffn), new_k, new_v


def setup_inputs(seed: int = 0) -> dict:
    key = jax.random.key(seed)
    ks = iter(jax.random.split(key, 48))

    def nrm(shape, scale):
        return jax.random.normal(next(ks), shape, jnp.float32) * scale

    def gain(shape):
        return 1.0 + nrm(shape, 0.05)

    D, F = D_MODEL, D_FF
    QD = N_HEADS * HEAD_DIM
    KD = N_KV_HEADS * HEAD_DIM
    WB = min(WINDOW, PAST_LEN)
    return {
        'x_prompt': nrm((BATCH, SEQ, D), 1.0),
        'x_sample': nrm((DEC_BATCH, DEC_SEQ, D), 1.0),
        'state_conv': nrm((N_A, DEC_BATCH, CONV_K - 1, D), 0.5),
        'state_ffn': nrm((DEPTH, DEC_BATCH, FFN_CONV_K - 1, 2 * F), 0.5),
        'cache_k': nrm((DEC_BATCH, WB, N_KV_HEADS, HEAD_DIM), 1.0),
        'cache_v': nrm((DEC_BATCH, WB, N_KV_HEADS, HEAD_DIM), 1.0),
        'p_prompt': nrm((DEPTH, BATCH, SEQ, PLE_DIM), 1.0),
        'p_sample': nrm((DEPTH, DEC_BATCH, DEC_SEQ, PLE_DIM), 1.0),
        'g_mix': gain((DEPTH, D)),
        'cm_w_pw1': nrm((N_A, D, 2 * D), D ** -0.5),
        'cm_b_pw1': nrm((N_A, 2 * D), 0.02),
        'cm_w_dw': nrm((N_A, CONV_K, D), CONV_K ** -0.5),
        'cm_b_dw': nrm((N_A, D), 0.02),
        'cm_ln_g': gain((N_A, D)),
        'cm_ln_b': nrm((N_A, D), 0.02),
        'cm_w_pw2': nrm((N_A, D, D), 0.5 * D ** -0.5),
        'cm_b_pw2': nrm((N_A, D), 0.02),
        'at_w_q': nrm((N_B, D, QD), D ** -0.5),
        'at_g_q': gain((N_B, HEAD_DIM)),
        'at_sinks': nrm((N_B, N_HEADS), 1.0),
        'at_w_o': nrm((N_B, QD, D), 0.5 * QD ** -0.5),
        'kv_g': gain((D,)),
        'kv_w_k': nrm((D, KD), D ** -0.5),
        'kv_w_v': nrm((D, KD), D ** -0.5),
        'kv_g_k': gain((HEAD_DIM,)),
        'rel_bias': nrm((N_BUCKETS, N_HEADS), 0.5),
        'g_ffn': gain((DEPTH, D)),
        'ffn_w_up': nrm((DEPTH, D, 2 * F), D ** -0.5),
        'ffn_w_dw': nrm((DEPTH, FFN_CONV_K, 2 * F), FFN_CONV_K ** -0.5),
        'ffn_b_dw': nrm((DEPTH, 2 * F), 0.02),
        'ffn_w_down': nrm((DEPTH, F, D), 0.5 * F ** -0.5),
        'g_ple': gain((DEPTH, D)),
        'ple_w_gate': nrm((DEPTH, D, D), D ** -0.5),
        'ple_w_proj': nrm((DEPTH, PLE_DIM, D), 0.5 * PLE_DIM ** -0.5),
    }


def reference(x_prompt, x_sample, state_conv, state_ffn, cache_k, cache_v, p_prompt, p_sample,
              g_mix, cm_w_pw1, cm_b_pw1, cm_w_dw, cm_b_dw, cm_ln_g, cm_ln_b, cm_w_pw2, cm_b_pw2,
              at_w_q, at_g_q, at_sinks, at_w_o, kv_g, kv_w_k, kv_w_v, kv_g_k, rel_bias,
              g_ffn, ffn_w_up, ffn_w_dw, ffn_b_dw, ffn_w_down, g_ple, ple_w_gate, ple_w_proj):
    prm = {
        'g_mix': g_mix, 'cm_w_pw1': cm_w_pw1, 'cm_b_pw1': cm_b_pw1, 'cm_w_dw': cm_w_dw,
        'cm_b_dw': cm_b_dw, 'cm_ln_g': cm_ln_g, 'cm_ln_b': cm_ln_b, 'cm_w_pw2': cm_w_pw2,
        'cm_b_pw2': cm_b_pw2, 'at_w_q': at_w_q, 'at_g_q': at_g_q, 'at_sinks': at_sinks,
        'at_w_o': at_w_o, 'kv_g': kv_g, 'kv_w_k': kv_w_k, 'kv_w_v': kv_w_v, 'kv_g_k': kv_g_k,
        'rel_bias': rel_bias, 'g_ffn': g_ffn, 'ffn_w_up': ffn_w_up, 'ffn_w_dw': ffn_w_dw,
        'ffn_b_dw': ffn_b_dw, 'ffn_w_down': ffn_w_down, 'g_ple': g_ple,
        'ple_w_gate': ple_w_gate, 'ple_w_proj': ple_w_proj,
    }
    y_prompt, conv_p, ffn_p, k_p, v_p = _trunk(x_prompt, p_prompt, None, None, None, None, prm)
    y_sample, conv_s, ffn_s, k_s, v_s = _trunk(x_sample, p_sample, state_conv, state_ffn,
                                               cache_k, cache_v, prm)
    return (y_prompt, y_sample, conv_p, conv_s, ffn_p, ffn_s, k_p, k_s, v_p, v_s)
```

```python
import math
import os
import numpy as np
import concourse.bass as bass
import concourse.mybir as mybir
from concourse.bass_utils import run_bass_kernel_spmd

F32 = mybir.dt.float32
BF16 = mybir.dt.bfloat16
U8 = mybir.dt.uint8
AF = mybir.ActivationFunctionType
ALU = mybir.AluOpType

NCORES = 8
D = 1024
DFF = 2816
NF = 44
T = 2176
TT = [(0, 512), (512, 512), (1024, 512), (1536, 512), (2048, 128)]
NBLK = 17
EPS = 1e-6
SCALE = 0.125
LU = 383
GROUPS = [(0, 3), (3, 3), (6, 3), (9, 3), (12, 3), (15, 3), (18, 2), (20, 2)]
SLOT = 9216
UW = 30 + 2048 + 16 * 38
SB0 = 2078


def _vec_layout():
    ents = [("g_mix", 16), ("b_pw1", 16), ("w_dw", 248), ("b_dw", 8), ("ln_g", 8), ("ln_b", 8), ("b_pw2", 8),
            ("kv_g", 8), ("g_ffn", 16), ("f_w_dw", 264), ("f_b_dw", 88), ("g_ple", 16), ("gq2", 1), ("gk2", 1),
            ("sinks", 8)]
    off = {}
    o = 0
    for n, c in ents:
        off[n] = o
        o += c
    return off, o


VOFF, VN = _vec_layout()
VROWS = 768


def _chunk_heads(c):
    gp, j = c // 4, c % 4
    return 8 * gp + j, 8 * gp + 4 + j


class Res:
    __slots__ = ("w", "r", "name")

    def __init__(self, name=""):
        self.w = None
        self.r = []
        self.name = name


class Prog:
    def __init__(self, nc, n_dma_sems=24):
        self.nc = nc
        self.names = ["pe", "act", "dve", "pool", "sp"]
        self.ops = {n: [] for n in self.names}
        self.cnt = {n: 0 for n in self.names}
        self.seen = {n: {} for n in self.names}
        self.sem = {}
        for n in ["pe", "act", "dve", "pool"]:
            self.sem[n] = nc.alloc_semaphore("s_" + n)
        self.dq = ["sp", "pool"]
        self.dsem = {q: [nc.alloc_semaphore(f"d_{q}{i}") for i in range(n_dma_sems)] for q in self.dq}
        self.dcnt = {q: [0] * n_dma_sems for q in self.dq}
        self.dnext = {q: 0 for q in self.dq}

    def _waits(self, eng, evs):
        best = {}
        for ev in evs:
            if ev is None:
                continue
            s, v = ev
            k = id(s)
            if k not in best or best[k][1] < v:
                best[k] = (s, v)
        out = []
        seen = self.seen[eng]
        for k, (s, v) in best.items():
            if seen.get(k, 0) >= v:
                continue
            seen[k] = v
            out.append((s, v))
        return out

    @staticmethod
    def _deps(reads, writes, extra):
        evs = list(extra)
        for r in reads:
            evs.append(r.w)
        for w in writes:
            evs.append(w.w)
            evs.extend(w.r)
        return evs

    @staticmethod
    def _commit(ev, reads, writes):
        for r in reads:
            r.r.append(ev)
        for w in writes:
            w.w = ev
            w.r = []

    def op(self, eng, fn, reads=(), writes=(), extra=()):
        waits = self._waits(eng, self._deps(reads, writes, extra))
        self.cnt[eng] += 1
        sem = self.sem[eng]
        ev = (sem, self.cnt[eng])

        def run(e):
            for s, v in waits:
                e.wait_ge(s, v)
            fn(e).then_inc(sem, 1)

        self.ops[eng].append(run)
        self._commit(ev, reads, writes)
        return ev

    def dma(self, q, out, in_, reads=(), writes=(), extra=()):
        i = self.dnext[q]
        self.dnext[q] = (i + 1) % len(self.dsem[q])
        s = self.dsem[q][i]
        prev = self.dcnt[q][i]
        evs = self._deps(reads, writes, extra)
        if prev > 0:
            evs.append((s, prev))
        waits = self._waits(q, evs)
        self.dcnt[q][i] = prev + 16
        ev = (s, prev + 16)

        def run(e):
            for s2, v in waits:
                e.wait_ge(s2, v)
            e.dma_start(out=out, in_=in_).then_inc(s, 16)

        self.ops[q].append(run)
        self._commit(ev, reads, writes)
        return ev

    def finish(self):
        nc = self.nc
        finals = [(s, c) for q in self.dq for s, c in zip(self.dsem[q], self.dcnt[q]) if c > 0]
        waits = self._waits("sp", finals)

        def fin(e):
            for s, v in waits:
                e.wait_ge(s, v)

        self.ops["sp"].append(fin)
        with nc.Block() as block:
            @block.tensor
            def _(e):
                for f in self.ops["pe"]:
                    f(e)

            @block.scalar
            def _(e):
                for f in self.ops["act"]:
                    f(e)

            @block.vector
            def _(e):
                for f in self.ops["dve"]:
                    f(e)

            @block.gpsimd
            def _(e):
                for f in self.ops["pool"]:
                    f(e)

            @block.sync
            def _(e):
                for f in self.ops["sp"]:
                    f(e)


class Buf:
    def __init__(self, ap, off, nbytes, inherit, name):
        self.ap = ap
        self.off = off
        self.nbytes = nbytes
        self.inherit = inherit
        self.name = name
        self.R = {}

    def res(self, key=None):
        if key not in self.R:
            r = Res(f"{self.name}:{key}")
            r.r = list(self.inherit)
            self.R[key] = r
        return self.R[key]

    def all_events(self):
        evs = list(self.inherit)
        for r in self.R.values():
            if r.w is not None:
                evs.append(r.w)
            evs.extend(r.r)
        return evs


class Arena:
    def __init__(self, nc, nbytes):
        self.t = nc.alloc_sbuf_tensor("arena", [128, nbytes], U8)
        self.n = nbytes
        self.live = []
        self.dead = []
        self.peak = 0

    def alloc(self, name, shape, dtype):
        esz = 4 if dtype == F32 else 2
        n = 1
        for s in shape:
            n *= s
        nbytes = ((n * esz + 63) // 64) * 64
        spans = sorted((b.off, b.off + b.nbytes) for b in self.live)
        off = 0
        for a, b in spans:
            if off + nbytes <= a:
                break
            off = max(off, b)
        if off + nbytes > self.n:
            raise RuntimeError(f"arena OOM allocating {name} ({nbytes}); live={[(b.name, b.nbytes) for b in self.live]}")
        self.peak = max(self.peak, off + nbytes)
        inherit = []
        keep = []
        for (o2, n2, evs) in self.dead:
            if o2 < off + nbytes and off < o2 + n2:
                inherit.extend(evs)
                if not (off <= o2 and o2 + n2 <= off + nbytes):
                    keep.append((o2, n2, evs))
            else:
                keep.append((o2, n2, evs))
        self.dead = keep
        v = self.t[:, off:off + n * esz].bitcast(dtype)
        if len(shape) == 2:
            v = v.rearrange("p (a b) -> p a b", a=shape[0])
        elif len(shape) == 3:
            v = v.rearrange("p (a b c) -> p a b c", a=shape[0], b=shape[1])
        elif len(shape) == 4:
            v = v.rearrange("p (a b c d) -> p a b c d", a=shape[0], b=shape[1], c=shape[2])
        buf = Buf(v, off, nbytes, inherit, name)
        self.live.append(buf)
        return buf

    def free(self, buf):
        self.live.remove(buf)
        self.dead.append((buf.off, buf.nbytes, buf.all_events()))


def build_program():
    nc = bass.Bass("TRN2", target_bir_lowering=False)
    P = Prog(nc)

    def din(name, shape):
        return nc.dram_tensor(name, list(shape), F32, kind="ExternalInput").ap()

    def dout(name, shape):
        return nc.dram_tensor(name, list(shape), F32, kind="ExternalOutput").ap()

    x_all = din("x_all", [T, D])
    p_all = din("p_all", [2, T, 256])
    st_conv = din("st_conv", [16, 30, D])
    st_ffn = din("st_ffn", [2, 32, 2 * DFF])
    cache_k = din("cache_k", [16, 128, 256])
    cache_v = din("cache_v", [16, 128, 256])
    vecs = din("vecs", [VROWS, 128])
    cst = din("cst", [128, 256])
    ohu = din("ohu", [33, LU])
    rel_bias = din("rel_bias", [32, 16])
    w_pw1 = din("w_pw1", [D, 2 * D])
    w_pw2 = din("w_pw2", [D, D])
    w_q = din("w_q", [D, D])
    w_o = din("w_o", [D, D])
    w_k = din("w_k", [D, 256])
    w_v = din("w_v", [D, 256])
    w_up = din("w_up", [2, D, 2 * DFF])
    w_down = din("w_down", [2, DFF, D])
    w_gate = din("w_gate", [2, D, D])
    w_proj = din("w_proj", [2, 256, D])

    y = dout("y", [T, D])
    conv_p = dout("conv_p", [30, D])
    conv_s = dout("conv_s", [16, 30, D])
    ffn_p = dout("ffn_p", [2, 2, 2 * DFF])
    ffn_s = dout("ffn_s", [2, 32, 2 * DFF])
    k_p = dout("k_p", [128, 256])
    k_s = dout("k_s", [16, 128, 256])
    v_p = dout("v_p", [128, 256])
    v_s = dout("v_s", [16, 128, 256])
    sc2 = nc.dram_tensor("sc2", [16, 128 * (LU + 1)], F32, kind="Internal")

    A = Arena(nc, 211968)
    _stop = os.environ.get("KSTOP", "")

    def stop(tag):
        if _stop == tag:
            P.finish()
            return True
        return False
    banks = [nc.alloc_psum_tensor(f"bank{i}", [128, 512], F32) for i in range(8)]
    bank_res = [Res(f"bank{i}") for i in range(8)]
    bstate = {"n": 0, "held": set()}

    def bank(hold=False):
        while True:
            i = bstate["n"] % 8
            bstate["n"] += 1
            if i not in bstate["held"]:
                break
        if hold:
            bstate["held"].add(i)
        return banks[i], bank_res[i]

    def release(bk):
        bstate["held"].discard(banks.index(bk))

    def ACT(out, in_, func, reads, writes, bias=0.0, scale=1.0):
        return P.op("act", lambda e: e.activation(out=out, in_=in_, func=func, bias=bias, scale=scale), reads, writes)

    def TTo(eng, out, in0, in1, op, reads, writes):
        return P.op(eng, lambda e: e.tensor_tensor(out=out, in0=in0, in1=in1, op=op), reads, writes)

    def TS(eng, out, in0, s1, s2, op0, op1, reads, writes):
        if s2 is None:
            return P.op(eng, lambda e: e.tensor_scalar(out=out, in0=in0, scalar1=s1, scalar2=None, op0=op0), reads, writes)
        return P.op(eng, lambda e: e.tensor_scalar(out=out, in0=in0, scalar1=s1, scalar2=s2, op0=op0, op1=op1), reads, writes)

    def STT(eng, out, in0, scalar, in1, op0, op1, reads, writes, extra=()):
        return P.op(eng, lambda e: e.scalar_tensor_tensor(out=out, in0=in0, scalar=scalar, in1=in1, op0=op0, op1=op1),
                    reads, writes, extra)

    def RSTD(out, in_, reads, writes):
        ACT(out, in_, AF.Ln, list(reads) + [Rcb], writes, bias=epsc)
        return ACT(out, out, AF.Exp, writes, writes, scale=-0.5)

    def CP(eng, out, in_, reads, writes):
        if eng == "act":
            return P.op("act", lambda e: e.activation(out=out, in_=in_, func=AF.Copy), reads, writes)
        return P.op(eng, lambda e: e.tensor_copy(out=out, in_=in_), reads, writes)

    def MEMSET(eng, out, val, writes):
        return P.op(eng, lambda e: e.memset(out, val), (), writes)

    def MM(mms, reads, writes):
        def fn(e):
            last = None
            for m in mms:
                kw = {}
                if m.get("tp") is not None:
                    kw["tile_position"] = m["tp"]
                if m.get("sgc"):
                    kw["skip_group_check"] = True
                last = e.matmul(m["out"], lhsT=m["lhsT"], rhs=m["rhs"], start=m["start"], stop=m["stop"], **kw)
            return last
        return P.op("pe", fn, reads, writes)

    def TR(items, reads, writes):
        def fn(e):
            last = None
            for o, i, idn in items:
                last = e.transpose(o, i, idn)
            return last
        return P.op("pe", fn, reads, writes)

    def acc(out, pairs, tp=None):
        n = len(pairs)
        return [dict(out=out, lhsT=l, rhs=r, start=(k == 0), stop=(k == n - 1), tp=tp) for k, (l, r) in enumerate(pairs)]

    hB = A.alloc("h", [8, T], F32)
    h = hB.ap
    Rh = [[hB.res((i, c)) for c in range(8)] for i in range(5)]
    XN = {}

    def xn_alloc():
        XN["B"] = A.alloc("xn", [8, T], BF16)
        return XN["B"].ap, [XN["B"].res(i) for i in range(5)]

    def xn_free():
        A.free(XN["B"])

    xn, Rxn = xn_alloc()
    WB = A.alloc("W", [2 * SLOT], BF16)
    W = WB.ap
    RW = [WB.res(0), WB.res(1)]
    vtB = A.alloc("vt", [VROWS], F32)
    vt = vtB.ap
    Rvt = vtB.res()
    cB = A.alloc("consts", [2, 128], F32)
    identf = cB.ap[:, 0, :]
    bmask = cB.ap[:, 1, :]
    Rc = cB.res()
    cbB = A.alloc("constsb", [4, 128], BF16)
    identb = cbB.ap[:, 0, :]
    onesN = cbB.ap[:, 1, :]
    ones64 = cbB.ap[:, 2, :]
    ones1 = cbB.ap[:, 3, :]
    Rcb = cbB.res()

    def vcol(name, idx=0):
        c = VOFF[name] + idx
        return vt[:, c:c + 1]

    P.dma("sp", cB.ap[:, 0:2, :], cst.rearrange("p (a b) -> p a b", a=2), writes=[Rc])
    CP("dve", identb, identf, [Rc], [Rcb])
    MEMSET("dve", onesN, 1.0 / 1024.0, [Rcb])
    MEMSET("dve", ones64, 0.0, [Rcb])
    MEMSET("dve", cbB.ap[0:64, 2, 0:64], 1.0 / 64.0, [Rcb])
    MEMSET("dve", cbB.ap[64:128, 2, 64:128], 1.0 / 64.0, [Rcb])
    MEMSET("dve", ones1, 1.0, [Rcb])
    epsB = A.alloc("epsc", [1], F32)
    epsc = epsB.ap[:, 0:1]
    MEMSET("dve", epsc, EPS, [Rcb])

    vinB = A.alloc("vin", [6, 128], F32)
    Rvin = vinB.res()
    P.dma("sp", vinB.ap, vecs.rearrange("(a p) f -> p a f", p=128), writes=[Rvin])
    for half in range(2):
        bk, br = bank()
        TR([(bk[:, q * 128:(q + 1) * 128], vinB.ap[:, half * 3 + q, :], identf) for q in range(3)], [Rvin, Rc], [br])
        CP("dve", vt[:, half * 384:(half + 1) * 384], bk[:, 0:384], [br], [Rvt])
    A.free(vinB)

    def load_rows(dst3, src2, nk, slots):
        for kc in range(nk):
            P.dma("pool", dst3[:, kc, :], src2[kc * 128:(kc + 1) * 128, :], writes=slots)

    def wview(off, a, b):
        return W[:, off:off + a * b].rearrange("p (a b) -> p a b", a=a)

    def load_pw1():
        load_rows(wview(0, 8, 2048), w_pw1, 8, [RW[0], RW[1]])

    def load_pw2():
        load_rows(wview(0, 8, 1024), w_pw2, 8, [RW[0]])

    def load_ffn(l, gi, slot):
        j0, nj = GROUPS[gi]
        base = slot * SLOT
        wu = wview(base, 8, 2 * nj * 128)
        for kc in range(8):
            P.dma("pool", wu[:, kc, 0:nj * 128], w_up[l, kc * 128:(kc + 1) * 128, j0 * 128:(j0 + nj) * 128], writes=[RW[slot]])
            P.dma("pool", wu[:, kc, nj * 128:2 * nj * 128],
                  w_up[l, kc * 128:(kc + 1) * 128, DFF + j0 * 128:DFF + (j0 + nj) * 128], writes=[RW[slot]])
        wd = wview(base + 16 * nj * 128, nj, 1024)
        for jj in range(nj):
            P.dma("pool", wd[:, jj, :], w_down[l, (j0 + jj) * 128:(j0 + jj + 1) * 128, :], writes=[RW[slot]])
        return wu, wd

    def build_eb():
        tbB = A.alloc("tbl", [16], F32)
        Rtb = tbB.res()
        ohB = A.alloc("ohu", [LU], F32)
        Roh = ohB.res()
        UB = A.alloc("U", [LU], F32)
        RU = UB.res()
        MEMSET("dve", tbB.ap[32:33, :], -30000.0, [Rtb])
        P.dma("sp", tbB.ap[0:32, :], rel_bias, writes=[Rtb])
        P.dma("sp", ohB.ap[0:33, :], ohu, writes=[Roh])
        bk, br = bank()
        MM([dict(out=bk[0:16, 0:LU], lhsT=tbB.ap[0:33, :], rhs=ohB.ap[0:33, :], start=True, stop=True)], [Rtb, Roh], [br])
        ACT(UB.ap[0:16, :], bk[0:16, 0:LU], AF.Exp, [br], [RU])
        Rsc = Res("sc2")
        pst = UB.ap.ap[0][0]
        P.dma("sp", bass.AP(sc2, 0, [[128 * (LU + 1), 16], [LU + 1, 128], [1, LU]]),
              bass.AP(UB.ap.tensor, UB.ap.offset, [[pst, 16], [0, 128], [1, LU]]), reads=[RU], writes=[Rsc])
        EBc = A.alloc("EBc", [16, 128], BF16)
        EBp = A.alloc("EBp", [16, 128], BF16)
        P.dma("pool", EBc.ap, bass.AP(sc2, 127, [[LU, 128], [128 * (LU + 1), 16], [1, 128]]), reads=[Rsc], writes=[EBc.res()])
        P.dma("pool", EBp.ap, bass.AP(sc2, 255, [[LU, 128], [128 * (LU + 1), 16], [1, 128]]), reads=[Rsc], writes=[EBp.res()])
        A.free(tbB)
        A.free(ohB)
        A.free(UB)
        return EBc, EBp

    load_pw1()

    xinB = A.alloc("xin", [2, D], F32)
    for blk in range(NBLK):
        i = min(blk // 4, 4)
        rx = xinB.res(blk % 2)
        P.dma("sp", xinB.ap[:, blk % 2, :], x_all[blk * 128:(blk + 1) * 128, :], writes=[rx])
        for half in range(2):
            bk, br = bank()
            TR([(bk[:, q * 128:(q + 1) * 128], xinB.ap[:, blk % 2, (4 * half + q) * 128:(4 * half + q + 1) * 128], identf)
                for q in range(4)], [rx, Rc], [br])
            CP("act" if half == 0 else "dve", h[:, 4 * half:4 * half + 4, blk * 128:(blk + 1) * 128],
               bk[:, :].rearrange("p (a b) -> p a b", a=4), [br], [Rh[i][c] for c in range(4 * half, 4 * half + 4)])
    A.free(xinB)
    if stop("p0"):
        return nc

    def rmsnorm(i, gname, gidx):
        c0, n = TT[i]
        sq = A.alloc("sq", [8, n], BF16)
        rs = A.alloc("rstd", [n], F32)
        ACT(sq.ap, h[:, :, c0:c0 + n], AF.Square, Rh[i], [sq.res()])
        bk, br = bank()
        MM(acc(bk[:, 0:n], [(onesN, sq.ap[:, c, :]) for c in range(8)]), [sq.res(), Rcb], [br])
        RSTD(rs.ap, bk[:, 0:n], [br], [rs.res()])
        for c in range(8):
            STT("dve", xn[:, c, c0:c0 + n], h[:, c, c0:c0 + n], vcol(gname, gidx * 8 + c), rs.ap,
                ALU.mult, ALU.mult, [Rh[i][c], rs.res(), Rvt], [Rxn[i]])
        A.free(sq)
        A.free(rs)

    uhB = A.alloc("uh", [8, UW], BF16)
    uh = uhB.ap
    Ru = [uhB.res(i) for i in range(5)]
    Rhist = uhB.res("hist")
    ustB = A.alloc("ust", [8, 158], F32)
    Rust = ustB.res()
    MEMSET("pool", uh[:, :, 0:30], 0.0, [Ru[0]])

    P.dma("sp", conv_s[:, 0:22, :], st_conv[:, 8:30, :])
    scB = A.alloc("scin", [D], F32)
    for blk in range(4):
        rsb = scB.res()
        P.dma("sp", scB.ap[0:120, :], st_conv[4 * blk:4 * blk + 4, :, :].rearrange("b r d -> (b r) d"), writes=[rsb])
        for half in range(2):
            bk, br = bank()
            TR([(bk[:, q * 120:(q + 1) * 120], scB.ap[0:120, (4 * half + q) * 128:(4 * half + q + 1) * 128], identf[0:120, 0:120])
                for q in range(4)], [rsb, Rc], [br])
            for q in range(4):
                c = 4 * half + q
                dst = uh[:, c, SB0 + 38 * 4 * blk:SB0 + 38 * 4 * (blk + 1)].rearrange("p (b w) -> p b w", w=38)[:, :, 0:30]
                CP("dve" if q % 2 == 0 else "act", dst, bk[:, q * 120:(q + 1) * 120].rearrange("p (b r) -> p b r", r=30),
                   [br], [Rhist])
    A.free(scB)

    for i in range(5):
        rmsnorm(i, "g_mix", 0)

    wp1 = wview(0, 8, 2048)
    sgB = A.alloc("sg", [2, 512], F32)
    for i in range(5):
        c0, n = TT[i]
        for m in range(8):
            ba, bra = bank()
            bg, brg = bank()
            MM(acc(ba[:, 0:n], [(wp1[:, kc, m * 128:(m + 1) * 128], xn[:, kc, c0:c0 + n]) for kc in range(8)]),
               [RW[0], RW[1], Rxn[i]], [bra])
            MM(acc(bg[:, 0:n], [(wp1[:, kc, (8 + m) * 128:(9 + m) * 128], xn[:, kc, c0:c0 + n]) for kc in range(8)]),
               [RW[0], RW[1], Rxn[i]], [brg])
            rsg = sgB.res(m % 2)
            sg = sgB.ap[:, m % 2, 0:n]
            ACT(sg, bg[:, 0:n], AF.Sigmoid, [brg, Rvt], [rsg], bias=vcol("b_pw1", 8 + m))
            if i < 4:
                dst = uh[:, m, 30 + c0:30 + c0 + n]
                STT("dve", dst, ba[:, 0:n], vcol("b_pw1", m), sg, ALU.add, ALU.mult, [bra, rsg, Rvt], [Ru[i]])
                if i == 3:
                    STT("dve", ustB.ap[:, m, 0:30], ba[:, 482:512], vcol("b_pw1", m), sg[:, 482:512], ALU.add, ALU.mult,
                        [bra, rsg, Rvt], [Rust])
            else:
                dst = uh[:, m, SB0:UW].rearrange("p (b w) -> p b w", w=38)[:, :, 30:38]
                STT("dve", dst, ba[:, 0:128].rearrange("p (b t) -> p b t", t=8), vcol("b_pw1", m),
                    sg.rearrange("p (b t) -> p b t", t=8), ALU.add, ALU.mult, [bra, rsg, Rvt], [Ru[4]])
                STT("dve", ustB.ap[:, m, 30:158], ba[:, 0:128], vcol("b_pw1", m), sg, ALU.add, ALU.mult,
                    [bra, rsg, Rvt], [Rust])
    A.free(sgB)
    if stop("p1a"):
        return nc
    load_pw2()

    cpoB = A.alloc("cpo", [D], F32)
    for half in range(2):
        bk, br = bank()
        TR([(bk[0:30, q * 128:(q + 1) * 128], ustB.ap[:, 4 * half + q, 0:30], identf) for q in range(4)], [Rust, Rc], [br])
        CP("act", cpoB.ap[0:30, half * 512:(half + 1) * 512], bk[0:30, :], [br], [cpoB.res()])
    P.dma("sp", conv_p, cpoB.ap[0:30, :], reads=[cpoB.res()])
    csoB = A.alloc("cso", [D], F32)
    for half in range(2):
        bk, br = bank()
        TR([(bk[:, q * 128:(q + 1) * 128], ustB.ap[:, 4 * half + q, 30:158], identf) for q in range(4)], [Rust, Rc], [br])
        CP("act", csoB.ap[:, half * 512:(half + 1) * 512], bk[:, :], [br], [csoB.res()])
    for b in range(16):
        P.dma("sp", conv_s[b, 22:30, :], csoB.ap[8 * b:8 * b + 8, :], reads=[csoB.res()])
    A.free(cpoB)
    A.free(csoB)
    A.free(ustB)
    if stop("p1"):
        return nc

    xn_free()
    wp2 = wview(0, 8, 1024)
    ffw = {}
    c2B = A.alloc("c2t", [8, 512], BF16)
    Rc2 = c2B.res()
    DgB = A.alloc("Dg", [2, 31, 128], BF16)
    ccB = A.alloc("cc", [8, 512], F32)
    cbfB = A.alloc("cbf", [2, 512], BF16)
    sqcB = A.alloc("sqc", [2, 512], BF16)
    lnB = A.alloc("lnst", [3, 512], F32)
    wdw_pst = vt.ap[0][0]
    for i in range(5):
        c0, n = TT[i]
        bm, brm = bank(hold=True)
        bq, brq = bank(hold=True)
        for c in range(8):
            rdg = DgB.res(c % 2)
            NDV = 18
            for (eng_, k0, k1) in (("dve", 0, NDV), ("pool", NDV, 31)):
                in0 = bass.AP(identb.tensor, identb.offset, [[identb.ap[0][0], 128], [0, k1 - k0], [1, 128]])
                wsrc = vcol("w_dw", k0 * 8 + c)
                in1 = bass.AP(wsrc.tensor, wsrc.offset, [[wdw_pst, 128], [8, k1 - k0], [0, 128]])
                TTo(eng_, DgB.ap[:, c % 2, k0:k1, :], in0, in1, ALU.mult, [Rcb, Rvt], [DgB.res((c % 2, eng_))])
            bk, br = bank()
            if i < 4:
                rhs = [uh[:, c, c0 + k:c0 + k + n] for k in range(31)]
                rds = [Ru[i]] + ([Ru[i - 1]] if i > 0 else [])
            else:
                sv = uh[:, c, SB0:UW].rearrange("p (b w) -> p b w", w=38)
                rhs = [sv[:, :, k:k + 8] for k in range(31)]
                rds = [Ru[4], Rhist]
            outv = bk[:, 0:n] if i < 4 else bk[:, 0:128].rearrange("p (b t) -> p b t", t=8)
            MM(acc(outv, [(DgB.ap[:, c % 2, k, :], rhs[k]) for k in range(31)]),
               [DgB.res((c % 2, "dve")), DgB.res((c % 2, "pool"))] + rds, [br])
            rcc = ccB.res(c)
            ACT(ccB.ap[:, c, 0:n], bk[:, 0:n], AF.Identity, [br, Rvt], [rcc], bias=vcol("b_dw", c))
            CP("pool", cbfB.ap[:, c % 2, 0:n], ccB.ap[:, c, 0:n], [rcc], [cbfB.res(c % 2)])
            ACT(sqcB.ap[:, c % 2, 0:n], ccB.ap[:, c, 0:n], AF.Square, [rcc], [sqcB.res(c % 2)])
            MM([dict(out=bm[:, 0:n], lhsT=onesN, rhs=cbfB.ap[:, c % 2, 0:n], start=(c == 0), stop=(c == 7))],
               [cbfB.res(c % 2), Rcb], [brm])
            MM([dict(out=bq[:, 0:n], lhsT=onesN, rhs=sqcB.ap[:, c % 2, 0:n], start=(c == 0), stop=(c == 7))],
               [sqcB.res(c % 2), Rcb], [brq])
        mean = lnB.ap[:, 0, 0:n]
        m2 = lnB.ap[:, 1, 0:n]
        rstd = lnB.ap[:, 2, 0:n]
        rln = lnB.res()
        CP("act", mean, bm[:, 0:n], [brm], [rln])
        TTo("dve", m2, mean, mean, ALU.mult, [rln], [rln])
        TTo("dve", m2, bq[:, 0:n], m2, ALU.subtract, [brq, rln], [rln])
        RSTD(rstd, m2, [rln], [rln])
        release(bm)
        release(bq)
        for c in range(8):
            rcc = ccB.res(c)
            TTo("pool", ccB.ap[:, c, 0:n], ccB.ap[:, c, 0:n], mean, ALU.subtract, [rcc, rln], [rcc])
            TTo("dve", ccB.ap[:, c, 0:n], ccB.ap[:, c, 0:n], rstd, ALU.mult, [rcc, rln], [rcc])
            ACT(c2B.ap[:, c, 0:n], ccB.ap[:, c, 0:n], AF.Silu, [rcc, Rvt], [Rc2], bias=vcol("ln_b", c),
                scale=vcol("ln_g", c))
        if i == 3:
            ffw[(0, 0)] = load_ffn(0, 0, 1)
        for m in range(8):
            bk, br = bank()
            MM(acc(bk[:, 0:n], [(wp2[:, kc, m * 128:(m + 1) * 128], c2B.ap[:, kc, 0:n]) for kc in range(8)]),
               [RW[0], Rc2], [br])
            STT("dve", h[:, m, c0:c0 + n], bk[:, 0:n], vcol("b_pw2", m), h[:, m, c0:c0 + n], ALU.add, ALU.add,
                [br, Rvt], [Rh[i][m]])
    for b_ in (DgB, ccB, cbfB, sqcB, lnB, c2B):
        A.free(b_)
    A.free(uhB)
    xn, Rxn = xn_alloc()
    if stop("p3"):
        return nc

    def proj_residual(wv, i, rslots, bias_name=None):
        c0, n = TT[i]
        for m in range(8):
            bk, br = bank()
            MM(acc(bk[:, 0:n], [(wv[:, kc, m * 128:(m + 1) * 128], xn[:, kc, c0:c0 + n]) for kc in range(8)]),
               rslots + [Rxn[i]], [br])
            if bias_name is not None:
                STT("dve", h[:, m, c0:c0 + n], bk[:, 0:n], vcol(bias_name, m), h[:, m, c0:c0 + n], ALU.add, ALU.add,
                    [br, Rvt], [Rh[i][m]])
            else:
                TTo("dve", h[:, m, c0:c0 + n], bk[:, 0:n], h[:, m, c0:c0 + n], ALU.add, [br], [Rh[i][m]])


    def ffn(l, slot0, after_first_group=None):
        for i in range(5):
            rmsnorm(i, "g_ffn", l)
        fstB = A.alloc("fst", [NF, 34], F32)
        Rfst = fstB.res()
        upB = A.alloc("uprev", [3, NF, 2], F32)
        hsB = A.alloc("hs", [NF, 32], F32)
        Rhs = hsB.res()
        sfB = A.alloc("sfin", [1408], F32)
        for q4 in range(4):
            rsf = sfB.res()
            P.dma("sp", sfB.ap[0:32, :], st_ffn[l, :, q4 * 1408:(q4 + 1) * 1408], writes=[rsf])
            bk, br = bank()
            TR([(bk[:, q * 32:(q + 1) * 32], sfB.ap[0:32, q * 128:(q + 1) * 128], identf[0:32, 0:32]) for q in range(11)],
               [rsf, Rc], [br])
            CP("dve", hsB.ap[:, q4 * 11:(q4 + 1) * 11, :], bk[:, 0:352].rearrange("p (a b) -> p a b", a=11), [br], [Rhs])
        A.free(sfB)
        cgB = A.alloc("cg", [2, 2, 512], F32)
        sgtB = A.alloc("sgt", [2, 512], F32)
        actB = A.alloc("act", [2, 4, 512], BF16)
        ueB = A.alloc("uext", [2, 2, 514], F32)
        units = [(gi, i) for gi in range(len(GROUPS)) for i in range(5)]

        def unit_ctx(k):
            gi, i = units[k]
            j0, nj = GROUPS[gi]
            slot = (slot0 + gi) % 2
            wu, wd = ffw[(l, gi)]
            c0, n = TT[i]
            return gi, i, j0, nj, slot, wu, wd, c0, n, actB.res(k % 2)

        def emit_up(k):
            gi, i, j0, nj, slot, wu, wd, c0, n, ract = unit_ctx(k)
            for jj in range(nj):
                j = j0 + jj
                par = (jj + i) % 2
                cvals = []
                for gv in range(2):
                    f = j + 22 * gv
                    bk, br = bank()
                    MM(acc(bk[:, 0:n], [(wu[:, kc, (gv * nj + jj) * 128:(gv * nj + jj + 1) * 128], xn[:, kc, c0:c0 + n])
                                        for kc in range(8)]), [RW[slot], Rxn[i]], [br])
                    rcg = cgB.res((par, gv))
                    cg = cgB.ap[:, par, gv, 0:n]
                    w0 = vcol("f_w_dw", (l * 3 + 0) * NF + f)
                    w1 = vcol("f_w_dw", (l * 3 + 1) * NF + f)
                    w2 = vcol("f_w_dw", (l * 3 + 2) * NF + f)
                    bb = vcol("f_b_dw", l * NF + f)
                    rue = ueB.res((par, gv))
                    ue = ueB.ap[:, par, gv, :]
                    if i < 4:
                        CP("act", ue[:, 2:2 + n], bk[:, 0:n], [br], [rue])
                        ACT(cg, bk[:, 0:n], AF.Identity, [br, Rvt], [rcg], bias=bb, scale=w2)
                        if i == 0:
                            MEMSET("pool", ue[:, 0:2], 0.0, [rue])
                        else:
                            CP("pool", ue[:, 0:2], upB.ap[:, i - 1, f, :], [upB.res((i - 1, f))], [rue])
                        if i < 3:
                            CP("pool", upB.ap[:, i, f, :], ue[:, n:n + 2], [rue], [upB.res((i, f))])
                        else:
                            CP("pool", fstB.ap[:, f, 0:2], ue[:, n:n + 2], [rue], [Rfst])
                        STT("dve", cg, ue[:, 1:1 + n], w1, cg, ALU.mult, ALU.add, [rue, Rvt], [rcg])
                        STT("dve", cg, ue[:, 0:n], w0, cg, ALU.mult, ALU.add, [rue, Rvt], [rcg])
                    else:
                        ue3 = ue[:, 0:160].rearrange("p (b t) -> p b t", t=10)
                        cg3 = cg.rearrange("p (b t) -> p b t", t=8)
                        bk3 = bk[:, 0:128].rearrange("p (b t) -> p b t", t=8)
                        hs3 = hsB.ap[:, f, :].rearrange("p (b r) -> p b r", r=2)
                        CP("act", ue3[:, :, 2:10], bk3, [br], [rue])
                        ACT(cg, bk[:, 0:n], AF.Identity, [br, Rvt], [rcg], bias=bb, scale=w2)
                        CP("pool", ue3[:, :, 0:2], hs3, [Rhs], [rue])
                        CP("pool", fstB.ap[:, f, 2:34].rearrange("p (b r) -> p b r", r=2), ue3[:, :, 8:10], [rue], [Rfst])
                        STT("dve", cg3, ue3[:, :, 1:9], w1, cg3, ALU.mult, ALU.add, [rue, Rvt], [rcg])
                        STT("dve", cg3, ue3[:, :, 0:8], w0, cg3, ALU.mult, ALU.add, [rue, Rvt], [rcg])
                    cvals.append((cg, rcg))
                rsg = sgtB.res(par)
                sgt = sgtB.ap[:, par, 0:n]
                ACT(sgt, cvals[0][0], AF.Silu, [cvals[0][1]], [rsg])
                TTo("pool", actB.ap[:, k % 2, jj, 0:n], sgt, cvals[1][0], ALU.mult, [rsg, cvals[1][1]], [ract])

        def emit_down(k):
            gi, i, j0, nj, slot, wu, wd, c0, n, ract = unit_ctx(k)
            for m in range(8):
                bk, br = bank()
                MM(acc(bk[:, 0:n], [(wd[:, jj, m * 128:(m + 1) * 128], actB.ap[:, k % 2, jj, 0:n]) for jj in range(nj)]),
                   [RW[slot], ract], [br])
                TTo("dve", h[:, m, c0:c0 + n], bk[:, 0:n], h[:, m, c0:c0 + n], ALU.add, [br], [Rh[i][m]])

        if len(GROUPS) > 1:
            ffw[(l, 1)] = load_ffn(l, 1, (slot0 + 1) % 2)
        emit_up(0)
        for k in range(len(units)):
            if k + 1 < len(units):
                emit_up(k + 1)
            emit_down(k)
            gi, i = units[k]
            if i == 4 and gi + 2 < len(GROUPS):
                ffw[(l, gi + 2)] = load_ffn(l, gi + 2, (slot0 + gi) % 2)
        for b_ in (cgB, sgtB, actB, upB, hsB, ueB):
            A.free(b_)
        ostB = A.alloc("ost", [1408], F32)
        for q4 in range(4):
            rost = ostB.res()
            for t3 in range(3):
                nq = 4 if t3 < 2 else 3
                bk, br = bank()
                TR([(bk[0:34, q * 128:(q + 1) * 128], fstB.ap[:, q4 * 11 + t3 * 4 + q, :], identf) for q in range(nq)],
                   [Rfst, Rc], [br])
                CP("act", ostB.ap[0:34, t3 * 512:t3 * 512 + nq * 128], bk[0:34, 0:nq * 128], [br], [rost])
            P.dma("sp", ffn_p[l, :, q4 * 1408:(q4 + 1) * 1408], ostB.ap[0:2, :], reads=[rost])
            P.dma("sp", ffn_s[l, :, q4 * 1408:(q4 + 1) * 1408], ostB.ap[2:34, :], reads=[rost])
        A.free(ostB)
        A.free(fstB)

    def load_ple(l, slot):
        wg = wview(slot * SLOT, 8, 1024)
        wp = wview((1 - slot) * SLOT + 4096, 2, 1024)
        load_rows(wg, w_gate[l], 8, [RW[slot]])
        load_rows(wp, w_proj[l], 2, [RW[1 - slot]])
        return wg, wp

    def ple(l, slot, wg, wp, prefetch=None):
        pTB = A.alloc("pT", [2, T], BF16)
        pinB = A.alloc("pin", [2, 256], F32)
        RpT = [pTB.res(i) for i in range(5)]
        for blk in range(NBLK):
            i = min(blk // 4, 4)
            rp = pinB.res(blk % 2)
            P.dma("sp", pinB.ap[:, blk % 2, :], p_all[l, blk * 128:(blk + 1) * 128, :], writes=[rp])
            bk, br = bank()
            TR([(bk[:, q * 128:(q + 1) * 128], pinB.ap[:, blk % 2, q * 128:(q + 1) * 128], identf) for q in range(2)],
               [rp, Rc], [br])
            CP("act", pTB.ap[:, :, blk * 128:(blk + 1) * 128], bk[:, 0:256].rearrange("p (a b) -> p a b", a=2), [br], [RpT[i]])
        A.free(pinB)
        for i in range(5):
            rmsnorm(i, "g_ple", l)
        if prefetch is not None:
            prefetch()
        gtB = A.alloc("gt", [2, 512], F32)
        tmB = A.alloc("tm", [2, 512], F32)
        for i in range(5):
            c0, n = TT[i]
            for m in range(8):
                bg, brg = bank()
                bp, brp = bank()
                MM(acc(bg[:, 0:n], [(wg[:, kc, m * 128:(m + 1) * 128], xn[:, kc, c0:c0 + n]) for kc in range(8)]),
                   [RW[slot], Rxn[i]], [brg])
                MM(acc(bp[:, 0:n], [(wp[:, kc, m * 128:(m + 1) * 128], pTB.ap[:, kc, c0:c0 + n]) for kc in range(2)]),
                   [RW[1 - slot], RpT[i]], [brp])
                rgt = gtB.res(m % 2)
                rtm = tmB.res(m % 2)
                ACT(gtB.ap[:, m % 2, 0:n], bg[:, 0:n], AF.Sigmoid, [brg], [rgt])
                TTo("dve", tmB.ap[:, m % 2, 0:n], bp[:, 0:n], gtB.ap[:, m % 2, 0:n], ALU.mult, [brp, rgt], [rtm])
                TTo("pool", h[:, m, c0:c0 + n], h[:, m, c0:c0 + n], tmB.ap[:, m % 2, 0:n], ALU.add, [rtm], [Rh[i][m]])
        for b_ in (gtB, tmB, pTB):
            A.free(b_)

    ple_w = {}

    def pf_ple0():
        ple_w[0] = load_ple(0, 1)

    ffn(0, 1, after_first_group=None)
    if stop("ffn0"):
        return nc
    pf_ple0()

    kvw = {}

    def load_kv():
        base = 0
        wk = wview(base, 8, 256)
        wv = wview(base + 2048, 8, 256)
        load_rows(wk, w_k, 8, [RW[0]])
        load_rows(wv, w_v, 8, [RW[0]])
        kvw["k"], kvw["v"] = wk, wv

    ple(0, 1, ple_w[0][0], ple_w[0][1], prefetch=load_kv)
    if stop("ple0"):
        return nc

    KTB = A.alloc("KT", [2, T], BF16)
    KT = KTB.ap
    RKT = [KTB.res(i) for i in range(5)]
    VtB = A.alloc("Vtok", [NBLK, 256], BF16)
    Vt = VtB.ap
    RVt = [VtB.res(b) for b in range(NBLK)]
    for i in range(5):
        rmsnorm(i, "kv_g", 0)

    wqo = {}

    def load_wq():
        wqo["q"] = wview(SLOT, 8, 1024)
        stg = A.alloc("wqstg", [8, 1024], BF16)
        load_rows(stg.ap, w_q, 8, [stg.res()])
        for gp in range(2):
            for hf in range(2):
                dst = wqo["q"][:, :, gp * 512:(gp + 1) * 512].rearrange("p k (j hf d) -> p k j hf d", j=4, hf=2)[:, :, :, hf, :]
                src = stg.ap[:, :, gp * 512:(gp + 1) * 512].rearrange("p k (hf j d) -> p k hf j d", hf=2, j=4)[:, :, hf, :, :]
                CP("pool", dst, src, [stg.res()], [RW[1]])
        A.free(stg)

    load_wq()
    if stop("kv0"):
        return nc
    knoB = A.alloc("kno", [2, 256], F32)
    Rkno = knoB.res()
    kfB = A.alloc("kf", [2, 512], F32)
    ksqB = A.alloc("ksq", [2, 512], BF16)
    krsB = A.alloc("krs", [2, 512], F32)
    for i in range(5):
        c0, n = TT[i]
        for gp in range(2):
            bk, br = bank()
            MM(acc(bk[:, 0:n], [(kvw["k"][:, kc, gp * 128:(gp + 1) * 128], xn[:, kc, c0:c0 + n]) for kc in range(8)]),
               [RW[0], Rxn[i]], [br])
            rkf, rsq, rrs = kfB.res(gp), ksqB.res(gp), krsB.res(gp)
            kf = kfB.ap[:, gp, 0:n]
            CP("act", kf, bk[:, 0:n], [br], [rkf])
            ACT(ksqB.ap[:, gp, 0:n], bk[:, 0:n], AF.Square, [br], [rsq])
            b2, br2 = bank()
            MM([dict(out=b2[:, 0:n], lhsT=ones64, rhs=ksqB.ap[:, gp, 0:n], start=True, stop=True)], [rsq, Rcb], [br2])
            RSTD(krsB.ap[:, gp, 0:n], b2[:, 0:n], [br2], [rrs])
            STT("dve", KT[:, gp, c0:c0 + n], kf, vcol("gk2"), krsB.ap[:, gp, 0:n], ALU.mult, ALU.mult, [rkf, rrs, Rvt], [RKT[i]])
            if i == 3:
                STT("dve", knoB.ap[:, gp, 0:128], kf[:, 384:512], vcol("gk2"), krsB.ap[:, gp, 384:512], ALU.mult, ALU.mult,
                    [rkf, rrs, Rvt], [Rkno])
            if i == 4:
                STT("dve", knoB.ap[:, gp, 128:256], kf, vcol("gk2"), krsB.ap[:, gp, 0:128], ALU.mult, ALU.mult,
                    [rkf, rrs, Rvt], [Rkno])
    for b_ in (kfB, ksqB, krsB):
        A.free(b_)
    if stop("kv1"):
        return nc
    voB = A.alloc("vo", [2, 256], F32)
    for blk in range(NBLK):
        i = min(blk // 4, 4)
        bk, br = bank()
        MM(acc(bk[:, 0:256], [(xn[:, kc, blk * 128:(blk + 1) * 128], kvw["v"][:, kc, :]) for kc in range(8)]),
           [RW[0], Rxn[i]], [br])
        CP("act", Vt[:, blk, :], bk[:, 0:256], [br], [RVt[blk]])
        if blk >= 15 and os.environ.get("KSUB", "") != "a":
            rvo = voB.res(blk - 15)
            CP("act", voB.ap[:, blk - 15, :], bk[:, 0:256], [br], [rvo])
            if os.environ.get("KSUB", "") == "c":
                pass
            elif blk == 15:
                P.dma("sp", v_p, voB.ap[:, 0, :], reads=[rvo])
            elif os.environ.get("KSUB", "") != "b":
                for b in range(16):
                    P.dma("sp", v_s[b, 120:128, :], voB.ap[8 * b:8 * b + 8, 1, :], reads=[rvo])
    A.free(voB)
    if stop("kv2"):
        return nc
    koB = A.alloc("ko", [2, 256], F32)
    for w in range(2):
        bk, br = bank()
        TR([(bk[:, gp * 128:(gp + 1) * 128], knoB.ap[:, gp, w * 128:(w + 1) * 128], identf) for gp in range(2)], [Rkno, Rc], [br])
        rko = koB.res(w)
        CP("act", koB.ap[:, w, :], bk[:, 0:256], [br], [rko])
        if w == 0:
            P.dma("sp", k_p, koB.ap[:, 0, :], reads=[rko])
        else:
            for b in range(16):
                P.dma("sp", k_s[b, 120:128, :], koB.ap[8 * b:8 * b + 8, 1, :], reads=[rko])
    A.free(koB)
    A.free(knoB)
    P.dma("sp", k_s[:, 0:120, :], cache_k[:, 8:128, :])
    P.dma("sp", v_s[:, 0:120, :], cache_v[:, 8:128, :])
    if stop("kv"):
        return nc

    EBc, EBp = build_eb()
    for i in range(5):
        rmsnorm(i, "g_mix", 1)
    esB = A.alloc("es", [8], F32)
    Res_es = esB.res()
    ACT(esB.ap, vt[:, VOFF["sinks"]:VOFF["sinks"] + 8], AF.Exp, [Rvt], [Res_es])

    EB_ = A.alloc("E", [2, 512], F32)
    PTB = A.alloc("PT", [2, 512], BF16)
    denB = A.alloc("den", [2, 512], F32)
    EBn = A.alloc("EBn", [16, 128], BF16)
    for hh in range(16):
        TTo("pool", EBn.ap[:, hh, :], EBc.ap[:, hh, :], bmask, ALU.mult, [EBc.res(), Rc], [EBn.res()])
    EBs = A.alloc("EBs", [4, 16, 4, 8], BF16)
    for b in range(16):
        CP("pool", EBs.ap[:, :, b, :, :], EBp.ap[:, :, 0:8].rearrange("p (g j) q -> p g j q", g=4), [EBp.res()], [EBs.res()])
    class _View:
        def __init__(self, ap, r):
            self.ap = ap
            self._r = r

        def res(self, key=None):
            return self._r

    KcT = _View(W[:, 0:4096].rearrange("p (b g s) -> p b g s", b=16, g=2), RW[0])
    VcB = _View(W[:, 4096:8192].rearrange("p (b f) -> p b f", b=16), RW[0])
    P.dma("pool", VcB.ap, cache_v.rearrange("b s f -> s b f"), writes=[VcB.res()])
    ckB = A.alloc("ck", [8, 256], F32)
    for hb in range(2):
        rck = ckB.res()
        P.dma("sp", ckB.ap, cache_k[8 * hb:8 * hb + 8].rearrange("b s f -> s b f"), writes=[rck])
        for b2 in range(0, 8, 2):
            bk, br = bank()
            TR([(bk[:, (bb * 2 + gp) * 128:(bb * 2 + gp + 1) * 128], ckB.ap[:, b2 + bb, gp * 128:(gp + 1) * 128], identf)
                for bb in range(2) for gp in range(2)], [rck, Rc], [br])
            CP("act", KcT.ap[:, 8 * hb + b2:8 * hb + b2 + 2, :, :], bk[:, :].rearrange("p (b g s) -> p b g s", b=2, g=2),
               [br], [KcT.res()])
    A.free(ckB)

    def alloc_q(n):
        return (A.alloc("QT", [2, 8, n], BF16), A.alloc("qf", [2, n], F32), A.alloc("qsq", [2, n], BF16),
                A.alloc("qrs", [2, n], F32))

    QTB, qfB, qsqB, qrsB = alloc_q(128)
    est = {"n": 0}

    def qproj(i):
        c0, n = TT[i]
        for m in range(8):
            ha, hb_ = _chunk_heads(m)
            bk, br = bank()
            lw = wqo["q"]
            MM(acc(bk[:, 0:n], [(lw[:, kc, m * 128:(m + 1) * 128], xn[:, kc, c0:c0 + n]) for kc in range(8)]),
               [RW[1], Rxn[i]], [br])
            par = m % 2
            rqf, rsq, rrs = qfB.res(par), qsqB.res(par), qrsB.res(par)
            CP("act", qfB.ap[:, par, 0:n], bk[:, 0:n], [br], [rqf])
            ACT(qsqB.ap[:, par, 0:n], bk[:, 0:n], AF.Square, [br], [rsq])
            b2, br2 = bank()
            MM([dict(out=b2[:, 0:n], lhsT=ones64, rhs=qsqB.ap[:, par, 0:n], start=True, stop=True)], [rsq, Rcb], [br2])
            RSTD(qrsB.ap[:, par, 0:n], b2[:, 0:n], [br2], [rrs])
            STT("dve", QTB.ap[:, i % 2, m, 0:n], qfB.ap[:, par, 0:n], vcol("gq2"), qrsB.ap[:, par, 0:n], ALU.mult, ALU.mult,
                [rqf, rrs, Rvt], [QTB.res(i % 2)])

    def attn_units(i, qoff, nb, sample):
        qp = i % 2
        RQ = QTB.res(qp)
        ocol = nb * 128
        if sample:
            parts = [(16, EBn)]
        else:
            parts = ([(nb - 1, EBp)] if nb > 0 else []) + [(nb, EBc)]
        units = []
        for gp in range(2):
            ctx = {}

            def banks_(ctx=ctx):
                if "bo" not in ctx:
                    ctx["bo"], ctx["bro"] = bank(hold=True)
                    ctx["bs"], ctx["brs"] = bank(hold=True)
                return ctx["bo"], ctx["bro"], ctx["bs"], ctx["brs"]

            for hf in range(2):
                g = 2 * gp + hf
                rows = slice(hf * 64, (hf + 1) * 64)
                for pi, (kblk, EBt) in enumerate(parts):
                    st = {}

                    def A(st=st, gp=gp, hf=hf, g=g, rows=rows, kblk=kblk, EBt=EBt):
                        ki = min(kblk // 4, 4)
                        bk, br = bank()
                        MM([dict(out=bk[:, j * 128:(j + 1) * 128], lhsT=KT[rows, gp, kblk * 128:(kblk + 1) * 128],
                                 rhs=QTB.ap[rows, qp, 4 * gp + j, qoff:qoff + 128], start=True, stop=True, tp=(hf * 64, 0))
                            for j in range(4)], [RKT[ki], RQ], [br])
                        par = est["n"] % 2
                        est["n"] += 1
                        rE, rPT = EB_.res(par), PTB.res(par)
                        ACT(EB_.ap[:, par, :], bk[:, :], AF.Exp, [br], [rE], scale=SCALE)
                        TTo("dve", PTB.ap[:, par, :], EB_.ap[:, par, :],
                            EBt.ap[:, 4 * g:4 * g + 4, :].rearrange("p a b -> p (a b)"), ALU.mult, [rE, EBt.res()], [rPT])
                        st["par"], st["rPT"] = par, rPT

                    def B(st=st, gp=gp, hf=hf, g=g, rows=rows, kblk=kblk, pi=pi, banks_=banks_):
                        bo, bro, bs, brs = banks_()
                        par, rPT = st["par"], st["rPT"]
                        lastp = (pi == len(parts) - 1) and not sample
                        MM([dict(out=bo[rows, j * 128:(j + 1) * 128], lhsT=Vt[:, kblk, g * 64:(g + 1) * 64],
                                 rhs=PTB.ap[:, par, j * 128:(j + 1) * 128], start=(pi == 0 and j == 0),
                                 stop=(lastp and j == 3), tp=(0, hf * 64), sgc=True) for j in range(4)], [RVt[kblk], rPT], [bro])
                        MM([dict(out=bs[rows, j * 128:(j + 1) * 128], lhsT=ones1[:, 0:64],
                                 rhs=PTB.ap[:, par, j * 128:(j + 1) * 128], start=(pi == 0 and j == 0),
                                 stop=(lastp and j == 3), tp=(0, hf * 64), sgc=True) for j in range(4)], [Rcb, rPT], [brs])

                    units.append([A, B])
                if sample:
                    st = {}

                    def A2(st=st, gp=gp, hf=hf, g=g, rows=rows):
                        bk, br = bank()
                        MM([dict(out=bk[:, (b * 4 + j) * 8:(b * 4 + j) * 8 + 8], lhsT=KcT.ap[rows, b, gp, :],
                                 rhs=QTB.ap[rows, qp, 4 * gp + j, 8 * b:8 * b + 8], start=True, stop=True, tp=(hf * 64, 0))
                            for b in range(16) for j in range(4)], [KcT.res(), RQ], [br])
                        par = est["n"] % 2
                        est["n"] += 1
                        rE, rPT = EB_.res(par), PTB.res(par)
                        ACT(EB_.ap[:, par, :], bk[:, :], AF.Exp, [br], [rE], scale=SCALE)
                        TTo("dve", PTB.ap[:, par, :], EB_.ap[:, par, :],
                            EBs.ap[:, g, :, :, :].rearrange("p a b c -> p (a b c)"), ALU.mult, [rE, EBs.res()], [rPT])
                        st["par"], st["rPT"] = par, rPT

                    def B2(st=st, gp=gp, hf=hf, g=g, rows=rows, banks_=banks_):
                        bo, bro, bs, brs = banks_()
                        par, rPT = st["par"], st["rPT"]
                        MM([dict(out=bo[rows, j * 128 + 8 * b:j * 128 + 8 * b + 8], lhsT=VcB.ap[:, b, g * 64:(g + 1) * 64],
                                 rhs=PTB.ap[:, par, (b * 4 + j) * 8:(b * 4 + j) * 8 + 8], start=False,
                                 stop=(b == 15 and j == 3), tp=(0, hf * 64), sgc=True) for b in range(16) for j in range(4)],
                           [VcB.res(), rPT], [bro])
                        MM([dict(out=bs[rows, j * 128 + 8 * b:j * 128 + 8 * b + 8], lhsT=ones1[:, 0:64],
                                 rhs=PTB.ap[:, par, (b * 4 + j) * 8:(b * 4 + j) * 8 + 8], start=False,
                                 stop=(b == 15 and j == 3), tp=(0, hf * 64), sgc=True) for b in range(16) for j in range(4)],
                           [Rcb, rPT], [brs])

                    units.append([A2, B2])

            def N(gp=gp, banks_=banks_):
                bo, bro, bs, brs = banks_()
                rden = denB.res(gp)
                for j in range(4):
                    TS("dve", denB.ap[:, gp, j * 128:(j + 1) * 128], bs[:, j * 128:(j + 1) * 128],
                       esB.ap[:, 4 * gp + j:4 * gp + j + 1], None, ALU.add, None, [brs, Res_es], [rden])
                ACT(denB.ap[:, gp, :], denB.ap[:, gp, :], AF.Ln, [rden], [rden])
                ACT(denB.ap[:, gp, :], denB.ap[:, gp, :], AF.Exp, [rden], [rden], scale=-1.0)
                TTo("dve", xn[:, 4 * gp:4 * gp + 4, ocol:ocol + 128], bo[:, :].rearrange("p (a b) -> p a b", a=4),
                    denB.ap[:, gp, :].rearrange("p (a b) -> p a b", a=4), ALU.mult, [bro, rden], [Rxn[i]])
                release(bo)
                release(bs)

            units[-1].append(N)
        return units

    def run_units(units):
        if not units:
            return
        units[0][0]()
        for u in range(len(units)):
            if u + 1 < len(units):
                units[u + 1][0]()
            for f in units[u][1:]:
                f()

    qproj(4)
    run_units(attn_units(4, 0, 16, True))
    for b_ in (EBn, EBs, QTB, qfB, qsqB, qrsB):
        A.free(b_)
    QTB, qfB, qsqB, qrsB = alloc_q(512)

    def load_wo():
        wv_ = wview(0, 8, 1024)
        for c in range(8):
            ha, hb_ = _chunk_heads(c)
            P.dma("pool", wv_[0:64, c, :], w_o[ha * 64:ha * 64 + 64, :], writes=[RW[0]])
            P.dma("pool", wv_[64:128, c, :], w_o[hb_ * 64:hb_ * 64 + 64, :], writes=[RW[0]])
        wqo["o"] = wv_

    load_wo()
    allu = []
    qproj(0)
    for i in range(4):
        for q4 in range(4):
            allu.extend(attn_units(i, q4 * 128, 4 * i + q4, False))
            if q4 == 1 and i + 1 < 4:
                allu.append([(lambda ii: (lambda: qproj(ii)))(i + 1), (lambda: None)])
    run_units(allu)
    for b_ in (QTB, qfB, qsqB, qrsB, EB_, PTB, denB, esB, KTB, VtB, EBc, EBp):
        A.free(b_)
    ffw[(1, 0)] = load_ffn(1, 0, 1)
    for i in [4, 0, 1, 2, 3]:
        proj_residual(wqo["o"], i, [RW[0]], None)
    if stop("attn"):
        return nc

    ffn(1, 1)
    ple_w[1] = load_ple(1, 1)
    ple(1, 1, ple_w[1][0], ple_w[1][1])

    youtB = A.alloc("yout", [2, D], F32)
    for blk in range(NBLK):
        i = min(blk // 4, 4)
        ry = youtB.res(blk % 2)
        for half in range(2):
            bk, br = bank()
            TR([(bk[:, q * 128:(q + 1) * 128], h[:, 4 * half + q, blk * 128:(blk + 1) * 128], identf) for q in range(4)],
               [Rh[i][c] for c in range(4 * half, 4 * half + 4)] + [Rc], [br])
            CP("act" if half == 0 else "dve", youtB.ap[:, blk % 2, half * 512:(half + 1) * 512], bk[:, :], [br], [ry])
        P.dma("sp", y[blk * 128:(blk + 1) * 128, :], youtB.ap[:, blk % 2, :], reads=[ry])

    P.finish()
    return nc


def _t5_bucket_np(d):
    d = np.asarray(d)
    df = np.maximum(d, 1).astype(np.float32)
    large = 16 + (np.log(df / np.float32(16)) / np.float32(math.log(128 / 16)) * np.float32(16)).astype(np.int32)
    large = np.minimum(large, 31)
    return np.where(d < 16, d, large)


def _host_consts():
    cst = np.zeros((128, 256), np.float32)
    cst[:, 0:128] = np.eye(128, dtype=np.float32)
    idx = np.arange(128)
    cst[:, 128:256] = (idx[:, None] // 8 == idx[None, :] // 8).astype(np.float32)
    ohu = np.zeros((33, LU), np.float32)
    for j in range(LU):
        d = j - 127
        if 0 <= d < 128:
            ohu[int(_t5_bucket_np(d)), j] = 1.0
        else:
            ohu[32, j] = 1.0
    return cst, ohu


_NC_CACHE = {}


def kernel(x_prompt, x_sample, state_conv, state_ffn, cache_k, cache_v, p_prompt, p_sample,
           g_mix, cm_w_pw1, cm_b_pw1, cm_w_dw, cm_b_dw, cm_ln_g, cm_ln_b, cm_w_pw2, cm_b_pw2,
           at_w_q, at_g_q, at_sinks, at_w_o, kv_g, kv_w_k, kv_w_v, kv_g_k, rel_bias,
           g_ffn, ffn_w_up, ffn_w_dw, ffn_b_dw, ffn_w_down, g_ple, ple_w_gate, ple_w_proj):
    f = lambda a: np.ascontiguousarray(np.asarray(a, dtype=np.float32))
    sinks = f(at_sinks).reshape(16)
    sink_rows = np.stack([np.concatenate([np.full(64, sinks[_chunk_heads(c)[0]], np.float32),
                                          np.full(64, sinks[_chunk_heads(c)[1]], np.float32)]) for c in range(8)])
    parts = {
        "g_mix": f(g_mix).ravel(), "b_pw1": f(cm_b_pw1).ravel(), "w_dw": f(cm_w_dw).ravel(), "b_dw": f(cm_b_dw).ravel(),
        "ln_g": f(cm_ln_g).ravel(), "ln_b": f(cm_ln_b).ravel(), "b_pw2": f(cm_b_pw2).ravel(), "kv_g": f(kv_g).ravel(),
        "g_ffn": f(g_ffn).ravel(), "f_w_dw": f(ffn_w_dw).ravel(), "f_b_dw": f(ffn_b_dw).ravel(), "g_ple": f(g_ple).ravel(),
        "gq2": np.tile(f(at_g_q).ravel(), 2), "gk2": np.tile(f(kv_g_k).ravel(), 2), "sinks": sink_rows.ravel(),
    }
    vecs = np.zeros((VROWS, 128), np.float32)
    for name, off in VOFF.items():
        v = parts[name].reshape(-1, 128)
        vecs[off:off + v.shape[0]] = v
    cst, ohu = _host_consts()
    shared = {
        "vecs": vecs, "cst": cst, "ohu": ohu, "rel_bias": f(rel_bias),
        "w_pw1": f(cm_w_pw1)[0], "w_pw2": f(cm_w_pw2)[0], "w_q": f(at_w_q)[0], "w_o": f(at_w_o)[0],
        "w_k": f(kv_w_k), "w_v": f(kv_w_v), "w_up": f(ffn_w_up), "w_down": f(ffn_w_down),
        "w_gate": f(ple_w_gate), "w_proj": f(ple_w_proj),
    }
    xp, xs = f(x_prompt), f(x_sample)
    pp, ps = f(p_prompt), f(p_sample)
    sc, sf = f(state_conv), f(state_ffn)
    ck, cv = f(cache_k), f(cache_v)
    in_maps = []
    for c in range(NCORES):
        sl = slice(16 * c, 16 * c + 16)
        m = dict(shared)
        m["x_all"] = np.concatenate([xp[c], xs[sl].reshape(128, D)], axis=0)
        m["p_all"] = np.concatenate([pp[:, c], ps[:, sl].reshape(2, 128, 256)], axis=1)
        m["st_conv"] = np.ascontiguousarray(sc[0, sl])
        m["st_ffn"] = np.ascontiguousarray(sf[:, sl].reshape(2, 32, 2 * DFF))
        m["cache_k"] = np.ascontiguousarray(ck[sl].reshape(16, 128, 256))
        m["cache_v"] = np.ascontiguousarray(cv[sl].reshape(16, 128, 256))
        in_maps.append(m)
    if "nc" not in _NC_CACHE:
        _NC_CACHE["nc"] = build_program()
    nc = _NC_CACHE["nc"]
    res = run_bass_kernel_spmd(nc, in_maps, core_ids=list(range(NCORES)))
    R = res.results
    y_prompt = np.stack([R[c]["y"][0:2048] for c in range(NCORES)])
    y_sample = np.concatenate([R[c]["y"][2048:].reshape(16, 8, D) for c in range(NCORES)], axis=0)
    conv_p = np.stack([R[c]["conv_p"] for c in range(NCORES)])[None]
    conv_s = np.concatenate([R[c]["conv_s"] for c in range(NCORES)], axis=0)[None]
    ffn_p = np.stack([R[c]["ffn_p"] for c in range(NCORES)], axis=1)
    ffn_s = np.concatenate([R[c]["ffn_s"].reshape(2, 16, 2, 2 * DFF) for c in range(NCORES)], axis=1)
    k_p = np.stack([R[c]["k_p"].reshape(128, 4, 64) for c in range(NCORES)])
    k_s = np.concatenate([R[c]["k_s"].reshape(16, 128, 4, 64) for c in range(NCORES)], axis=0)
    v_p = np.stack([R[c]["v_p"].reshape(128, 4, 64) for c in range(NCORES)])
    v_s = np.concatenate([R[c]["v_s"].reshape(16, 128, 4, 64) for c in range(NCORES)], axis=0)
    outs = (y_prompt, y_sample, conv_p, conv_s, ffn_p, ffn_s, k_p, k_s, v_p, v_s)
    return tuple(np.ascontiguousarray(o, dtype=np.float32) for o in outs)
```

```python
import math
import os
import numpy as np
import concourse.bass as bass
import concourse.mybir as mybir
from concourse.bass_utils import run_bass_kernel_spmd

F32 = mybir.dt.float32
BF16 = mybir.dt.bfloat16
U8 = mybir.dt.uint8
AF = mybir.ActivationFunctionType
ALU = mybir.AluOpType

NCORES = 8
D = 1024
DFF = 2816
NF = 44
T = 2176
TT = [(0, 512), (512, 512), (1024, 512), (1536, 512), (2048, 128)]
NBLK = 17
EPS = 1e-6
SCALE = 0.125
LU = 383
GROUPS = [(0, 3), (3, 3), (6, 3), (9, 3), (12, 3), (15, 3), (18, 2), (20, 2)]
SLOT = 9216
UW = 30 + 2048 + 16 * 38
SB0 = 2078


def _vec_layout():
    ents = [("g_mix", 16), ("b_pw1", 16), ("w_dw", 248), ("b_dw", 8), ("ln_g", 8), ("ln_b", 8), ("b_pw2", 8),
            ("kv_g", 8), ("g_ffn", 16), ("f_w_dw", 264), ("f_b_dw", 88), ("g_ple", 16), ("gq2", 1), ("gk2", 1),
            ("sinks", 8)]
    off = {}
    o = 0
    for n, c in ents:
        off[n] = o
        o += c
    return off, o


VOFF, VN = _vec_layout()
VROWS = 768


def _chunk_heads(c):
    gp, j = c // 4, c % 4
    return 8 * gp + j, 8 * gp + 4 + j


class Res:
    __slots__ = ("w", "r", "name")

    def __init__(self, name=""):
        self.w = []
        self.r = []
        self.name = name


class Prog:
    def __init__(self, nc, n_dma_sems=24):
        self.nc = nc
        self.names = ["pe", "act", "dve", "pool", "sp"]
        self.ops = {n: [] for n in self.names}
        self.cnt = {n: 0 for n in self.names}
        self.seen = {n: {} for n in self.names}
        self.sem = {}
        for n in ["pe", "act", "dve", "pool"]:
            self.sem[n] = nc.alloc_semaphore("s_" + n)
        self.dq = ["sp", "pool"]
        self.dsem = {q: [nc.alloc_semaphore(f"d_{q}{i}") for i in range(n_dma_sems)] for q in self.dq}
        self.dcnt = {q: [0] * n_dma_sems for q in self.dq}
        self.dnext = {q: 0 for q in self.dq}

    def _waits(self, eng, evs):
        best = {}
        for ev in evs:
            if ev is None:
                continue
            s, v = ev
            k = id(s)
            if k not in best or best[k][1] < v:
                best[k] = (s, v)
        out = []
        seen = self.seen[eng]
        for k, (s, v) in best.items():
            if seen.get(k, 0) >= v:
                continue
            seen[k] = v
            out.append((s, v))
        return out

    @staticmethod
    def _deps(reads, writes, extra):
        evs = list(extra)
        for r in reads:
            evs.extend(r.w)
        for w in writes:
            evs.extend(w.w)
            evs.extend(w.r)
        return evs

    @staticmethod
    def _commit(ev, reads, writes):
        evl = ev if isinstance(ev, list) else [ev]
        for r in reads:
            r.r.extend(evl)
        for w in writes:
            w.w = list(evl)
            w.r = []

    def op(self, eng, fn, reads=(), writes=(), extra=()):
        waits = self._waits(eng, self._deps(reads, writes, extra))
        self.cnt[eng] += 1
        sem = self.sem[eng]
        ev = (sem, self.cnt[eng])

        def run(e):
            for s, v in waits:
                e.wait_ge(s, v)
            fn(e).then_inc(sem, 1)

        self.ops[eng].append(run)
        self._commit(ev, reads, writes)
        return ev

    def dma(self, q, out, in_, reads=(), writes=(), extra=()):
        i = self.dnext[q]
        self.dnext[q] = (i + 1) % len(self.dsem[q])
        s = self.dsem[q][i]
        prev = self.dcnt[q][i]
        evs = self._deps(reads, writes, extra)
        if prev > 0:
            evs.append((s, prev))
        waits = self._waits(q, evs)
        self.dcnt[q][i] = prev + 16
        ev = (s, prev + 16)

        def run(e):
            for s2, v in waits:
                e.wait_ge(s2, v)
            e.dma_start(out=out, in_=in_).then_inc(s, 16)

        self.ops[q].append(run)
        self._commit(ev, reads, writes)
        return ev

    def dma_many(self, q, pairs, reads=(), writes=()):
        evs0 = self._deps(reads, writes, ())
        out_evs = []
        for (out, in_) in pairs:
            i = self.dnext[q]
            self.dnext[q] = (i + 1) % len(self.dsem[q])
            s = self.dsem[q][i]
            prev = self.dcnt[q][i]
            evs = list(evs0) + ([(s, prev)] if prev > 0 else [])
            waits = self._waits(q, evs)
            self.dcnt[q][i] = prev + 16
            out_evs.append((s, prev + 16))

            def run(e, waits=waits, out=out, in_=in_, s=s):
                for s2, v in waits:
                    e.wait_ge(s2, v)
                e.dma_start(out=out, in_=in_).then_inc(s, 16)

            self.ops[q].append(run)
        self._commit(out_evs, reads, writes)
        return out_evs

    def finish(self):
        nc = self.nc
        finals = [(s, c) for q in self.dq for s, c in zip(self.dsem[q], self.dcnt[q]) if c > 0]
        waits = self._waits("sp", finals)

        def fin(e):
            for s, v in waits:
                e.wait_ge(s, v)

        self.ops["sp"].append(fin)
        with nc.Block() as block:
            @block.tensor
            def _(e):
                for f in self.ops["pe"]:
                    f(e)

            @block.scalar
            def _(e):
                for f in self.ops["act"]:
                    f(e)

            @block.vector
            def _(e):
                for f in self.ops["dve"]:
                    f(e)

            @block.gpsimd
            def _(e):
                for f in self.ops["pool"]:
                    f(e)

            @block.sync
            def _(e):
                for f in self.ops["sp"]:
                    f(e)


class Buf:
    def __init__(self, ap, off, nbytes, inherit, name):
        self.ap = ap
        self.off = off
        self.nbytes = nbytes
        self.inherit = inherit
        self.name = name
        self.R = {}

    def res(self, key=None):
        if key not in self.R:
            r = Res(f"{self.name}:{key}")
            r.r = list(self.inherit)
            self.R[key] = r
        return self.R[key]

    def all_events(self):
        evs = list(self.inherit)
        for r in self.R.values():
            evs.extend(r.w)
            evs.extend(r.r)
        return evs


class Arena:
    def __init__(self, nc, nbytes):
        self.t = nc.alloc_sbuf_tensor("arena", [128, nbytes], U8)
        self.n = nbytes
        self.live = []
        self.dead = []
        self.peak = 0

    def alloc(self, name, shape, dtype):
        esz = 4 if dtype == F32 else 2
        n = 1
        for s in shape:
            n *= s
        nbytes = ((n * esz + 63) // 64) * 64
        spans = sorted((b.off, b.off + b.nbytes) for b in self.live)
        off = 0
        for a, b in spans:
            if off + nbytes <= a:
                break
            off = max(off, b)
        if off + nbytes > self.n:
            raise RuntimeError(f"arena OOM allocating {name} ({nbytes}); live={[(b.name, b.nbytes) for b in self.live]}")
        self.peak = max(self.peak, off + nbytes)
        inherit = []
        keep = []
        for (o2, n2, evs) in self.dead:
            if o2 < off + nbytes and off < o2 + n2:
                inherit.extend(evs)
                if not (off <= o2 and o2 + n2 <= off + nbytes):
                    keep.append((o2, n2, evs))
            else:
                keep.append((o2, n2, evs))
        self.dead = keep
        v = self.t[:, off:off + n * esz].bitcast(dtype)
        if len(shape) == 2:
            v = v.rearrange("p (a b) -> p a b", a=shape[0])
        elif len(shape) == 3:
            v = v.rearrange("p (a b c) -> p a b c", a=shape[0], b=shape[1])
        elif len(shape) == 4:
            v = v.rearrange("p (a b c d) -> p a b c d", a=shape[0], b=shape[1], c=shape[2])
        buf = Buf(v, off, nbytes, inherit, name)
        self.live.append(buf)
        return buf

    def free(self, buf):
        self.live.remove(buf)
        self.dead.append((buf.off, buf.nbytes, buf.all_events()))


def build_program():
    nc = bass.Bass("TRN2", target_bir_lowering=False)
    P = Prog(nc)

    def din(name, shape):
        return nc.dram_tensor(name, list(shape), F32, kind="ExternalInput").ap()

    def dout(name, shape):
        return nc.dram_tensor(name, list(shape), F32, kind="ExternalOutput").ap()

    x_all = din("x_all", [T, D])
    p_all = din("p_all", [2, T, 256])
    st_conv = din("st_conv", [16, 30, D])
    st_ffn = din("st_ffn", [2, 32, 2 * DFF])
    cache_k = din("cache_k", [16, 128, 256])
    cache_v = din("cache_v", [16, 128, 256])
    vecs = din("vecs", [VROWS, 128])
    cst = din("cst", [128, 256])
    ohu = din("ohu", [33, LU])
    rel_bias = din("rel_bias", [32, 16])
    w_pw1 = din("w_pw1", [D, 2 * D])
    w_pw2 = din("w_pw2", [D, D])
    w_q = din("w_q", [D, D])
    w_o = din("w_o", [D, D])
    w_k = din("w_k", [D, 256])
    w_v = din("w_v", [D, 256])
    w_up = din("w_up", [2, D, 2 * DFF])
    w_down = din("w_down", [2, DFF, D])
    w_gate = din("w_gate", [2, D, D])
    w_proj = din("w_proj", [2, 256, D])

    y = dout("y", [T, D])
    conv_p = dout("conv_p", [30, D])
    conv_s = dout("conv_s", [16, 30, D])
    ffn_p = dout("ffn_p", [2, 2, 2 * DFF])
    ffn_s = dout("ffn_s", [2, 32, 2 * DFF])
    k_p = dout("k_p", [128, 256])
    k_s = dout("k_s", [16, 128, 256])
    v_p = dout("v_p", [128, 256])
    v_s = dout("v_s", [16, 128, 256])
    sc2 = nc.dram_tensor("sc2", [16, 128 * (LU + 1)], F32, kind="Internal")

    A = Arena(nc, 211968)
    _stop = os.environ.get("KSTOP", "")

    def stop(tag):
        if _stop == tag:
            P.finish()
            return True
        return False
    banks = [nc.alloc_psum_tensor(f"bank{i}", [128, 512], F32) for i in range(8)]
    bank_res = [Res(f"bank{i}") for i in range(8)]
    bstate = {"n": 0, "held": set()}

    def bank(hold=False):
        while True:
            i = bstate["n"] % 8
            bstate["n"] += 1
            if i not in bstate["held"]:
                break
        if hold:
            bstate["held"].add(i)
        return banks[i], bank_res[i]

    def release(bk):
        bstate["held"].discard(banks.index(bk))

    def ACT(out, in_, func, reads, writes, bias=0.0, scale=1.0):
        return P.op("act", lambda e: e.activation(out=out, in_=in_, func=func, bias=bias, scale=scale), reads, writes)

    def TTo(eng, out, in0, in1, op, reads, writes):
        return P.op(eng, lambda e: e.tensor_tensor(out=out, in0=in0, in1=in1, op=op), reads, writes)

    def TS(eng, out, in0, s1, s2, op0, op1, reads, writes):
        if s2 is None:
            return P.op(eng, lambda e: e.tensor_scalar(out=out, in0=in0, scalar1=s1, scalar2=None, op0=op0), reads, writes)
        return P.op(eng, lambda e: e.tensor_scalar(out=out, in0=in0, scalar1=s1, scalar2=s2, op0=op0, op1=op1), reads, writes)

    def STT(eng, out, in0, scalar, in1, op0, op1, reads, writes, extra=()):
        return P.op(eng, lambda e: e.scalar_tensor_tensor(out=out, in0=in0, scalar=scalar, in1=in1, op0=op0, op1=op1),
                    reads, writes, extra)

    def RSTD(out, in_, reads, writes):
        ACT(out, in_, AF.Ln, list(reads) + [Rcb], writes, bias=epsc)
        return ACT(out, out, AF.Exp, writes, writes, scale=-0.5)

    def CP(eng, out, in_, reads, writes):
        if eng == "act":
            return P.op("act", lambda e: e.activation(out=out, in_=in_, func=AF.Copy), reads, writes)
        return P.op(eng, lambda e: e.tensor_copy(out=out, in_=in_), reads, writes)

    def MEMSET(eng, out, val, writes):
        return P.op(eng, lambda e: e.memset(out, val), (), writes)

    def MM(mms, reads, writes):
        def fn(e):
            last = None
            for m in mms:
                kw = {}
                if m.get("tp") is not None:
                    kw["tile_position"] = m["tp"]
                if m.get("sgc"):
                    kw["skip_group_check"] = True
                last = e.matmul(m["out"], lhsT=m["lhsT"], rhs=m["rhs"], start=m["start"], stop=m["stop"], **kw)
            return last
        return P.op("pe", fn, reads, writes)

    def TR(items, reads, writes):
        def fn(e):
            last = None
            for o, i, idn in items:
                last = e.transpose(o, i, idn)
            return last
        return P.op("pe", fn, reads, writes)

    def acc(out, pairs, tp=None):
        n = len(pairs)
        return [dict(out=out, lhsT=l, rhs=r, start=(k == 0), stop=(k == n - 1), tp=tp) for k, (l, r) in enumerate(pairs)]

    hB = A.alloc("h", [8, T], F32)
    h = hB.ap
    Rh = [[hB.res((i, c)) for c in range(8)] for i in range(5)]
    XN = {}

    def xn_alloc():
        XN["B"] = A.alloc("xn", [8, T], BF16)
        return XN["B"].ap, [XN["B"].res(i) for i in range(5)]

    def xn_free():
        A.free(XN["B"])

    xn, Rxn = xn_alloc()
    WB = A.alloc("W", [2 * SLOT], BF16)
    W = WB.ap
    RW = [WB.res(0), WB.res(1)]
    vtB = A.alloc("vt", [VROWS], F32)
    vt = vtB.ap
    Rvt = vtB.res()
    cB = A.alloc("consts", [2, 128], F32)
    identf = cB.ap[:, 0, :]
    bmask = cB.ap[:, 1, :]
    Rc = cB.res()
    cbB = A.alloc("constsb", [4, 128], BF16)
    identb = cbB.ap[:, 0, :]
    onesN = cbB.ap[:, 1, :]
    ones64 = cbB.ap[:, 2, :]
    ones1 = cbB.ap[:, 3, :]
    Rcb = cbB.res()

    def vcol(name, idx=0):
        c = VOFF[name] + idx
        return vt[:, c:c + 1]

    P.dma("sp", cB.ap[:, 0:2, :], cst.rearrange("p (a b) -> p a b", a=2), writes=[Rc])
    CP("dve", identb, identf, [Rc], [Rcb])
    MEMSET("dve", onesN, 1.0 / 1024.0, [Rcb])
    MEMSET("dve", ones64, 0.0, [Rcb])
    MEMSET("dve", cbB.ap[0:64, 2, 0:64], 1.0 / 64.0, [Rcb])
    MEMSET("dve", cbB.ap[64:128, 2, 64:128], 1.0 / 64.0, [Rcb])
    MEMSET("dve", ones1, 1.0, [Rcb])
    epsB = A.alloc("epsc", [1], F32)
    epsc = epsB.ap[:, 0:1]
    MEMSET("dve", epsc, EPS, [Rcb])

    vinB = A.alloc("vin", [6, 128], F32)
    Rvin = vinB.res()
    P.dma("sp", vinB.ap, vecs.rearrange("(a p) f -> p a f", p=128), writes=[Rvin])
    for half in range(2):
        bk, br = bank()
        TR([(bk[:, q * 128:(q + 1) * 128], vinB.ap[:, half * 3 + q, :], identf) for q in range(3)], [Rvin, Rc], [br])
        CP("dve", vt[:, half * 384:(half + 1) * 384], bk[:, 0:384], [br], [Rvt])
    A.free(vinB)

    def load_rows(dst3, src2, nk, slots):
        P.dma_many("pool", [(dst3[:, kc, :], src2[kc * 128:(kc + 1) * 128, :]) for kc in range(nk)], writes=slots)

    def wview(off, a, b):
        return W[:, off:off + a * b].rearrange("p (a b) -> p a b", a=a)

    def load_pw1():
        load_rows(wview(0, 8, 2048), w_pw1, 8, [RW[0], RW[1]])

    def load_pw2():
        load_rows(wview(0, 8, 1024), w_pw2, 8, [RW[0]])

    def load_ffn(l, gi, slot):
        j0, nj = GROUPS[gi]
        base = slot * SLOT
        wu = wview(base, 8, 2 * nj * 128)
        wd = wview(base + 16 * nj * 128, nj, 1024)
        pairs = []
        for kc in range(8):
            pairs.append((wu[:, kc, 0:nj * 128], w_up[l, kc * 128:(kc + 1) * 128, j0 * 128:(j0 + nj) * 128]))
            pairs.append((wu[:, kc, nj * 128:2 * nj * 128],
                          w_up[l, kc * 128:(kc + 1) * 128, DFF + j0 * 128:DFF + (j0 + nj) * 128]))
        for jj in range(nj):
            pairs.append((wd[:, jj, :], w_down[l, (j0 + jj) * 128:(j0 + jj + 1) * 128, :]))
        P.dma_many("pool", pairs, writes=[RW[slot]])
        return wu, wd

    def build_eb():
        tbB = A.alloc("tbl", [16], F32)
        Rtb = tbB.res()
        ohB = A.alloc("ohu", [LU], F32)
        Roh = ohB.res()
        UB = A.alloc("U", [LU], F32)
        RU = UB.res()
        MEMSET("dve", tbB.ap[32:33, :], -30000.0, [Rtb])
        P.dma("sp", tbB.ap[0:32, :], rel_bias, writes=[Rtb])
        P.dma("sp", ohB.ap[0:33, :], ohu, writes=[Roh])
        bk, br = bank()
        MM([dict(out=bk[0:16, 0:LU], lhsT=tbB.ap[0:33, :], rhs=ohB.ap[0:33, :], start=True, stop=True)], [Rtb, Roh], [br])
        ACT(UB.ap[0:16, :], bk[0:16, 0:LU], AF.Exp, [br], [RU])
        Rsc = Res("sc2")
        pst = UB.ap.ap[0][0]
        P.dma("sp", bass.AP(sc2, 0, [[128 * (LU + 1), 16], [LU + 1, 128], [1, LU]]),
              bass.AP(UB.ap.tensor, UB.ap.offset, [[pst, 16], [0, 128], [1, LU]]), reads=[RU], writes=[Rsc])
        EBc = A.alloc("EBc", [16, 128], BF16)
        EBp = A.alloc("EBp", [16, 128], BF16)
        P.dma("pool", EBc.ap, bass.AP(sc2, 127, [[LU, 128], [128 * (LU + 1), 16], [1, 128]]), reads=[Rsc], writes=[EBc.res()])
        P.dma("pool", EBp.ap, bass.AP(sc2, 255, [[LU, 128], [128 * (LU + 1), 16], [1, 128]]), reads=[Rsc], writes=[EBp.res()])
        A.free(tbB)
        A.free(ohB)
        A.free(UB)
        return EBc, EBp

    load_pw1()

    xinB = A.alloc("xin", [2, D], F32)
    for blk in range(NBLK):
        i = min(blk // 4, 4)
        rx = xinB.res(blk % 2)
        P.dma("sp", xinB.ap[:, blk % 2, :], x_all[blk * 128:(blk + 1) * 128, :], writes=[rx])
        for half in range(2):
            bk, br = bank()
            TR([(bk[:, q * 128:(q + 1) * 128], xinB.ap[:, blk % 2, (4 * half + q) * 128:(4 * half + q + 1) * 128], identf)
                for q in range(4)], [rx, Rc], [br])
            CP("act" if half == 0 else "dve", h[:, 4 * half:4 * half + 4, blk * 128:(blk + 1) * 128],
               bk[:, :].rearrange("p (a b) -> p a b", a=4), [br], [Rh[i][c] for c in range(4 * half, 4 * half + 4)])
    A.free(xinB)
    if stop("p0"):
        return nc

    def rmsnorm(i, gname, gidx):
        c0, n = TT[i]
        sq = A.alloc("sq", [8, n], BF16)
        rs = A.alloc("rstd", [n], F32)
        ACT(sq.ap, h[:, :, c0:c0 + n], AF.Square, Rh[i], [sq.res()])
        bk, br = bank()
        MM(acc(bk[:, 0:n], [(onesN, sq.ap[:, c, :]) for c in range(8)]), [sq.res(), Rcb], [br])
        RSTD(rs.ap, bk[:, 0:n], [br], [rs.res()])
        for c in range(8):
            STT("dve", xn[:, c, c0:c0 + n], h[:, c, c0:c0 + n], vcol(gname, gidx * 8 + c), rs.ap,
                ALU.mult, ALU.mult, [Rh[i][c], rs.res(), Rvt], [Rxn[i]])
        A.free(sq)
        A.free(rs)

    uhB = A.alloc("uh", [8, UW], BF16)
    uh = uhB.ap
    Ru = [uhB.res(i) for i in range(5)]
    Rhist = uhB.res("hist")
    ustB = A.alloc("ust", [8, 158], F32)
    Rust = ustB.res()
    MEMSET("pool", uh[:, :, 0:30], 0.0, [Ru[0]])

    P.dma("sp", conv_s[:, 0:22, :], st_conv[:, 8:30, :])
    scB = A.alloc("scin", [D], F32)
    for blk in range(4):
        rsb = scB.res()
        P.dma("sp", scB.ap[0:120, :], st_conv[4 * blk:4 * blk + 4, :, :].rearrange("b r d -> (b r) d"), writes=[rsb])
        for half in range(2):
            bk, br = bank()
            TR([(bk[:, q * 120:(q + 1) * 120], scB.ap[0:120, (4 * half + q) * 128:(4 * half + q + 1) * 128], identf[0:120, 0:120])
                for q in range(4)], [rsb, Rc], [br])
            for q in range(4):
                c = 4 * half + q
                dst = uh[:, c, SB0 + 38 * 4 * blk:SB0 + 38 * 4 * (blk + 1)].rearrange("p (b w) -> p b w", w=38)[:, :, 0:30]
                CP("dve" if q % 2 == 0 else "act", dst, bk[:, q * 120:(q + 1) * 120].rearrange("p (b r) -> p b r", r=30),
                   [br], [Rhist])
    A.free(scB)

    for i in range(5):
        rmsnorm(i, "g_mix", 0)

    wp1 = wview(0, 8, 2048)
    sgB = A.alloc("sg", [2, 512], F32)
    for i in range(5):
        c0, n = TT[i]
        for m in range(8):
            ba, bra = bank()
            bg, brg = bank()
            MM(acc(ba[:, 0:n], [(wp1[:, kc, m * 128:(m + 1) * 128], xn[:, kc, c0:c0 + n]) for kc in range(8)]),
               [RW[0], RW[1], Rxn[i]], [bra])
            MM(acc(bg[:, 0:n], [(wp1[:, kc, (8 + m) * 128:(9 + m) * 128], xn[:, kc, c0:c0 + n]) for kc in range(8)]),
               [RW[0], RW[1], Rxn[i]], [brg])
            rsg = sgB.res(m % 2)
            sg = sgB.ap[:, m % 2, 0:n]
            ACT(sg, bg[:, 0:n], AF.Sigmoid, [brg, Rvt], [rsg], bias=vcol("b_pw1", 8 + m))
            if i < 4:
                dst = uh[:, m, 30 + c0:30 + c0 + n]
                STT("dve", dst, ba[:, 0:n], vcol("b_pw1", m), sg, ALU.add, ALU.mult, [bra, rsg, Rvt], [Ru[i]])
                if i == 3:
                    STT("dve", ustB.ap[:, m, 0:30], ba[:, 482:512], vcol("b_pw1", m), sg[:, 482:512], ALU.add, ALU.mult,
                        [bra, rsg, Rvt], [Rust])
            else:
                dst = uh[:, m, SB0:UW].rearrange("p (b w) -> p b w", w=38)[:, :, 30:38]
                STT("dve", dst, ba[:, 0:128].rearrange("p (b t) -> p b t", t=8), vcol("b_pw1", m),
                    sg.rearrange("p (b t) -> p b t", t=8), ALU.add, ALU.mult, [bra, rsg, Rvt], [Ru[4]])
                STT("dve", ustB.ap[:, m, 30:158], ba[:, 0:128], vcol("b_pw1", m), sg, ALU.add, ALU.mult,
                    [bra, rsg, Rvt], [Rust])
    A.free(sgB)
    if stop("p1a"):
        return nc
    load_pw2()

    cpoB = A.alloc("cpo", [D], F32)
    for half in range(2):
        bk, br = bank()
        TR([(bk[0:30, q * 128:(q + 1) * 128], ustB.ap[:, 4 * half + q, 0:30], identf) for q in range(4)], [Rust, Rc], [br])
        CP("act", cpoB.ap[0:30, half * 512:(half + 1) * 512], bk[0:30, :], [br], [cpoB.res()])
    P.dma("sp", conv_p, cpoB.ap[0:30, :], reads=[cpoB.res()])
    csoB = A.alloc("cso", [D], F32)
    for half in range(2):
        bk, br = bank()
        TR([(bk[:, q * 128:(q + 1) * 128], ustB.ap[:, 4 * half + q, 30:158], identf) for q in range(4)], [Rust, Rc], [br])
        CP("act", csoB.ap[:, half * 512:(half + 1) * 512], bk[:, :], [br], [csoB.res()])
    for b in range(16):
        P.dma("sp", conv_s[b, 22:30, :], csoB.ap[8 * b:8 * b + 8, :], reads=[csoB.res()])
    A.free(cpoB)
    A.free(csoB)
    A.free(ustB)
    if stop("p1"):
        return nc

    xn_free()
    wp2 = wview(0, 8, 1024)
    ffw = {}
    c2B = A.alloc("c2t", [8, 512], BF16)
    Rc2 = c2B.res()
    DgB = A.alloc("Dg", [2, 31, 128], BF16)
    ccB = A.alloc("cc", [8, 512], F32)
    cbfB = A.alloc("cbf", [2, 512], BF16)
    sqcB = A.alloc("sqc", [2, 512], BF16)
    lnB = A.alloc("lnst", [3, 512], F32)
    wdw_pst = vt.ap[0][0]
    for i in range(5):
        c0, n = TT[i]
        bm, brm = bank(hold=True)
        bq, brq = bank(hold=True)
        for c in range(8):
            rdg = DgB.res(c % 2)
            NDV = 25
            for (eng_, k0, k1) in (("dve", 0, NDV), ("pool", NDV, 31)):
                in0 = bass.AP(identb.tensor, identb.offset, [[identb.ap[0][0], 128], [0, k1 - k0], [1, 128]])
                wsrc = vcol("w_dw", k0 * 8 + c)
                in1 = bass.AP(wsrc.tensor, wsrc.offset, [[wdw_pst, 128], [8, k1 - k0], [0, 128]])
                TTo(eng_, DgB.ap[:, c % 2, k0:k1, :], in0, in1, ALU.mult, [Rcb, Rvt], [DgB.res((c % 2, eng_))])
            bk, br = bank()
            if i < 4:
                rhs = [uh[:, c, c0 + k:c0 + k + n] for k in range(31)]
                rds = [Ru[i]] + ([Ru[i - 1]] if i > 0 else [])
            else:
                sv = uh[:, c, SB0:UW].rearrange("p (b w) -> p b w", w=38)
                rhs = [sv[:, :, k:k + 8] for k in range(31)]
                rds = [Ru[4], Rhist]
            outv = bk[:, 0:n] if i < 4 else bk[:, 0:128].rearrange("p (b t) -> p b t", t=8)
            MM(acc(outv, [(DgB.ap[:, c % 2, k, :], rhs[k]) for k in range(31)]),
               [DgB.res((c % 2, "dve")), DgB.res((c % 2, "pool"))] + rds, [br])
            rcc = ccB.res(c)
            ACT(ccB.ap[:, c, 0:n], bk[:, 0:n], AF.Identity, [br, Rvt], [rcc], bias=vcol("b_dw", c))
            ACT(cbfB.ap[:, c % 2, 0:n], bk[:, 0:n], AF.Identity, [br, Rvt], [cbfB.res(c % 2)], bias=vcol("b_dw", c))
            ACT(sqcB.ap[:, c % 2, 0:n], ccB.ap[:, c, 0:n], AF.Square, [rcc], [sqcB.res(c % 2)])
            MM([dict(out=bm[:, 0:n], lhsT=onesN, rhs=cbfB.ap[:, c % 2, 0:n], start=(c == 0), stop=(c == 7))],
               [cbfB.res(c % 2), Rcb], [brm])
            MM([dict(out=bq[:, 0:n], lhsT=onesN, rhs=sqcB.ap[:, c % 2, 0:n], start=(c == 0), stop=(c == 7))],
               [sqcB.res(c % 2), Rcb], [brq])
        mean = lnB.ap[:, 0, 0:n]
        m2 = lnB.ap[:, 1, 0:n]
        rstd = lnB.ap[:, 2, 0:n]
        rln = lnB.res()
        CP("act", mean, bm[:, 0:n], [brm], [rln])
        TTo("dve", m2, mean, mean, ALU.mult, [rln], [rln])
        TTo("dve", m2, bq[:, 0:n], m2, ALU.subtract, [brq, rln], [rln])
        RSTD(rstd, m2, [rln], [rln])
        release(bm)
        release(bq)
        for c in range(8):
            rcc = ccB.res(c)
            TTo("dve", ccB.ap[:, c, 0:n], ccB.ap[:, c, 0:n], mean, ALU.subtract, [rcc, rln], [rcc])
            TTo("dve", ccB.ap[:, c, 0:n], ccB.ap[:, c, 0:n], rstd, ALU.mult, [rcc, rln], [rcc])
            ACT(c2B.ap[:, c, 0:n], ccB.ap[:, c, 0:n], AF.Silu, [rcc, Rvt], [Rc2], bias=vcol("ln_b", c),
                scale=vcol("ln_g", c))
        if i == 3:
            ffw[(0, 0)] = load_ffn(0, 0, 1)
        for m in range(8):
            bk, br = bank()
            MM(acc(bk[:, 0:n], [(wp2[:, kc, m * 128:(m + 1) * 128], c2B.ap[:, kc, 0:n]) for kc in range(8)]),
               [RW[0], Rc2], [br])
            STT("dve", h[:, m, c0:c0 + n], bk[:, 0:n], vcol("b_pw2", m), h[:, m, c0:c0 + n], ALU.add, ALU.add,
                [br, Rvt], [Rh[i][m]])
    for b_ in (DgB, ccB, cbfB, sqcB, lnB, c2B):
        A.free(b_)
    A.free(uhB)
    xn, Rxn = xn_alloc()
    if stop("p3"):
        return nc

    def proj_residual(wv, i, rslots, bias_name=None):
        c0, n = TT[i]
        for m in range(8):
            bk, br = bank()
            MM(acc(bk[:, 0:n], [(wv[:, kc, m * 128:(m + 1) * 128], xn[:, kc, c0:c0 + n]) for kc in range(8)]),
               rslots + [Rxn[i]], [br])
            if bias_name is not None:
                STT("dve", h[:, m, c0:c0 + n], bk[:, 0:n], vcol(bias_name, m), h[:, m, c0:c0 + n], ALU.add, ALU.add,
                    [br, Rvt], [Rh[i][m]])
            else:
                TTo("dve", h[:, m, c0:c0 + n], bk[:, 0:n], h[:, m, c0:c0 + n], ALU.add, [br], [Rh[i][m]])


    def ffn(l, slot0, after_first_group=None):
        for i in range(5):
            rmsnorm(i, "g_ffn", l)
        fstB = A.alloc("fst", [NF, 34], F32)
        Rfst = fstB.res()
        upB = A.alloc("uprev", [3, NF, 2], F32)
        hsB = A.alloc("hs", [NF, 32], F32)
        Rhs = hsB.res()
        sfB = A.alloc("sfin", [1408], F32)
        for q4 in range(4):
            rsf = sfB.res()
            P.dma("sp", sfB.ap[0:32, :], st_ffn[l, :, q4 * 1408:(q4 + 1) * 1408], writes=[rsf])
            bk, br = bank()
            TR([(bk[:, q * 32:(q + 1) * 32], sfB.ap[0:32, q * 128:(q + 1) * 128], identf[0:32, 0:32]) for q in range(11)],
               [rsf, Rc], [br])
            CP("dve", hsB.ap[:, q4 * 11:(q4 + 1) * 11, :], bk[:, 0:352].rearrange("p (a b) -> p a b", a=11), [br], [Rhs])
        A.free(sfB)
        cgB = A.alloc("cg", [2, 2, 512], F32)
        sgtB = A.alloc("sgt", [2, 512], F32)
        actB = A.alloc("act", [2, 4, 512], BF16)
        units = [(gi, i) for gi in range(len(GROUPS)) for i in range(5)]

        def unit_ctx(k):
            gi, i = units[k]
            j0, nj = GROUPS[gi]
            slot = (slot0 + gi) % 2
            wu, wd = ffw[(l, gi)]
            c0, n = TT[i]
            return gi, i, j0, nj, slot, wu, wd, c0, n, actB.res(k % 2)

        def emit_up(k):
            gi, i, j0, nj, slot, wu, wd, c0, n, ract = unit_ctx(k)
            for jj in range(nj):
                j = j0 + jj
                par = (jj + i) % 2
                cvals = []
                for gv in range(2):
                    f = j + 22 * gv
                    bk, br = bank()
                    MM(acc(bk[:, 0:n], [(wu[:, kc, (gv * nj + jj) * 128:(gv * nj + jj + 1) * 128], xn[:, kc, c0:c0 + n])
                                        for kc in range(8)]), [RW[slot], Rxn[i]], [br])
                    rcg = cgB.res((par, gv))
                    cg = cgB.ap[:, par, gv, 0:n]
                    w0 = vcol("f_w_dw", (l * 3 + 0) * NF + f)
                    w1 = vcol("f_w_dw", (l * 3 + 1) * NF + f)
                    w2 = vcol("f_w_dw", (l * 3 + 2) * NF + f)
                    bb = vcol("f_b_dw", l * NF + f)
                    ACT(cg, bk[:, 0:n], AF.Identity, [br, Rvt], [rcg], bias=bb, scale=w2)
                    if i < 4:
                        if i < 3:
                            rup = upB.res((i, f))
                            evc = CP("act", upB.ap[:, i, f, :], bk[:, n - 2:n], [br], [rup])
                        else:
                            evc = CP("act", fstB.ap[:, f, 0:2], bk[:, n - 2:n], [br], [Rfst])
                        STT("dve", cg[:, 1:n], bk[:, 0:n - 1], w1, cg[:, 1:n], ALU.mult, ALU.add, [br, Rvt], [rcg], extra=[evc])
                        STT("dve", cg[:, 2:n], bk[:, 0:n - 2], w0, cg[:, 2:n], ALU.mult, ALU.add, [br, Rvt], [rcg])
                        if i > 0:
                            rprev = upB.res((i - 1, f))
                            ext = upB.ap[:, i - 1, f, :]
                            STT("dve", cg[:, 0:2], ext, w0, cg[:, 0:2], ALU.mult, ALU.add, [rprev, Rvt], [rcg])
                            STT("dve", cg[:, 0:1], ext[:, 1:2], w1, cg[:, 0:1], ALU.mult, ALU.add, [rprev, Rvt], [rcg])
                    else:
                        cg3 = cg.rearrange("p (b t) -> p b t", t=8)
                        bk3 = bk[:, 0:128].rearrange("p (b t) -> p b t", t=8)
                        hs3 = hsB.ap[:, f, :].rearrange("p (b r) -> p b r", r=2)
                        evc = CP("act", fstB.ap[:, f, 2:34].rearrange("p (b r) -> p b r", r=2), bk3[:, :, 6:8], [br], [Rfst])
                        STT("dve", cg3[:, :, 1:8], bk3[:, :, 0:7], w1, cg3[:, :, 1:8], ALU.mult, ALU.add, [br, Rvt], [rcg], extra=[evc])
                        STT("dve", cg3[:, :, 2:8], bk3[:, :, 0:6], w0, cg3[:, :, 2:8], ALU.mult, ALU.add, [br, Rvt], [rcg])
                        STT("dve", cg3[:, :, 0:2], hs3, w0, cg3[:, :, 0:2], ALU.mult, ALU.add, [Rhs, Rvt], [rcg])
                        STT("dve", cg3[:, :, 0:1], hs3[:, :, 1:2], w1, cg3[:, :, 0:1], ALU.mult, ALU.add, [Rhs, Rvt], [rcg])
                    cvals.append((cg, rcg))
                rsg = sgtB.res(par)
                sgt = sgtB.ap[:, par, 0:n]
                ACT(sgt, cvals[0][0], AF.Silu, [cvals[0][1]], [rsg])
                TTo("pool", actB.ap[:, k % 2, jj, 0:n], sgt, cvals[1][0], ALU.mult, [rsg, cvals[1][1]], [ract])

        def emit_down(k):
            gi, i, j0, nj, slot, wu, wd, c0, n, ract = unit_ctx(k)
            for m in range(8):
                bk, br = bank()
                MM(acc(bk[:, 0:n], [(wd[:, jj, m * 128:(m + 1) * 128], actB.ap[:, k % 2, jj, 0:n]) for jj in range(nj)]),
                   [RW[slot], ract], [br])
                TTo("dve", h[:, m, c0:c0 + n], bk[:, 0:n], h[:, m, c0:c0 + n], ALU.add, [br], [Rh[i][m]])

        if len(GROUPS) > 1:
            ffw[(l, 1)] = load_ffn(l, 1, (slot0 + 1) % 2)
        emit_up(0)
        for k in range(len(units)):
            if k + 1 < len(units):
                emit_up(k + 1)
            emit_down(k)
            gi, i = units[k]
            if i == 4 and gi + 2 < len(GROUPS):
                ffw[(l, gi + 2)] = load_ffn(l, gi + 2, (slot0 + gi) % 2)
        for b_ in (cgB, sgtB, actB, upB, hsB):
            A.free(b_)
        ostB = A.alloc("ost", [1408], F32)
        for q4 in range(4):
            rost = ostB.res()
            for t3 in range(3):
                nq = 4 if t3 < 2 else 3
                bk, br = bank()
                TR([(bk[0:34, q * 128:(q + 1) * 128], fstB.ap[:, q4 * 11 + t3 * 4 + q, :], identf) for q in range(nq)],
                   [Rfst, Rc], [br])
                CP("act", ostB.ap[0:34, t3 * 512:t3 * 512 + nq * 128], bk[0:34, 0:nq * 128], [br], [rost])
            P.dma("sp", ffn_p[l, :, q4 * 1408:(q4 + 1) * 1408], ostB.ap[0:2, :], reads=[rost])
            P.dma("sp", ffn_s[l, :, q4 * 1408:(q4 + 1) * 1408], ostB.ap[2:34, :], reads=[rost])
        A.free(ostB)
        A.free(fstB)

    def load_ple(l, slot):
        wg = wview(slot * SLOT, 8, 1024)
        wp = wview((1 - slot) * SLOT + 4096, 2, 1024)
        load_rows(wg, w_gate[l], 8, [RW[slot]])
        load_rows(wp, w_proj[l], 2, [RW[1 - slot]])
        return wg, wp

    def ple(l, slot, wg, wp, prefetch=None):
        pTB = A.alloc("pT", [2, T], BF16)
        pinB = A.alloc("pin", [2, 256], F32)
        RpT = [pTB.res(i) for i in range(5)]
        for blk in range(NBLK):
            i = min(blk // 4, 4)
            rp = pinB.res(blk % 2)
            P.dma("sp", pinB.ap[:, blk % 2, :], p_all[l, blk * 128:(blk + 1) * 128, :], writes=[rp])
            bk, br = bank()
            TR([(bk[:, q * 128:(q + 1) * 128], pinB.ap[:, blk % 2, q * 128:(q + 1) * 128], identf) for q in range(2)],
               [rp, Rc], [br])
            CP("act", pTB.ap[:, :, blk * 128:(blk + 1) * 128], bk[:, 0:256].rearrange("p (a b) -> p a b", a=2), [br], [RpT[i]])
        A.free(pinB)
        for i in range(5):
            rmsnorm(i, "g_ple", l)
        if prefetch is not None:
            prefetch()
        gtB = A.alloc("gt", [2, 512], F32)
        tmB = A.alloc("tm", [2, 512], F32)
        for i in range(5):
            c0, n = TT[i]
            for m in range(8):
                bg, brg = bank()
                bp, brp = bank()
                MM(acc(bg[:, 0:n], [(wg[:, kc, m * 128:(m + 1) * 128], xn[:, kc, c0:c0 + n]) for kc in range(8)]),
                   [RW[slot], Rxn[i]], [brg])
                MM(acc(bp[:, 0:n], [(wp[:, kc, m * 128:(m + 1) * 128], pTB.ap[:, kc, c0:c0 + n]) for kc in range(2)]),
                   [RW[1 - slot], RpT[i]], [brp])
                rgt = gtB.res(m % 2)
                rtm = tmB.res(m % 2)
                ACT(gtB.ap[:, m % 2, 0:n], bg[:, 0:n], AF.Sigmoid, [brg], [rgt])
                TTo("dve", tmB.ap[:, m % 2, 0:n], bp[:, 0:n], gtB.ap[:, m % 2, 0:n], ALU.mult, [brp, rgt], [rtm])
                TTo("pool", h[:, m, c0:c0 + n], h[:, m, c0:c0 + n], tmB.ap[:, m % 2, 0:n], ALU.add, [rtm], [Rh[i][m]])
        for b_ in (gtB, tmB, pTB):
            A.free(b_)

    ple_w = {}

    def pf_ple0():
        ple_w[0] = load_ple(0, 1)

    ffn(0, 1, after_first_group=None)
    if stop("ffn0"):
        return nc
    pf_ple0()

    kvw = {}

    def load_kv():
        base = 0
        wk = wview(base, 8, 256)
        wv = wview(base + 2048, 8, 256)
        P.dma_many("pool", [(wk[:, kc, :], w_k[kc * 128:(kc + 1) * 128, :]) for kc in range(8)] +
                   [(wv[:, kc, :], w_v[kc * 128:(kc + 1) * 128, :]) for kc in range(8)], writes=[RW[0]])
        kvw["k"], kvw["v"] = wk, wv

    ple(0, 1, ple_w[0][0], ple_w[0][1], prefetch=load_kv)
    if stop("ple0"):
        return nc

    KTB = A.alloc("KT", [2, T], BF16)
    KT = KTB.ap
    RKT = [KTB.res(i) for i in range(5)]
    VtB = A.alloc("Vtok", [NBLK, 256], BF16)
    Vt = VtB.ap
    RVt = [VtB.res(b) for b in range(NBLK)]
    for i in range(5):
        rmsnorm(i, "kv_g", 0)

    wqo = {}

    def load_wq():
        wqo["q"] = wview(SLOT, 8, 1024)
        stg = A.alloc("wqstg", [8, 1024], BF16)
        load_rows(stg.ap, w_q, 8, [stg.res()])
        for gp in range(2):
            for hf in range(2):
                dst = wqo["q"][:, :, gp * 512:(gp + 1) * 512].rearrange("p k (j hf d) -> p k j hf d", j=4, hf=2)[:, :, :, hf, :]
                src = stg.ap[:, :, gp * 512:(gp + 1) * 512].rearrange("p k (hf j d) -> p k hf j d", hf=2, j=4)[:, :, hf, :, :]
                CP("pool", dst, src, [stg.res()], [RW[1]])
        A.free(stg)

    load_wq()
    if stop("kv0"):
        return nc
    knoB = A.alloc("kno", [2, 256], F32)
    Rkno = knoB.res()
    kfB = A.alloc("kf", [2, 512], F32)
    ksqB = A.alloc("ksq", [2, 512], BF16)
    krsB = A.alloc("krs", [2, 512], F32)
    for i in range(5):
        c0, n = TT[i]
        for gp in range(2):
            bk, br = bank()
            MM(acc(bk[:, 0:n], [(kvw["k"][:, kc, gp * 128:(gp + 1) * 128], xn[:, kc, c0:c0 + n]) for kc in range(8)]),
               [RW[0], Rxn[i]], [br])
            rkf, rsq, rrs = kfB.res(gp), ksqB.res(gp), krsB.res(gp)
            kf = kfB.ap[:, gp, 0:n]
            CP("act", kf, bk[:, 0:n], [br], [rkf])
            ACT(ksqB.ap[:, gp, 0:n], bk[:, 0:n], AF.Square, [br], [rsq])
            b2, br2 = bank()
            MM([dict(out=b2[:, 0:n], lhsT=ones64, rhs=ksqB.ap[:, gp, 0:n], start=True, stop=True)], [rsq, Rcb], [br2])
            RSTD(krsB.ap[:, gp, 0:n], b2[:, 0:n], [br2], [rrs])
            STT("dve", KT[:, gp, c0:c0 + n], kf, vcol("gk2"), krsB.ap[:, gp, 0:n], ALU.mult, ALU.mult, [rkf, rrs, Rvt], [RKT[i]])
            if i == 3:
                STT("dve", knoB.ap[:, gp, 0:128], kf[:, 384:512], vcol("gk2"), krsB.ap[:, gp, 384:512], ALU.mult, ALU.mult,
                    [rkf, rrs, Rvt], [Rkno])
            if i == 4:
                STT("dve", knoB.ap[:, gp, 128:256], kf, vcol("gk2"), krsB.ap[:, gp, 0:128], ALU.mult, ALU.mult,
                    [rkf, rrs, Rvt], [Rkno])
    for b_ in (kfB, ksqB, krsB):
        A.free(b_)
    if stop("kv1"):
        return nc
    voB = A.alloc("vo", [2, 256], F32)
    for blk in range(NBLK):
        i = min(blk // 4, 4)
        bk, br = bank()
        MM(acc(bk[:, 0:256], [(xn[:, kc, blk * 128:(blk + 1) * 128], kvw["v"][:, kc, :]) for kc in range(8)]),
           [RW[0], Rxn[i]], [br])
        CP("act", Vt[:, blk, :], bk[:, 0:256], [br], [RVt[blk]])
        if blk >= 15 and os.environ.get("KSUB", "") != "a":
            rvo = voB.res(blk - 15)
            CP("act", voB.ap[:, blk - 15, :], bk[:, 0:256], [br], [rvo])
            if os.environ.get("KSUB", "") == "c":
                pass
            elif blk == 15:
                P.dma("sp", v_p, voB.ap[:, 0, :], reads=[rvo])
            elif os.environ.get("KSUB", "") != "b":
                for b in range(16):
                    P.dma("sp", v_s[b, 120:128, :], voB.ap[8 * b:8 * b + 8, 1, :], reads=[rvo])
    A.free(voB)
    if stop("kv2"):
        return nc
    koB = A.alloc("ko", [2, 256], F32)
    for w in range(2):
        bk, br = bank()
        TR([(bk[:, gp * 128:(gp + 1) * 128], knoB.ap[:, gp, w * 128:(w + 1) * 128], identf) for gp in range(2)], [Rkno, Rc], [br])
        rko = koB.res(w)
        CP("act", koB.ap[:, w, :], bk[:, 0:256], [br], [rko])
        if w == 0:
            P.dma("sp", k_p, koB.ap[:, 0, :], reads=[rko])
        else:
            for b in range(16):
                P.dma("sp", k_s[b, 120:128, :], koB.ap[8 * b:8 * b + 8, 1, :], reads=[rko])
    A.free(koB)
    A.free(knoB)
    P.dma("sp", k_s[:, 0:120, :], cache_k[:, 8:128, :])
    P.dma("sp", v_s[:, 0:120, :], cache_v[:, 8:128, :])
    if stop("kv"):
        return nc

    EBc, EBp = build_eb()
    for i in range(5):
        rmsnorm(i, "g_mix", 1)
    esB = A.alloc("es", [8], F32)
    Res_es = esB.res()
    ACT(esB.ap, vt[:, VOFF["sinks"]:VOFF["sinks"] + 8], AF.Exp, [Rvt], [Res_es])

    EB_ = A.alloc("E", [2, 512], F32)
    PTB = A.alloc("PT", [2, 512], BF16)
    denB = A.alloc("den", [2, 512], F32)
    EBn = A.alloc("EBn", [16, 128], BF16)
    for hh in range(16):
        TTo("pool", EBn.ap[:, hh, :], EBc.ap[:, hh, :], bmask, ALU.mult, [EBc.res(), Rc], [EBn.res()])
    EBs = A.alloc("EBs", [4, 16, 4, 8], BF16)
    for b in range(16):
        CP("pool", EBs.ap[:, :, b, :, :], EBp.ap[:, :, 0:8].rearrange("p (g j) q -> p g j q", g=4), [EBp.res()], [EBs.res()])
    class _View:
        def __init__(self, ap, r):
            self.ap = ap
            self._r = r

        def res(self, key=None):
            return self._r

    KcT = _View(W[:, 0:4096].rearrange("p (b g s) -> p b g s", b=16, g=2), RW[0])
    VcB = _View(W[:, 4096:8192].rearrange("p (b f) -> p b f", b=16), RW[0])
    P.dma("pool", VcB.ap, cache_v.rearrange("b s f -> s b f"), writes=[VcB.res()])
    ckB = A.alloc("ck", [8, 256], F32)
    for hb in range(2):
        rck = ckB.res()
        P.dma("sp", ckB.ap, cache_k[8 * hb:8 * hb + 8].rearrange("b s f -> s b f"), writes=[rck])
        for b2 in range(0, 8, 2):
            bk, br = bank()
            TR([(bk[:, (bb * 2 + gp) * 128:(bb * 2 + gp + 1) * 128], ckB.ap[:, b2 + bb, gp * 128:(gp + 1) * 128], identf)
                for bb in range(2) for gp in range(2)], [rck, Rc], [br])
            CP("act", KcT.ap[:, 8 * hb + b2:8 * hb + b2 + 2, :, :], bk[:, :].rearrange("p (b g s) -> p b g s", b=2, g=2),
               [br], [KcT.res()])
    A.free(ckB)

    def alloc_q(n):
        return (A.alloc("QT", [2, 8, n], BF16), A.alloc("qf", [2, n], F32), A.alloc("qsq", [2, n], BF16),
                A.alloc("qrs", [2, n], F32))

    QTB, qfB, qsqB, qrsB = alloc_q(128)
    est = {"n": 0}

    def qproj(i):
        c0, n = TT[i]
        for m in range(8):
            ha, hb_ = _chunk_heads(m)
            bk, br = bank()
            lw = wqo["q"]
            MM(acc(bk[:, 0:n], [(lw[:, kc, m * 128:(m + 1) * 128], xn[:, kc, c0:c0 + n]) for kc in range(8)]),
               [RW[1], Rxn[i]], [br])
            par = m % 2
            rqf, rsq, rrs = qfB.res(par), qsqB.res(par), qrsB.res(par)
            CP("act", qfB.ap[:, par, 0:n], bk[:, 0:n], [br], [rqf])
            ACT(qsqB.ap[:, par, 0:n], bk[:, 0:n], AF.Square, [br], [rsq])
            b2, br2 = bank()
            MM([dict(out=b2[:, 0:n], lhsT=ones64, rhs=qsqB.ap[:, par, 0:n], start=True, stop=True)], [rsq, Rcb], [br2])
            RSTD(qrsB.ap[:, par, 0:n], b2[:, 0:n], [br2], [rrs])
            STT("dve", QTB.ap[:, i % 2, m, 0:n], qfB.ap[:, par, 0:n], vcol("gq2"), qrsB.ap[:, par, 0:n], ALU.mult, ALU.mult,
                [rqf, rrs, Rvt], [QTB.res(i % 2)])

    def attn_units(i, qoff, nb, sample):
        qp = i % 2
        RQ = QTB.res(qp)
        ocol = nb * 128
        if sample:
            parts = [(16, EBn)]
        else:
            parts = ([(nb - 1, EBp)] if nb > 0 else []) + [(nb, EBc)]
        units = []
        for gp in range(2):
            ctx = {}

            def banks_(ctx=ctx):
                if "bo" not in ctx:
                    ctx["bo"], ctx["bro"] = bank(hold=True)
                    ctx["bs"], ctx["brs"] = bank(hold=True)
                return ctx["bo"], ctx["bro"], ctx["bs"], ctx["brs"]

            for hf in range(2):
                g = 2 * gp + hf
                rows = slice(hf * 64, (hf + 1) * 64)
                for pi, (kblk, EBt) in enumerate(parts):
                    st = {}

                    def A(st=st, gp=gp, hf=hf, g=g, rows=rows, kblk=kblk, EBt=EBt):
                        ki = min(kblk // 4, 4)
                        bk, br = bank()
                        MM([dict(out=bk[:, j * 128:(j + 1) * 128], lhsT=KT[rows, gp, kblk * 128:(kblk + 1) * 128],
                                 rhs=QTB.ap[rows, qp, 4 * gp + j, qoff:qoff + 128], start=True, stop=True, tp=(hf * 64, 0))
                            for j in range(4)], [RKT[ki], RQ], [br])
                        par = est["n"] % 2
                        est["n"] += 1
                        rE, rPT = EB_.res(par), PTB.res(par)
                        ACT(EB_.ap[:, par, :], bk[:, :], AF.Exp, [br], [rE], scale=SCALE)
                        TTo("dve", PTB.ap[:, par, :], EB_.ap[:, par, :],
                            EBt.ap[:, 4 * g:4 * g + 4, :].rearrange("p a b -> p (a b)"), ALU.mult, [rE, EBt.res()], [rPT])
                        st["par"], st["rPT"] = par, rPT

                    def B(st=st, gp=gp, hf=hf, g=g, rows=rows, kblk=kblk, pi=pi, banks_=banks_):
                        bo, bro, bs, brs = banks_()
                        par, rPT = st["par"], st["rPT"]
                        lastp = (pi == len(parts) - 1) and not sample
                        MM([dict(out=bo[rows, j * 128:(j + 1) * 128], lhsT=Vt[:, kblk, g * 64:(g + 1) * 64],
                                 rhs=PTB.ap[:, par, j * 128:(j + 1) * 128], start=(pi == 0 and j == 0),
                                 stop=(lastp and j == 3), tp=(0, hf * 64), sgc=True) for j in range(4)], [RVt[kblk], rPT], [bro])
                        MM([dict(out=bs[rows, j * 128:(j + 1) * 128], lhsT=ones1[:, 0:64],
                                 rhs=PTB.ap[:, par, j * 128:(j + 1) * 128], start=(pi == 0 and j == 0),
                                 stop=(lastp and j == 3), tp=(0, hf * 64), sgc=True) for j in range(4)], [Rcb, rPT], [brs])

                    units.append([A, B])
                if sample:
                    st = {}

                    def A2(st=st, gp=gp, hf=hf, g=g, rows=rows):
                        bk, br = bank()
                        MM([dict(out=bk[:, (b * 4 + j) * 8:(b * 4 + j) * 8 + 8], lhsT=KcT.ap[rows, b, gp, :],
                                 rhs=QTB.ap[rows, qp, 4 * gp + j, 8 * b:8 * b + 8], start=True, stop=True, tp=(hf * 64, 0))
                            for b in range(16) for j in range(4)], [KcT.res(), RQ], [br])
                        par = est["n"] % 2
                        est["n"] += 1
                        rE, rPT = EB_.res(par), PTB.res(par)
                        ACT(EB_.ap[:, par, :], bk[:, :], AF.Exp, [br], [rE], scale=SCALE)
                        TTo("dve", PTB.ap[:, par, :], EB_.ap[:, par, :],
                            EBs.ap[:, g, :, :, :].rearrange("p a b c -> p (a b c)"), ALU.mult, [rE, EBs.res()], [rPT])
                        st["par"], st["rPT"] = par, rPT

                    def B2(st=st, gp=gp, hf=hf, g=g, rows=rows, banks_=banks_):
                        bo, bro, bs, brs = banks_()
                        par, rPT = st["par"], st["rPT"]
                        MM([dict(out=bo[rows, j * 128 + 8 * b:j * 128 + 8 * b + 8], lhsT=VcB.ap[:, b, g * 64:(g + 1) * 64],
                                 rhs=PTB.ap[:, par, (b * 4 + j) * 8:(b * 4 + j) * 8 + 8], start=False,
                                 stop=(b == 15 and j == 3), tp=(0, hf * 64), sgc=True) for b in range(16) for j in range(4)],
                           [VcB.res(), rPT], [bro])
                        MM([dict(out=bs[rows, j * 128 + 8 * b:j * 128 + 8 * b + 8], lhsT=ones1[:, 0:64],
                                 rhs=PTB.ap[:, par, (b * 4 + j) * 8:(b * 4 + j) * 8 + 8], start=False,
                                 stop=(b == 15 and j == 3), tp=(0, hf * 64), sgc=True) for b in range(16) for j in range(4)],
                           [Rcb, rPT], [brs])

                    units.append([A2, B2])

            def N(gp=gp, banks_=banks_):
                bo, bro, bs, brs = banks_()
                rden = denB.res(gp)
                for j in range(4):
                    TS("dve", denB.ap[:, gp, j * 128:(j + 1) * 128], bs[:, j * 128:(j + 1) * 128],
                       esB.ap[:, 4 * gp + j:4 * gp + j + 1], None, ALU.add, None, [brs, Res_es], [rden])
                ACT(denB.ap[:, gp, :], denB.ap[:, gp, :], AF.Ln, [rden], [rden])
                ACT(denB.ap[:, gp, :], denB.ap[:, gp, :], AF.Exp, [rden], [rden], scale=-1.0)
                TTo("dve", xn[:, 4 * gp:4 * gp + 4, ocol:ocol + 128], bo[:, :].rearrange("p (a b) -> p a b", a=4),
                    denB.ap[:, gp, :].rearrange("p (a b) -> p a b", a=4), ALU.mult, [bro, rden], [Rxn[i]])
                release(bo)
                release(bs)

            units[-1].append(N)
        return units

    def run_units(units):
        if not units:
            return
        units[0][0]()
        for u in range(len(units)):
            if u + 1 < len(units):
                units[u + 1][0]()
            for f in units[u][1:]:
                f()

    qproj(4)
    run_units(attn_units(4, 0, 16, True))
    for b_ in (EBn, EBs, QTB, qfB, qsqB, qrsB):
        A.free(b_)
    QTB, qfB, qsqB, qrsB = alloc_q(512)

    def load_wo():
        wv_ = wview(0, 8, 1024)
        pairs = []
        for c in range(8):
            ha, hb_ = _chunk_heads(c)
            pairs.append((wv_[0:64, c, :], w_o[ha * 64:ha * 64 + 64, :]))
            pairs.append((wv_[64:128, c, :], w_o[hb_ * 64:hb_ * 64 + 64, :]))
        P.dma_many("pool", pairs, writes=[RW[0]])
        wqo["o"] = wv_

    load_wo()
    allu = []
    qproj(0)
    for i in range(4):
        for q4 in range(4):
            allu.extend(attn_units(i, q4 * 128, 4 * i + q4, False))
            if q4 == 1 and i + 1 < 4:
                allu.append([(lambda ii: (lambda: qproj(ii)))(i + 1), (lambda: None)])
    run_units(allu)
    for b_ in (QTB, qfB, qsqB, qrsB, EB_, PTB, denB, esB, KTB, VtB, EBc, EBp):
        A.free(b_)
    ffw[(1, 0)] = load_ffn(1, 0, 1)
    for i in [4, 0, 1, 2, 3]:
        proj_residual(wqo["o"], i, [RW[0]], None)
    if stop("attn"):
        return nc

    ffn(1, 1)
    ple_w[1] = load_ple(1, 1)
    ple(1, 1, ple_w[1][0], ple_w[1][1])

    youtB = A.alloc("yout", [2, D], F32)
    for blk in range(NBLK):
        i = min(blk // 4, 4)
        ry = youtB.res(blk % 2)
        for half in range(2):
            bk, br = bank()
            TR([(bk[:, q * 128:(q + 1) * 128], h[:, 4 * half + q, blk * 128:(blk + 1) * 128], identf) for q in range(4)],
               [Rh[i][c] for c in range(4 * half, 4 * half + 4)] + [Rc], [br])
            CP("act" if half == 0 else "dve", youtB.ap[:, blk % 2, half * 512:(half + 1) * 512], bk[:, :], [br], [ry])
        P.dma("sp", y[blk * 128:(blk + 1) * 128, :], youtB.ap[:, blk % 2, :], reads=[ry])

    P.finish()
    return nc


def _t5_bucket_np(d):
    d = np.asarray(d)
    df = np.maximum(d, 1).astype(np.float32)
    large = 16 + (np.log(df / np.float32(16)) / np.float32(math.log(128 / 16)) * np.float32(16)).astype(np.int32)
    large = np.minimum(large, 31)
    return np.where(d < 16, d, large)


def _host_consts():
    cst = np.zeros((128, 256), np.float32)
    cst[:, 0:128] = np.eye(128, dtype=np.float32)
    idx = np.arange(128)
    cst[:, 128:256] = (idx[:, None] // 8 == idx[None, :] // 8).astype(np.float32)
    ohu = np.zeros((33, LU), np.float32)
    for j in range(LU):
        d = j - 127
        if 0 <= d < 128:
            ohu[int(_t5_bucket_np(d)), j] = 1.0
        else:
            ohu[32, j] = 1.0
    return cst, ohu


_NC_CACHE = {}


def kernel(x_prompt, x_sample, state_conv, state_ffn, cache_k, cache_v, p_prompt, p_sample,
           g_mix, cm_w_pw1, cm_b_pw1, cm_w_dw, cm_b_dw, cm_ln_g, cm_ln_b, cm_w_pw2, cm_b_pw2,
           at_w_q, at_g_q, at_sinks, at_w_o, kv_g, kv_w_k, kv_w_v, kv_g_k, rel_bias,
           g_ffn, ffn_w_up, ffn_w_dw, ffn_b_dw, ffn_w_down, g_ple, ple_w_gate, ple_w_proj):
    f = lambda a: np.ascontiguousarray(np.asarray(a, dtype=np.float32))
    sinks = f(at_sinks).reshape(16)
    sink_rows = np.stack([np.concatenate([np.full(64, sinks[_chunk_heads(c)[0]], np.float32),
                                          np.full(64, sinks[_chunk_heads(c)[1]], np.float32)]) for c in range(8)])
    parts = {
        "g_mix": f(g_mix).ravel(), "b_pw1": f(cm_b_pw1).ravel(), "w_dw": f(cm_w_dw).ravel(), "b_dw": f(cm_b_dw).ravel(),
        "ln_g": f(cm_ln_g).ravel(), "ln_b": f(cm_ln_b).ravel(), "b_pw2": f(cm_b_pw2).ravel(), "kv_g": f(kv_g).ravel(),
        "g_ffn": f(g_ffn).ravel(), "f_w_dw": f(ffn_w_dw).ravel(), "f_b_dw": f(ffn_b_dw).ravel(), "g_ple": f(g_ple).ravel(),
        "gq2": np.tile(f(at_g_q).ravel(), 2), "gk2": np.tile(f(kv_g_k).ravel(), 2), "sinks": sink_rows.ravel(),
    }
    vecs = np.zeros((VROWS, 128), np.float32)
    for name, off in VOFF.items():
        v = parts[name].reshape(-1, 128)
        vecs[off:off + v.shape[0]] = v
    cst, ohu = _host_consts()
    shared = {
        "vecs": vecs, "cst": cst, "ohu": ohu, "rel_bias": f(rel_bias),
        "w_pw1": f(cm_w_pw1)[0], "w_pw2": f(cm_w_pw2)[0], "w_q": f(at_w_q)[0], "w_o": f(at_w_o)[0],
        "w_k": f(kv_w_k), "w_v": f(kv_w_v), "w_up": f(ffn_w_up), "w_down": f(ffn_w_down),
        "w_gate": f(ple_w_gate), "w_proj": f(ple_w_proj),
    }
    xp, xs = f(x_prompt), f(x_sample)
    pp, ps = f(p_prompt), f(p_sample)
    sc, sf = f(state_conv), f(state_ffn)
    ck, cv = f(cache_k), f(cache_v)
    in_maps = []
    for c in range(NCORES):
        sl = slice(16 * c, 16 * c + 16)
        m = dict(shared)
        m["x_all"] = np.concatenate([xp[c], xs[sl].reshape(128, D)], axis=0)
        m["p_all"] = np.concatenate([pp[:, c], ps[:, sl].reshape(2, 128, 256)], axis=1)
        m["st_conv"] = np.ascontiguousarray(sc[0, sl])
        m["st_ffn"] = np.ascontiguousarray(sf[:, sl].reshape(2, 32, 2 * DFF))
        m["cache_k"] = np.ascontiguousarray(ck[sl].reshape(16, 128, 256))
        m["cache_v"] = np.ascontiguousarray(cv[sl].reshape(16, 128, 256))
        in_maps.append(m)
    if "nc" not in _NC_CACHE:
        _NC_CACHE["nc"] = build_program()
    nc = _NC_CACHE["nc"]
    res = run_bass_kernel_spmd(nc, in_maps, core_ids=list(range(NCORES)))
    R = res.results
    y_prompt = np.stack([R[c]["y"][0:2048] for c in range(NCORES)])
    y_sample = np.concatenate([R[c]["y"][2048:].reshape(16, 8, D) for c in range(NCORES)], axis=0)
    conv_p = np.stack([R[c]["conv_p"] for c in range(NCORES)])[None]
    conv_s = np.concatenate([R[c]["conv_s"] for c in range(NCORES)], axis=0)[None]
    ffn_p = np.stack([R[c]["ffn_p"] for c in range(NCORES)], axis=1)
    ffn_s = np.concatenate([R[c]["ffn_s"].reshape(2, 16, 2, 2 * DFF) for c in range(NCORES)], axis=1)
    k_p = np.stack([R[c]["k_p"].reshape(128, 4, 64) for c in range(NCORES)])
    k_s = np.concatenate([R[c]["k_s"].reshape(16, 128, 4, 64) for c in range(NCORES)], axis=0)
    v_p = np.stack([R[c]["v_p"].reshape(128, 4, 64) for c in range(NCORES)])
    v_s = np.concatenate([R[c]["v_s"].reshape(16, 128, 4, 64) for c in range(NCORES)], axis=0)
    outs = (y_prompt, y_sample, conv_p, conv_s, ffn_p, ffn_s, k_p, k_s, v_p, v_s)
    return tuple(np.ascontiguousarray(o, dtype=np.float32) for o in outs)
```

```python
import math
import os
import numpy as np
import concourse.bass as bass
import concourse.mybir as mybir
from concourse.bass_utils import run_bass_kernel_spmd

F32 = mybir.dt.float32
BF16 = mybir.dt.bfloat16
U8 = mybir.dt.uint8
AF = mybir.ActivationFunctionType
ALU = mybir.AluOpType

NCORES = 8
D = 1024
DFF = 2816
NF = 44
T = 2176
TT = [(0, 512), (512, 512), (1024, 512), (1536, 512), (2048, 128)]
NBLK = 17
EPS = 1e-6
SCALE = 0.125
LU = 383
GROUPS = [(0, 3), (3, 3), (6, 3), (9, 3), (12, 3), (15, 3), (18, 2), (20, 2)]
SLOT = 9216
UW = 30 + 2048 + 16 * 38
SB0 = 2078


def _vec_layout():
    ents = [("g_mix", 16), ("b_pw1", 16), ("w_dw", 248), ("b_dw", 8), ("ln_g", 8), ("ln_b", 8), ("b_pw2", 8),
            ("kv_g", 8), ("g_ffn", 16), ("f_w_dw", 264), ("f_b_dw", 88), ("g_ple", 16), ("gq2", 1), ("gk2", 1),
            ("sinks", 8)]
    off = {}
    o = 0
    for n, c in ents:
        off[n] = o
        o += c
    return off, o


VOFF, VN = _vec_layout()
VROWS = 768


def _chunk_heads(c):
    gp, j = c // 4, c % 4
    return 8 * gp + j, 8 * gp + 4 + j


class Res:
    __slots__ = ("w", "r", "name")

    def __init__(self, name=""):
        self.w = []
        self.r = []
        self.name = name


class Prog:
    def __init__(self, nc, n_dma_sems=10):
        self.nc = nc
        self.names = ["pe", "act", "dve", "pool", "sp"]
        self.ops = {n: [] for n in self.names}
        self.cnt = {n: 0 for n in self.names}
        self.seen = {n: {} for n in self.names}
        self.sem = {}
        for n in ["pe", "act", "dve", "pool"]:
            self.sem[n] = nc.alloc_semaphore("s_" + n)
        self.dq = ["sp", "pool"]
        self.dsem = {q: [nc.alloc_semaphore(f"d_{q}{i}") for i in range(n_dma_sems)] for q in self.dq}
        self.dcnt = {q: [0] * n_dma_sems for q in self.dq}
        self.dnext = {q: 0 for q in self.dq}

    def _waits(self, eng, evs):
        best = {}
        for ev in evs:
            if ev is None:
                continue
            s, v = ev
            k = id(s)
            if k not in best or best[k][1] < v:
                best[k] = (s, v)
        out = []
        seen = self.seen[eng]
        for k, (s, v) in best.items():
            if seen.get(k, 0) >= v:
                continue
            seen[k] = v
            out.append((s, v))
        return out

    @staticmethod
    def _deps(reads, writes, extra):
        evs = list(extra)
        for r in reads:
            evs.extend(r.w)
        for w in writes:
            evs.extend(w.w)
            evs.extend(w.r)
        return evs

    @staticmethod
    def _commit(ev, reads, writes):
        evl = ev if isinstance(ev, list) else [ev]
        for r in reads:
            r.r.extend(evl)
        for w in writes:
            w.w = list(evl)
            w.r = []

    def op(self, eng, fn, reads=(), writes=(), extra=()):
        waits = self._waits(eng, self._deps(reads, writes, extra))
        self.cnt[eng] += 1
        sem = self.sem[eng]
        ev = (sem, self.cnt[eng])

        def run(e):
            for s, v in waits:
                e.wait_ge(s, v)
            fn(e).then_inc(sem, 1)

        self.ops[eng].append(run)
        self._commit(ev, reads, writes)
        return ev

    def dma(self, q, out, in_, reads=(), writes=(), extra=()):
        i = self.dnext[q]
        self.dnext[q] = (i + 1) % len(self.dsem[q])
        s = self.dsem[q][i]
        prev = self.dcnt[q][i]
        evs = self._deps(reads, writes, extra)
        if prev > 0:
            evs.append((s, prev))
        waits = self._waits(q, evs)
        self.dcnt[q][i] = prev + 16
        ev = (s, prev + 16)

        def run(e):
            for s2, v in waits:
                e.wait_ge(s2, v)
            e.dma_start(out=out, in_=in_).then_inc(s, 16)

        self.ops[q].append(run)
        self._commit(ev, reads, writes)
        return ev

    def dma_many(self, q, pairs, reads=(), writes=()):
        evs0 = self._deps(reads, writes, ())
        out_evs = []
        for (out, in_) in pairs:
            i = self.dnext[q]
            self.dnext[q] = (i + 1) % len(self.dsem[q])
            s = self.dsem[q][i]
            prev = self.dcnt[q][i]
            evs = list(evs0) + ([(s, prev)] if prev > 0 else [])
            waits = self._waits(q, evs)
            self.dcnt[q][i] = prev + 16
            out_evs.append((s, prev + 16))

            def run(e, waits=waits, out=out, in_=in_, s=s):
                for s2, v in waits:
                    e.wait_ge(s2, v)
                e.dma_start(out=out, in_=in_).then_inc(s, 16)

            self.ops[q].append(run)
        self._commit(out_evs, reads, writes)
        return out_evs

    def finish(self):
        nc = self.nc
        finals = [(s, c) for q in self.dq for s, c in zip(self.dsem[q], self.dcnt[q]) if c > 0]
        waits = self._waits("sp", finals)

        def fin(e):
            for s, v in waits:
                e.wait_ge(s, v)

        self.ops["sp"].append(fin)
        with nc.Block() as block:
            @block.tensor
            def _(e):
                for f in self.ops["pe"]:
                    f(e)

            @block.scalar
            def _(e):
                for f in self.ops["act"]:
                    f(e)

            @block.vector
            def _(e):
                for f in self.ops["dve"]:
                    f(e)

            @block.gpsimd
            def _(e):
                for f in self.ops["pool"]:
                    f(e)

            @block.sync
            def _(e):
                for f in self.ops["sp"]:
                    f(e)


class Buf:
    def __init__(self, ap, off, nbytes, inherit, name):
        self.ap = ap
        self.off = off
        self.nbytes = nbytes
        self.inherit = inherit
        self.name = name
        self.R = {}

    def res(self, key=None):
        if key not in self.R:
            r = Res(f"{self.name}:{key}")
            r.r = list(self.inherit)
            self.R[key] = r
        return self.R[key]

    def all_events(self):
        evs = list(self.inherit)
        for r in self.R.values():
            evs.extend(r.w)
            evs.extend(r.r)
        return evs


class Arena:
    def __init__(self, nc, nbytes):
        self.t = nc.alloc_sbuf_tensor("arena", [128, nbytes], U8)
        self.n = nbytes
        self.live = []
        self.dead = []
        self.peak = 0

    def alloc(self, name, shape, dtype):
        esz = 4 if dtype == F32 else 2
        n = 1
        for s in shape:
            n *= s
        nbytes = ((n * esz + 63) // 64) * 64
        spans = sorted((b.off, b.off + b.nbytes) for b in self.live)
        off = 0
        for a, b in spans:
            if off + nbytes <= a:
                break
            off = max(off, b)
        if off + nbytes > self.n:
            raise RuntimeError(f"arena OOM allocating {name} ({nbytes}); live={[(b.name, b.nbytes) for b in self.live]}")
        self.peak = max(self.peak, off + nbytes)
        inherit = []
        keep = []
        for (o2, n2, evs) in self.dead:
            if o2 < off + nbytes and off < o2 + n2:
                inherit.extend(evs)
                if not (off <= o2 and o2 + n2 <= off + nbytes):
                    keep.append((o2, n2, evs))
            else:
                keep.append((o2, n2, evs))
        self.dead = keep
        v = self.t[:, off:off + n * esz].bitcast(dtype)
        if len(shape) == 2:
            v = v.rearrange("p (a b) -> p a b", a=shape[0])
        elif len(shape) == 3:
            v = v.rearrange("p (a b c) -> p a b c", a=shape[0], b=shape[1])
        elif len(shape) == 4:
            v = v.rearrange("p (a b c d) -> p a b c d", a=shape[0], b=shape[1], c=shape[2])
        buf = Buf(v, off, nbytes, inherit, name)
        self.live.append(buf)
        return buf

    def free(self, buf):
        self.live.remove(buf)
        self.dead.append((buf.off, buf.nbytes, buf.all_events()))


def build_program():
    nc = bass.Bass("TRN2", target_bir_lowering=False)
    P = Prog(nc)

    def din(name, shape):
        return nc.dram_tensor(name, list(shape), F32, kind="ExternalInput").ap()

    def dout(name, shape):
        return nc.dram_tensor(name, list(shape), F32, kind="ExternalOutput").ap()

    x_all = din("x_all", [T, D])
    p_all = din("p_all", [2, T, 256])
    st_conv = din("st_conv", [16, 30, D])
    st_ffn = din("st_ffn", [2, 32, 2 * DFF])
    cache_k = din("cache_k", [16, 128, 256])
    cache_v = din("cache_v", [16, 128, 256])
    vecs = din("vecs", [VROWS, 128])
    cst = din("cst", [128, 256])
    ohu = din("ohu", [33, LU])
    rel_bias = din("rel_bias", [32, 16])
    w_pw1 = din("w_pw1", [D, 2 * D])
    w_pw2 = din("w_pw2", [D, D])
    w_q = din("w_q", [D, D])
    w_o = din("w_o", [D, D])
    w_k = din("w_k", [D, 256])
    w_v = din("w_v", [D, 256])
    w_up = din("w_up", [2, D, 2 * DFF])
    w_down = din("w_down", [2, DFF, D])
    w_gate = din("w_gate", [2, D, D])
    w_proj = din("w_proj", [2, 256, D])

    y = dout("y", [T, D])
    conv_p = dout("conv_p", [30, D])
    conv_s = dout("conv_s", [16, 30, D])
    ffn_p = dout("ffn_p", [2, 2, 2 * DFF])
    ffn_s = dout("ffn_s", [2, 32, 2 * DFF])
    k_p = dout("k_p", [128, 256])
    k_s = dout("k_s", [16, 128, 256])
    v_p = dout("v_p", [128, 256])
    v_s = dout("v_s", [16, 128, 256])
    sc2 = nc.dram_tensor("sc2", [16, 128 * (LU + 1)], F32, kind="Internal")

    A = Arena(nc, 211968)
    _stop = os.environ.get("KSTOP", "")

    def stop(tag):
        if _stop == tag:
            P.finish()
            return True
        return False
    banks = [nc.alloc_psum_tensor(f"bank{i}", [128, 512], F32) for i in range(8)]
    bank_res = [Res(f"bank{i}") for i in range(8)]
    bstate = {"n": 0, "held": set()}

    def bank(hold=False):
        while True:
            i = bstate["n"] % 8
            bstate["n"] += 1
            if i not in bstate["held"]:
                break
        if hold:
            bstate["held"].add(i)
        return banks[i], bank_res[i]

    def release(bk):
        bstate["held"].discard(banks.index(bk))

    def ACT(out, in_, func, reads, writes, bias=0.0, scale=1.0):
        return P.op("act", lambda e: e.activation(out=out, in_=in_, func=func, bias=bias, scale=scale), reads, writes)

    def TTo(eng, out, in0, in1, op, reads, writes):
        return P.op(eng, lambda e: e.tensor_tensor(out=out, in0=in0, in1=in1, op=op), reads, writes)

    def TS(eng, out, in0, s1, s2, op0, op1, reads, writes):
        if s2 is None:
            return P.op(eng, lambda e: e.tensor_scalar(out=out, in0=in0, scalar1=s1, scalar2=None, op0=op0), reads, writes)
        return P.op(eng, lambda e: e.tensor_scalar(out=out, in0=in0, scalar1=s1, scalar2=s2, op0=op0, op1=op1), reads, writes)

    def STT(eng, out, in0, scalar, in1, op0, op1, reads, writes, extra=()):
        return P.op(eng, lambda e: e.scalar_tensor_tensor(out=out, in0=in0, scalar=scalar, in1=in1, op0=op0, op1=op1),
                    reads, writes, extra)

    def RSTD(out, in_, reads, writes):
        ACT(out, in_, AF.Ln, list(reads) + [Rcb], writes, bias=epsc)
        return ACT(out, out, AF.Exp, writes, writes, scale=-0.5)

    def CP(eng, out, in_, reads, writes):
        if eng == "act":
            return P.op("act", lambda e: e.activation(out=out, in_=in_, func=AF.Copy), reads, writes)
        return P.op(eng, lambda e: e.tensor_copy(out=out, in_=in_), reads, writes)

    def MEMSET(eng, out, val, writes):
        return P.op(eng, lambda e: e.memset(out, val), (), writes)

    def MM(mms, reads, writes):
        def fn(e):
            last = None
            for m in mms:
                kw = {}
                if m.get("tp") is not None:
                    kw["tile_position"] = m["tp"]
                if m.get("sgc"):
                    kw["skip_group_check"] = True
                last = e.matmul(m["out"], lhsT=m["lhsT"], rhs=m["rhs"], start=m["start"], stop=m["stop"], **kw)
            return last
        return P.op("pe", fn, reads, writes)

    def TR(items, reads, writes):
        def fn(e):
            last = None
            for o, i, idn in items:
                last = e.transpose(o, i, idn)
            return last
        return P.op("pe", fn, reads, writes)

    def acc(out, pairs, tp=None):
        n = len(pairs)
        return [dict(out=out, lhsT=l, rhs=r, start=(k == 0), stop=(k == n - 1), tp=tp) for k, (l, r) in enumerate(pairs)]

    hB = A.alloc("h", [8, T], F32)
    h = hB.ap
    Rh = [[hB.res((i, c)) for c in range(8)] for i in range(5)]
    XN = {}

    def xn_alloc():
        XN["B"] = A.alloc("xn", [8, T], BF16)
        return XN["B"].ap, [XN["B"].res(i) for i in range(5)]

    def xn_free():
        A.free(XN["B"])

    xn, Rxn = xn_alloc()
    WB = A.alloc("W", [2 * SLOT], BF16)
    W = WB.ap
    RW = [WB.res(0), WB.res(1)]
    vtB = A.alloc("vt", [VROWS], F32)
    vt = vtB.ap
    Rvt = vtB.res()
    cB = A.alloc("consts", [2, 128], F32)
    identf = cB.ap[:, 0, :]
    bmask = cB.ap[:, 1, :]
    Rc = cB.res()
    cbB = A.alloc("constsb", [4, 128], BF16)
    identb = cbB.ap[:, 0, :]
    onesN = cbB.ap[:, 1, :]
    ones64 = cbB.ap[:, 2, :]
    ones1 = cbB.ap[:, 3, :]
    Rcb = cbB.res()

    def vcol(name, idx=0):
        c = VOFF[name] + idx
        return vt[:, c:c + 1]

    P.dma("sp", cB.ap[:, 0:2, :], cst.rearrange("p (a b) -> p a b", a=2), writes=[Rc])
    CP("dve", identb, identf, [Rc], [Rcb])
    MEMSET("dve", onesN, 1.0 / 1024.0, [Rcb])
    MEMSET("dve", ones64, 0.0, [Rcb])
    MEMSET("dve", cbB.ap[0:64, 2, 0:64], 1.0 / 64.0, [Rcb])
    MEMSET("dve", cbB.ap[64:128, 2, 64:128], 1.0 / 64.0, [Rcb])
    MEMSET("dve", ones1, 1.0, [Rcb])
    epsB = A.alloc("epsc", [1], F32)
    epsc = epsB.ap[:, 0:1]
    MEMSET("dve", epsc, EPS, [Rcb])

    vinB = A.alloc("vin", [6, 128], F32)
    Rvin = vinB.res()
    P.dma("sp", vinB.ap, vecs.rearrange("(a p) f -> p a f", p=128), writes=[Rvin])
    for half in range(2):
        bk, br = bank()
        TR([(bk[:, q * 128:(q + 1) * 128], vinB.ap[:, half * 3 + q, :], identf) for q in range(3)], [Rvin, Rc], [br])
        CP("dve", vt[:, half * 384:(half + 1) * 384], bk[:, 0:384], [br], [Rvt])
    A.free(vinB)

    def load_rows(dst3, src2, nk, slots):
        P.dma_many("pool", [(dst3[:, kc, :], src2[kc * 128:(kc + 1) * 128, :]) for kc in range(nk)], writes=slots)

    def wview(off, a, b):
        return W[:, off:off + a * b].rearrange("p (a b) -> p a b", a=a)

    def load_pw1():
        load_rows(wview(0, 8, 2048), w_pw1, 8, [RW[0], RW[1]])

    def load_pw2():
        load_rows(wview(0, 8, 1024), w_pw2, 8, [RW[0]])

    def load_ffn(l, gi, slot):
        j0, nj = GROUPS[gi]
        base = slot * SLOT
        wu = wview(base, 8, 2 * nj * 128)
        wd = wview(base + 16 * nj * 128, nj, 1024)
        pairs = []
        for kc in range(8):
            pairs.append((wu[:, kc, 0:nj * 128], w_up[l, kc * 128:(kc + 1) * 128, j0 * 128:(j0 + nj) * 128]))
            pairs.append((wu[:, kc, nj * 128:2 * nj * 128],
                          w_up[l, kc * 128:(kc + 1) * 128, DFF + j0 * 128:DFF + (j0 + nj) * 128]))
        for jj in range(nj):
            pairs.append((wd[:, jj, :], w_down[l, (j0 + jj) * 128:(j0 + jj + 1) * 128, :]))
        P.dma_many("pool", pairs, writes=[RW[slot]])
        return wu, wd

    def build_eb():
        tbB = A.alloc("tbl", [16], F32)
        Rtb = tbB.res()
        ohB = A.alloc("ohu", [LU], F32)
        Roh = ohB.res()
        UB = A.alloc("U", [LU], F32)
        RU = UB.res()
        MEMSET("dve", tbB.ap[32:33, :], -30000.0, [Rtb])
        P.dma("sp", tbB.ap[0:32, :], rel_bias, writes=[Rtb])
        P.dma("sp", ohB.ap[0:33, :], ohu, writes=[Roh])
        bk, br = bank()
        MM([dict(out=bk[0:16, 0:LU], lhsT=tbB.ap[0:33, :], rhs=ohB.ap[0:33, :], start=True, stop=True)], [Rtb, Roh], [br])
        ACT(UB.ap[0:16, :], bk[0:16, 0:LU], AF.Exp, [br], [RU])
        Rsc = Res("sc2")
        pst = UB.ap.ap[0][0]
        P.dma("sp", bass.AP(sc2, 0, [[128 * (LU + 1), 16], [LU + 1, 128], [1, LU]]),
              bass.AP(UB.ap.tensor, UB.ap.offset, [[pst, 16], [0, 128], [1, LU]]), reads=[RU], writes=[Rsc])
        EBc = A.alloc("EBc", [16, 128], BF16)
        EBp = A.alloc("EBp", [16, 128], BF16)
        P.dma("pool", EBc.ap, bass.AP(sc2, 127, [[LU, 128], [128 * (LU + 1), 16], [1, 128]]), reads=[Rsc], writes=[EBc.res()])
        P.dma("pool", EBp.ap, bass.AP(sc2, 255, [[LU, 128], [128 * (LU + 1), 16], [1, 128]]), reads=[Rsc], writes=[EBp.res()])
        A.free(tbB)
        A.free(ohB)
        A.free(UB)
        return EBc, EBp

    load_pw1()

    xinB = A.alloc("xin", [2, D], F32)
    for blk in range(NBLK):
        i = min(blk // 4, 4)
        rx = xinB.res(blk % 2)
        P.dma("sp", xinB.ap[:, blk % 2, :], x_all[blk * 128:(blk + 1) * 128, :], writes=[rx])
        for half in range(2):
            bk, br = bank()
            TR([(bk[:, q * 128:(q + 1) * 128], xinB.ap[:, blk % 2, (4 * half + q) * 128:(4 * half + q + 1) * 128], identf)
                for q in range(4)], [rx, Rc], [br])
            CP("act" if half == 0 else "dve", h[:, 4 * half:4 * half + 4, blk * 128:(blk + 1) * 128],
               bk[:, :].rearrange("p (a b) -> p a b", a=4), [br], [Rh[i][c] for c in range(4 * half, 4 * half + 4)])
    A.free(xinB)
    if stop("p0"):
        return nc

    def rmsnorm(i, gname, gidx):
        c0, n = TT[i]
        sq = A.alloc("sq", [8, n], BF16)
        rs = A.alloc("rstd", [n], F32)
        ACT(sq.ap, h[:, :, c0:c0 + n], AF.Square, Rh[i], [sq.res()])
        bk, br = bank()
        MM(acc(bk[:, 0:n], [(onesN, sq.ap[:, c, :]) for c in range(8)]), [sq.res(), Rcb], [br])
        RSTD(rs.ap, bk[:, 0:n], [br], [rs.res()])
        for c in range(8):
            STT("dve", xn[:, c, c0:c0 + n], h[:, c, c0:c0 + n], vcol(gname, gidx * 8 + c), rs.ap,
                ALU.mult, ALU.mult, [Rh[i][c], rs.res(), Rvt], [Rxn[i]])
        A.free(sq)
        A.free(rs)

    uhB = A.alloc("uh", [8, UW], BF16)
    uh = uhB.ap
    Ru = [uhB.res(i) for i in range(5)]
    Rhist = uhB.res("hist")
    ustB = A.alloc("ust", [8, 158], F32)
    Rust = ustB.res()
    MEMSET("pool", uh[:, :, 0:30], 0.0, [Ru[0]])

    P.dma("sp", conv_s[:, 0:22, :], st_conv[:, 8:30, :])
    scB = A.alloc("scin", [D], F32)
    for blk in range(4):
        rsb = scB.res()
        P.dma("sp", scB.ap[0:120, :], st_conv[4 * blk:4 * blk + 4, :, :].rearrange("b r d -> (b r) d"), writes=[rsb])
        for half in range(2):
            bk, br = bank()
            TR([(bk[:, q * 120:(q + 1) * 120], scB.ap[0:120, (4 * half + q) * 128:(4 * half + q + 1) * 128], identf[0:120, 0:120])
                for q in range(4)], [rsb, Rc], [br])
            for q in range(4):
                c = 4 * half + q
                dst = uh[:, c, SB0 + 38 * 4 * blk:SB0 + 38 * 4 * (blk + 1)].rearrange("p (b w) -> p b w", w=38)[:, :, 0:30]
                CP("dve" if q % 2 == 0 else "act", dst, bk[:, q * 120:(q + 1) * 120].rearrange("p (b r) -> p b r", r=30),
                   [br], [Rhist])
    A.free(scB)

    for i in range(5):
        rmsnorm(i, "g_mix", 0)

    wp1 = wview(0, 8, 2048)
    sgB = A.alloc("sg", [2, 512], F32)
    for i in range(5):
        c0, n = TT[i]
        for m in range(8):
            ba, bra = bank()
            bg, brg = bank()
            MM(acc(ba[:, 0:n], [(wp1[:, kc, m * 128:(m + 1) * 128], xn[:, kc, c0:c0 + n]) for kc in range(8)]),
               [RW[0], RW[1], Rxn[i]], [bra])
            MM(acc(bg[:, 0:n], [(wp1[:, kc, (8 + m) * 128:(9 + m) * 128], xn[:, kc, c0:c0 + n]) for kc in range(8)]),
               [RW[0], RW[1], Rxn[i]], [brg])
            rsg = sgB.res(m % 2)
            sg = sgB.ap[:, m % 2, 0:n]
            ACT(sg, bg[:, 0:n], AF.Sigmoid, [brg, Rvt], [rsg], bias=vcol("b_pw1", 8 + m))
            if i < 4:
                dst = uh[:, m, 30 + c0:30 + c0 + n]
                STT("dve", dst, ba[:, 0:n], vcol("b_pw1", m), sg, ALU.add, ALU.mult, [bra, rsg, Rvt], [Ru[i]])
                if i == 3:
                    STT("dve", ustB.ap[:, m, 0:30], ba[:, 482:512], vcol("b_pw1", m), sg[:, 482:512], ALU.add, ALU.mult,
                        [bra, rsg, Rvt], [Rust])
            else:
                dst = uh[:, m, SB0:UW].rearrange("p (b w) -> p b w", w=38)[:, :, 30:38]
                STT("dve", dst, ba[:, 0:128].rearrange("p (b t) -> p b t", t=8), vcol("b_pw1", m),
                    sg.rearrange("p (b t) -> p b t", t=8), ALU.add, ALU.mult, [bra, rsg, Rvt], [Ru[4]])
                STT("dve", ustB.ap[:, m, 30:158], ba[:, 0:128], vcol("b_pw1", m), sg, ALU.add, ALU.mult,
                    [bra, rsg, Rvt], [Rust])
    A.free(sgB)
    if stop("p1a"):
        return nc
    load_pw2()

    cpoB = A.alloc("cpo", [D], F32)
    for half in range(2):
        bk, br = bank()
        TR([(bk[0:30, q * 128:(q + 1) * 128], ustB.ap[:, 4 * half + q, 0:30], identf) for q in range(4)], [Rust, Rc], [br])
        CP("act", cpoB.ap[0:30, half * 512:(half + 1) * 512], bk[0:30, :], [br], [cpoB.res()])
    P.dma("sp", conv_p, cpoB.ap[0:30, :], reads=[cpoB.res()])
    csoB = A.alloc("cso", [D], F32)
    for half in range(2):
        bk, br = bank()
        TR([(bk[:, q * 128:(q + 1) * 128], ustB.ap[:, 4 * half + q, 30:158], identf) for q in range(4)], [Rust, Rc], [br])
        CP("act", csoB.ap[:, half * 512:(half + 1) * 512], bk[:, :], [br], [csoB.res()])
    for b in range(16):
        P.dma("sp", conv_s[b, 22:30, :], csoB.ap[8 * b:8 * b + 8, :], reads=[csoB.res()])
    A.free(cpoB)
    A.free(csoB)
    A.free(ustB)
    if stop("p1"):
        return nc

    xn_free()
    wp2 = wview(0, 8, 1024)
    ffw = {}
    c2B = A.alloc("c2t", [8, 512], BF16)
    Rc2 = c2B.res()
    DgB = A.alloc("Dg", [2, 31, 128], BF16)
    ccB = A.alloc("cc", [8, 512], F32)
    cbfB = A.alloc("cbf", [2, 512], BF16)
    sqcB = A.alloc("sqc", [2, 512], BF16)
    lnB = A.alloc("lnst", [3, 512], F32)
    wdw_pst = vt.ap[0][0]
    for i in range(5):
        c0, n = TT[i]
        bm, brm = bank(hold=True)
        bq, brq = bank(hold=True)
        for c in range(8):
            rdg = DgB.res(c % 2)
            NDV = 25
            for (eng_, k0, k1) in (("dve", 0, NDV), ("pool", NDV, 31)):
                in0 = bass.AP(identb.tensor, identb.offset, [[identb.ap[0][0], 128], [0, k1 - k0], [1, 128]])
                wsrc = vcol("w_dw", k0 * 8 + c)
                in1 = bass.AP(wsrc.tensor, wsrc.offset, [[wdw_pst, 128], [8, k1 - k0], [0, 128]])
                TTo(eng_, DgB.ap[:, c % 2, k0:k1, :], in0, in1, ALU.mult, [Rcb, Rvt], [DgB.res((c % 2, eng_))])
            bk, br = bank()
            if i < 4:
                rhs = [uh[:, c, c0 + k:c0 + k + n] for k in range(31)]
                rds = [Ru[i]] + ([Ru[i - 1]] if i > 0 else [])
            else:
                sv = uh[:, c, SB0:UW].rearrange("p (b w) -> p b w", w=38)
                rhs = [sv[:, :, k:k + 8] for k in range(31)]
                rds = [Ru[4], Rhist]
            outv = bk[:, 0:n] if i < 4 else bk[:, 0:128].rearrange("p (b t) -> p b t", t=8)
            MM(acc(outv, [(DgB.ap[:, c % 2, k, :], rhs[k]) for k in range(31)]),
               [DgB.res((c % 2, "dve")), DgB.res((c % 2, "pool"))] + rds, [br])
            rcc = ccB.res(c)
            ACT(ccB.ap[:, c, 0:n], bk[:, 0:n], AF.Identity, [br, Rvt], [rcc], bias=vcol("b_dw", c))
            ACT(cbfB.ap[:, c % 2, 0:n], bk[:, 0:n], AF.Identity, [br, Rvt], [cbfB.res(c % 2)], bias=vcol("b_dw", c))
            ACT(sqcB.ap[:, c % 2, 0:n], ccB.ap[:, c, 0:n], AF.Square, [rcc], [sqcB.res(c % 2)])
            MM([dict(out=bm[:, 0:n], lhsT=onesN, rhs=cbfB.ap[:, c % 2, 0:n], start=(c == 0), stop=(c == 7))],
               [cbfB.res(c % 2), Rcb], [brm])
            MM([dict(out=bq[:, 0:n], lhsT=onesN, rhs=sqcB.ap[:, c % 2, 0:n], start=(c == 0), stop=(c == 7))],
               [sqcB.res(c % 2), Rcb], [brq])
        mean = lnB.ap[:, 0, 0:n]
        m2 = lnB.ap[:, 1, 0:n]
        rstd = lnB.ap[:, 2, 0:n]
        rln = lnB.res()
        CP("act", mean, bm[:, 0:n], [brm], [rln])
        TTo("dve", m2, mean, mean, ALU.mult, [rln], [rln])
        TTo("dve", m2, bq[:, 0:n], m2, ALU.subtract, [brq, rln], [rln])
        RSTD(rstd, m2, [rln], [rln])
        release(bm)
        release(bq)
        for c in range(8):
            rcc = ccB.res(c)
            TTo("dve", ccB.ap[:, c, 0:n], ccB.ap[:, c, 0:n], mean, ALU.subtract, [rcc, rln], [rcc])
            TTo("dve", ccB.ap[:, c, 0:n], ccB.ap[:, c, 0:n], rstd, ALU.mult, [rcc, rln], [rcc])
            ACT(c2B.ap[:, c, 0:n], ccB.ap[:, c, 0:n], AF.Silu, [rcc, Rvt], [Rc2], bias=vcol("ln_b", c),
                scale=vcol("ln_g", c))
        if i == 3:
            ffw[(0, 0)] = load_ffn(0, 0, 1)
        for m in range(8):
            bk, br = bank()
            MM(acc(bk[:, 0:n], [(wp2[:, kc, m * 128:(m + 1) * 128], c2B.ap[:, kc, 0:n]) for kc in range(8)]),
               [RW[0], Rc2], [br])
            STT("dve", h[:, m, c0:c0 + n], bk[:, 0:n], vcol("b_pw2", m), h[:, m, c0:c0 + n], ALU.add, ALU.add,
                [br, Rvt], [Rh[i][m]])
    for b_ in (DgB, ccB, cbfB, sqcB, lnB, c2B):
        A.free(b_)
    A.free(uhB)
    xn, Rxn = xn_alloc()
    if stop("p3"):
        return nc

    def proj_residual(wv, i, rslots, bias_name=None):
        c0, n = TT[i]
        for m in range(8):
            bk, br = bank()
            MM(acc(bk[:, 0:n], [(wv[:, kc, m * 128:(m + 1) * 128], xn[:, kc, c0:c0 + n]) for kc in range(8)]),
               rslots + [Rxn[i]], [br])
            if bias_name is not None:
                STT("dve", h[:, m, c0:c0 + n], bk[:, 0:n], vcol(bias_name, m), h[:, m, c0:c0 + n], ALU.add, ALU.add,
                    [br, Rvt], [Rh[i][m]])
            else:
                TTo("dve", h[:, m, c0:c0 + n], bk[:, 0:n], h[:, m, c0:c0 + n], ALU.add, [br], [Rh[i][m]])


    def ffn(l, slot0, after_first_group=None):
        for i in range(5):
            rmsnorm(i, "g_ffn", l)
        fstB = A.alloc("fst", [NF, 34], F32)
        Rfst = fstB.res()
        upB = A.alloc("uprev", [3, NF, 2], F32)
        hsB = A.alloc("hs", [NF, 32], F32)
        Rhs = hsB.res()
        sfB = A.alloc("sfin", [1408], F32)
        for q4 in range(4):
            rsf = sfB.res()
            P.dma("sp", sfB.ap[0:32, :], st_ffn[l, :, q4 * 1408:(q4 + 1) * 1408], writes=[rsf])
            bk, br = bank()
            TR([(bk[:, q * 32:(q + 1) * 32], sfB.ap[0:32, q * 128:(q + 1) * 128], identf[0:32, 0:32]) for q in range(11)],
               [rsf, Rc], [br])
            CP("dve", hsB.ap[:, q4 * 11:(q4 + 1) * 11, :], bk[:, 0:352].rearrange("p (a b) -> p a b", a=11), [br], [Rhs])
        A.free(sfB)
        cgB = A.alloc("cg", [2, 2, 512], F32)
        sgtB = A.alloc("sgt", [2, 512], F32)
        actB = A.alloc("act", [2, 4, 512], BF16)
        units = [(gi, i) for gi in range(len(GROUPS)) for i in range(5)]

        def unit_ctx(k):
            gi, i = units[k]
            j0, nj = GROUPS[gi]
            slot = (slot0 + gi) % 2
            wu, wd = ffw[(l, gi)]
            c0, n = TT[i]
            return gi, i, j0, nj, slot, wu, wd, c0, n, actB.res(k % 2)

        def emit_up(k):
            gi, i, j0, nj, slot, wu, wd, c0, n, ract = unit_ctx(k)
            for jj in range(nj):
                j = j0 + jj
                par = (jj + i) % 2
                cvals = []
                for gv in range(2):
                    f = j + 22 * gv
                    bk, br = bank()
                    MM(acc(bk[:, 0:n], [(wu[:, kc, (gv * nj + jj) * 128:(gv * nj + jj + 1) * 128], xn[:, kc, c0:c0 + n])
                                        for kc in range(8)]), [RW[slot], Rxn[i]], [br])
                    rcg = cgB.res((par, gv))
                    cg = cgB.ap[:, par, gv, 0:n]
                    w0 = vcol("f_w_dw", (l * 3 + 0) * NF + f)
                    w1 = vcol("f_w_dw", (l * 3 + 1) * NF + f)
                    w2 = vcol("f_w_dw", (l * 3 + 2) * NF + f)
                    bb = vcol("f_b_dw", l * NF + f)
                    ACT(cg, bk[:, 0:n], AF.Identity, [br, Rvt], [rcg], bias=bb, scale=w2)
                    if i < 4:
                        if i < 3:
                            rup = upB.res((i, f))
                            evc = CP("act", upB.ap[:, i, f, :], bk[:, n - 2:n], [br], [rup])
                        else:
                            evc = CP("act", fstB.ap[:, f, 0:2], bk[:, n - 2:n], [br], [Rfst])
                        STT("dve", cg[:, 1:n], bk[:, 0:n - 1], w1, cg[:, 1:n], ALU.mult, ALU.add, [br, Rvt], [rcg], extra=[evc])
                        STT("dve", cg[:, 2:n], bk[:, 0:n - 2], w0, cg[:, 2:n], ALU.mult, ALU.add, [br, Rvt], [rcg])
                        if i > 0:
                            rprev = upB.res((i - 1, f))
                            ext = upB.ap[:, i - 1, f, :]
                            STT("dve", cg[:, 0:2], ext, w0, cg[:, 0:2], ALU.mult, ALU.add, [rprev, Rvt], [rcg])
                            STT("dve", cg[:, 0:1], ext[:, 1:2], w1, cg[:, 0:1], ALU.mult, ALU.add, [rprev, Rvt], [rcg])
                    else:
                        cg3 = cg.rearrange("p (b t) -> p b t", t=8)
                        bk3 = bk[:, 0:128].rearrange("p (b t) -> p b t", t=8)
                        hs3 = hsB.ap[:, f, :].rearrange("p (b r) -> p b r", r=2)
                        evc = CP("act", fstB.ap[:, f, 2:34].rearrange("p (b r) -> p b r", r=2), bk3[:, :, 6:8], [br], [Rfst])
                        STT("dve", cg3[:, :, 1:8], bk3[:, :, 0:7], w1, cg3[:, :, 1:8], ALU.mult, ALU.add, [br, Rvt], [rcg], extra=[evc])
                        STT("dve", cg3[:, :, 2:8], bk3[:, :, 0:6], w0, cg3[:, :, 2:8], ALU.mult, ALU.add, [br, Rvt], [rcg])
                        STT("dve", cg3[:, :, 0:2], hs3, w0, cg3[:, :, 0:2], ALU.mult, ALU.add, [Rhs, Rvt], [rcg])
                        STT("dve", cg3[:, :, 0:1], hs3[:, :, 1:2], w1, cg3[:, :, 0:1], ALU.mult, ALU.add, [Rhs, Rvt], [rcg])
                    cvals.append((cg, rcg))
                rsg = sgtB.res(par)
                sgt = sgtB.ap[:, par, 0:n]
                ACT(sgt, cvals[0][0], AF.Silu, [cvals[0][1]], [rsg])
                TTo("pool", actB.ap[:, k % 2, jj, 0:n], sgt, cvals[1][0], ALU.mult, [rsg, cvals[1][1]], [ract])

        def emit_down(k):
            gi, i, j0, nj, slot, wu, wd, c0, n, ract = unit_ctx(k)
            for m in range(8):
                bk, br = bank()
                MM(acc(bk[:, 0:n], [(wd[:, jj, m * 128:(m + 1) * 128], actB.ap[:, k % 2, jj, 0:n]) for jj in range(nj)]),
                   [RW[slot], ract], [br])
                TTo("dve", h[:, m, c0:c0 + n], bk[:, 0:n], h[:, m, c0:c0 + n], ALU.add, [br], [Rh[i][m]])

        if len(GROUPS) > 1:
            ffw[(l, 1)] = load_ffn(l, 1, (slot0 + 1) % 2)
        emit_up(0)
        for k in range(len(units)):
            if k + 1 < len(units):
                emit_up(k + 1)
            emit_down(k)
            gi, i = units[k]
            if i == 4 and gi + 2 < len(GROUPS):
                ffw[(l, gi + 2)] = load_ffn(l, gi + 2, (slot0 + gi) % 2)
        for b_ in (cgB, sgtB, actB, upB, hsB):
            A.free(b_)
        ostB = A.alloc("ost", [1408], F32)
        for q4 in range(4):
            rost = ostB.res()
            for t3 in range(3):
                nq = 4 if t3 < 2 else 3
                bk, br = bank()
                TR([(bk[0:34, q * 128:(q + 1) * 128], fstB.ap[:, q4 * 11 + t3 * 4 + q, :], identf) for q in range(nq)],
                   [Rfst, Rc], [br])
                CP("act", ostB.ap[0:34, t3 * 512:t3 * 512 + nq * 128], bk[0:34, 0:nq * 128], [br], [rost])
            P.dma("sp", ffn_p[l, :, q4 * 1408:(q4 + 1) * 1408], ostB.ap[0:2, :], reads=[rost])
            P.dma("sp", ffn_s[l, :, q4 * 1408:(q4 + 1) * 1408], ostB.ap[2:34, :], reads=[rost])
        A.free(ostB)
        A.free(fstB)

    def load_ple(l, slot):
        wg = wview(slot * SLOT, 8, 1024)
        wp = wview((1 - slot) * SLOT + 4096, 2, 1024)
        load_rows(wg, w_gate[l], 8, [RW[slot]])
        load_rows(wp, w_proj[l], 2, [RW[1 - slot]])
        return wg, wp

    def ple(l, slot, wg, wp, prefetch=None):
        pTB = A.alloc("pT", [2, T], BF16)
        pinB = A.alloc("pin", [2, 256], F32)
        RpT = [pTB.res(i) for i in range(5)]
        for blk in range(NBLK):
            i = min(blk // 4, 4)
            rp = pinB.res(blk % 2)
            P.dma("sp", pinB.ap[:, blk % 2, :], p_all[l, blk * 128:(blk + 1) * 128, :], writes=[rp])
            bk, br = bank()
            TR([(bk[:, q * 128:(q + 1) * 128], pinB.ap[:, blk % 2, q * 128:(q + 1) * 128], identf) for q in range(2)],
               [rp, Rc], [br])
            CP("act", pTB.ap[:, :, blk * 128:(blk + 1) * 128], bk[:, 0:256].rearrange("p (a b) -> p a b", a=2), [br], [RpT[i]])
        A.free(pinB)
        for i in range(5):
            rmsnorm(i, "g_ple", l)
        if prefetch is not None:
            prefetch()
        gtB = A.alloc("gt", [2, 512], F32)
        tmB = A.alloc("tm", [2, 512], F32)
        for i in range(5):
            c0, n = TT[i]
            for m in range(8):
                bg, brg = bank()
                bp, brp = bank()
                MM(acc(bg[:, 0:n], [(wg[:, kc, m * 128:(m + 1) * 128], xn[:, kc, c0:c0 + n]) for kc in range(8)]),
                   [RW[slot], Rxn[i]], [brg])
                MM(acc(bp[:, 0:n], [(wp[:, kc, m * 128:(m + 1) * 128], pTB.ap[:, kc, c0:c0 + n]) for kc in range(2)]),
                   [RW[1 - slot], RpT[i]], [brp])
                rgt = gtB.res(m % 2)
                rtm = tmB.res(m % 2)
                ACT(gtB.ap[:, m % 2, 0:n], bg[:, 0:n], AF.Sigmoid, [brg], [rgt])
                TTo("dve", tmB.ap[:, m % 2, 0:n], bp[:, 0:n], gtB.ap[:, m % 2, 0:n], ALU.mult, [brp, rgt], [rtm])
                TTo("pool", h[:, m, c0:c0 + n], h[:, m, c0:c0 + n], tmB.ap[:, m % 2, 0:n], ALU.add, [rtm], [Rh[i][m]])
        for b_ in (gtB, tmB, pTB):
            A.free(b_)

    ple_w = {}

    def pf_ple0():
        ple_w[0] = load_ple(0, 1)

    ffn(0, 1, after_first_group=None)
    if stop("ffn0"):
        return nc
    pf_ple0()

    kvw = {}

    def load_kv():
        base = 0
        wk = wview(base, 8, 256)
        wv = wview(base + 2048, 8, 256)
        P.dma_many("pool", [(wk[:, kc, :], w_k[kc * 128:(kc + 1) * 128, :]) for kc in range(8)] +
                   [(wv[:, kc, :], w_v[kc * 128:(kc + 1) * 128, :]) for kc in range(8)], writes=[RW[0]])
        kvw["k"], kvw["v"] = wk, wv

    ple(0, 1, ple_w[0][0], ple_w[0][1], prefetch=load_kv)
    if stop("ple0"):
        return nc

    KTB = A.alloc("KT", [2, T], BF16)
    KT = KTB.ap
    RKT = [KTB.res(i) for i in range(5)]
    VtB = A.alloc("Vtok", [NBLK, 256], BF16)
    Vt = VtB.ap
    RVt = [VtB.res(b) for b in range(NBLK)]
    for i in range(5):
        rmsnorm(i, "kv_g", 0)

    wqo = {}

    def load_wq():
        wqo["q"] = wview(SLOT, 8, 1024)
        stg = A.alloc("wqstg", [8, 1024], BF16)
        load_rows(stg.ap, w_q, 8, [stg.res()])
        for gp in range(2):
            for hf in range(2):
                dst = wqo["q"][:, :, gp * 512:(gp + 1) * 512].rearrange("p k (j hf d) -> p k j hf d", j=4, hf=2)[:, :, :, hf, :]
                src = stg.ap[:, :, gp * 512:(gp + 1) * 512].rearrange("p k (hf j d) -> p k hf j d", hf=2, j=4)[:, :, hf, :, :]
                CP("pool", dst, src, [stg.res()], [RW[1]])
        A.free(stg)

    load_wq()
    if stop("kv0"):
        return nc
    knoB = A.alloc("kno", [2, 256], F32)
    Rkno = knoB.res()
    kfB = A.alloc("kf", [2, 512], F32)
    ksqB = A.alloc("ksq", [2, 512], BF16)
    krsB = A.alloc("krs", [2, 512], F32)
    for i in range(5):
        c0, n = TT[i]
        for gp in range(2):
            bk, br = bank()
            MM(acc(bk[:, 0:n], [(kvw["k"][:, kc, gp * 128:(gp + 1) * 128], xn[:, kc, c0:c0 + n]) for kc in range(8)]),
               [RW[0], Rxn[i]], [br])
            rkf, rsq, rrs = kfB.res(gp), ksqB.res(gp), krsB.res(gp)
            kf = kfB.ap[:, gp, 0:n]
            CP("act", kf, bk[:, 0:n], [br], [rkf])
            ACT(ksqB.ap[:, gp, 0:n], bk[:, 0:n], AF.Square, [br], [rsq])
            b2, br2 = bank()
            MM([dict(out=b2[:, 0:n], lhsT=ones64, rhs=ksqB.ap[:, gp, 0:n], start=True, stop=True)], [rsq, Rcb], [br2])
            RSTD(krsB.ap[:, gp, 0:n], b2[:, 0:n], [br2], [rrs])
            STT("dve", KT[:, gp, c0:c0 + n], kf, vcol("gk2"), krsB.ap[:, gp, 0:n], ALU.mult, ALU.mult, [rkf, rrs, Rvt], [RKT[i]])
            if i == 3:
                STT("dve", knoB.ap[:, gp, 0:128], kf[:, 384:512], vcol("gk2"), krsB.ap[:, gp, 384:512], ALU.mult, ALU.mult,
                    [rkf, rrs, Rvt], [Rkno])
            if i == 4:
                STT("dve", knoB.ap[:, gp, 128:256], kf, vcol("gk2"), krsB.ap[:, gp, 0:128], ALU.mult, ALU.mult,
                    [rkf, rrs, Rvt], [Rkno])
    for b_ in (kfB, ksqB, krsB):
        A.free(b_)
    if stop("kv1"):
        return nc
    voB = A.alloc("vo", [2, 256], F32)
    for blk in range(NBLK):
        i = min(blk // 4, 4)
        bk, br = bank()
        MM(acc(bk[:, 0:256], [(xn[:, kc, blk * 128:(blk + 1) * 128], kvw["v"][:, kc, :]) for kc in range(8)]),
           [RW[0], Rxn[i]], [br])
        CP("act", Vt[:, blk, :], bk[:, 0:256], [br], [RVt[blk]])
        if blk >= 15 and os.environ.get("KSUB", "") != "a":
            rvo = voB.res(blk - 15)
            CP("act", voB.ap[:, blk - 15, :], bk[:, 0:256], [br], [rvo])
            if os.environ.get("KSUB", "") == "c":
                pass
            elif blk == 15:
                P.dma("sp", v_p, voB.ap[:, 0, :], reads=[rvo])
            elif os.environ.get("KSUB", "") != "b":
                for b in range(16):
                    P.dma("sp", v_s[b, 120:128, :], voB.ap[8 * b:8 * b + 8, 1, :], reads=[rvo])
    A.free(voB)
    if stop("kv2"):
        return nc
    koB = A.alloc("ko", [2, 256], F32)
    for w in range(2):
        bk, br = bank()
        TR([(bk[:, gp * 128:(gp + 1) * 128], knoB.ap[:, gp, w * 128:(w + 1) * 128], identf) for gp in range(2)], [Rkno, Rc], [br])
        rko = koB.res(w)
        CP("act", koB.ap[:, w, :], bk[:, 0:256], [br], [rko])
        if w == 0:
            P.dma("sp", k_p, koB.ap[:, 0, :], reads=[rko])
        else:
            for b in range(16):
                P.dma("sp", k_s[b, 120:128, :], koB.ap[8 * b:8 * b + 8, 1, :], reads=[rko])
    A.free(koB)
    A.free(knoB)
    P.dma("sp", k_s[:, 0:120, :], cache_k[:, 8:128, :])
    P.dma("sp", v_s[:, 0:120, :], cache_v[:, 8:128, :])
    if stop("kv"):
        return nc

    EBc, EBp = build_eb()
    for i in range(5):
        rmsnorm(i, "g_mix", 1)
    esB = A.alloc("es", [8], F32)
    Res_es = esB.res()
    ACT(esB.ap, vt[:, VOFF["sinks"]:VOFF["sinks"] + 8], AF.Exp, [Rvt], [Res_es])

    EB_ = A.alloc("E", [2, 512], F32)
    PTB = A.alloc("PT", [2, 512], BF16)
    denB = A.alloc("den", [2, 512], F32)
    EBn = A.alloc("EBn", [16, 128], BF16)
    for hh in range(16):
        TTo("pool", EBn.ap[:, hh, :], EBc.ap[:, hh, :], bmask, ALU.mult, [EBc.res(), Rc], [EBn.res()])
    EBs = A.alloc("EBs", [4, 16, 4, 8], BF16)
    for b in range(16):
        CP("pool", EBs.ap[:, :, b, :, :], EBp.ap[:, :, 0:8].rearrange("p (g j) q -> p g j q", g=4), [EBp.res()], [EBs.res()])
    class _View:
        def __init__(self, ap, r):
            self.ap = ap
            self._r = r

        def res(self, key=None):
            return self._r

    KcT = _View(W[:, 0:4096].rearrange("p (b g s) -> p b g s", b=16, g=2), RW[0])
    VcB = _View(W[:, 4096:8192].rearrange("p (b f) -> p b f", b=16), RW[0])
    P.dma("pool", VcB.ap, cache_v.rearrange("b s f -> s b f"), writes=[VcB.res()])
    ckB = A.alloc("ck", [8, 256], F32)
    for hb in range(2):
        rck = ckB.res()
        P.dma("sp", ckB.ap, cache_k[8 * hb:8 * hb + 8].rearrange("b s f -> s b f"), writes=[rck])
        for b2 in range(0, 8, 2):
            bk, br = bank()
            TR([(bk[:, (bb * 2 + gp) * 128:(bb * 2 + gp + 1) * 128], ckB.ap[:, b2 + bb, gp * 128:(gp + 1) * 128], identf)
                for bb in range(2) for gp in range(2)], [rck, Rc], [br])
            CP("act", KcT.ap[:, 8 * hb + b2:8 * hb + b2 + 2, :, :], bk[:, :].rearrange("p (b g s) -> p b g s", b=2, g=2),
               [br], [KcT.res()])
    A.free(ckB)

    def alloc_q(n):
        return (A.alloc("QT", [2, 8, n], BF16), A.alloc("qf", [2, n], F32), A.alloc("qsq", [2, n], BF16),
                A.alloc("qrs", [2, n], F32))

    QTB, qfB, qsqB, qrsB = alloc_q(128)
    est = {"n": 0}

    def qproj(i):
        c0, n = TT[i]
        for m in range(8):
            ha, hb_ = _chunk_heads(m)
            bk, br = bank()
            lw = wqo["q"]
            MM(acc(bk[:, 0:n], [(lw[:, kc, m * 128:(m + 1) * 128], xn[:, kc, c0:c0 + n]) for kc in range(8)]),
               [RW[1], Rxn[i]], [br])
            par = m % 2
            rqf, rsq, rrs = qfB.res(par), qsqB.res(par), qrsB.res(par)
            CP("act", qfB.ap[:, par, 0:n], bk[:, 0:n], [br], [rqf])
            ACT(qsqB.ap[:, par, 0:n], bk[:, 0:n], AF.Square, [br], [rsq])
            b2, br2 = bank()
            MM([dict(out=b2[:, 0:n], lhsT=ones64, rhs=qsqB.ap[:, par, 0:n], start=True, stop=True)], [rsq, Rcb], [br2])
            RSTD(qrsB.ap[:, par, 0:n], b2[:, 0:n], [br2], [rrs])
            STT("dve", QTB.ap[:, i % 2, m, 0:n], qfB.ap[:, par, 0:n], vcol("gq2"), qrsB.ap[:, par, 0:n], ALU.mult, ALU.mult,
                [rqf, rrs, Rvt], [QTB.res(i % 2)])

    def attn_units(i, qoff, nb, sample):
        qp = i % 2
        RQ = QTB.res(qp)
        ocol = nb * 128
        if sample:
            parts = [(16, EBn)]
        else:
            parts = ([(nb - 1, EBp)] if nb > 0 else []) + [(nb, EBc)]
        units = []
        for gp in range(2):
            ctx = {}

            def banks_(ctx=ctx):
                if "bo" not in ctx:
                    ctx["bo"], ctx["bro"] = bank(hold=True)
                    ctx["bs"], ctx["brs"] = bank(hold=True)
                return ctx["bo"], ctx["bro"], ctx["bs"], ctx["brs"]

            for hf in range(2):
                g = 2 * gp + hf
                rows = slice(hf * 64, (hf + 1) * 64)
                for pi, (kblk, EBt) in enumerate(parts):
                    st = {}

                    def A(st=st, gp=gp, hf=hf, g=g, rows=rows, kblk=kblk, EBt=EBt):
                        ki = min(kblk // 4, 4)
                        bk, br = bank()
                        MM([dict(out=bk[:, j * 128:(j + 1) * 128], lhsT=KT[rows, gp, kblk * 128:(kblk + 1) * 128],
                                 rhs=QTB.ap[rows, qp, 4 * gp + j, qoff:qoff + 128], start=True, stop=True, tp=(hf * 64, 0))
                            for j in range(4)], [RKT[ki], RQ], [br])
                        par = est["n"] % 2
                        est["n"] += 1
                        rE, rPT = EB_.res(par), PTB.res(par)
                        ACT(EB_.ap[:, par, :], bk[:, :], AF.Exp, [br], [rE], scale=SCALE)
                        TTo("dve", PTB.ap[:, par, :], EB_.ap[:, par, :],
                            EBt.ap[:, 4 * g:4 * g + 4, :].rearrange("p a b -> p (a b)"), ALU.mult, [rE, EBt.res()], [rPT])
                        st["par"], st["rPT"] = par, rPT

                    def B(st=st, gp=gp, hf=hf, g=g, rows=rows, kblk=kblk, pi=pi, banks_=banks_):
                        bo, bro, bs, brs = banks_()
                        par, rPT = st["par"], st["rPT"]
                        lastp = (pi == len(parts) - 1) and not sample
                        MM([dict(out=bo[rows, j * 128:(j + 1) * 128], lhsT=Vt[:, kblk, g * 64:(g + 1) * 64],
                                 rhs=PTB.ap[:, par, j * 128:(j + 1) * 128], start=(pi == 0 and j == 0),
                                 stop=(lastp and j == 3), tp=(0, hf * 64), sgc=True) for j in range(4)], [RVt[kblk], rPT], [bro])
                        MM([dict(out=bs[rows, j * 128:(j + 1) * 128], lhsT=ones1[:, 0:64],
                                 rhs=PTB.ap[:, par, j * 128:(j + 1) * 128], start=(pi == 0 and j == 0),
                                 stop=(lastp and j == 3), tp=(0, hf * 64), sgc=True) for j in range(4)], [Rcb, rPT], [brs])

                    units.append([A, B])
                if sample:
                    st = {}

                    def A2(st=st, gp=gp, hf=hf, g=g, rows=rows):
                        bk, br = bank()
                        MM([dict(out=bk[:, (b * 4 + j) * 8:(b * 4 + j) * 8 + 8], lhsT=KcT.ap[rows, b, gp, :],
                                 rhs=QTB.ap[rows, qp, 4 * gp + j, 8 * b:8 * b + 8], start=True, stop=True, tp=(hf * 64, 0))
                            for b in range(16) for j in range(4)], [KcT.res(), RQ], [br])
                        par = est["n"] % 2
                        est["n"] += 1
                        rE, rPT = EB_.res(par), PTB.res(par)
                        ACT(EB_.ap[:, par, :], bk[:, :], AF.Exp, [br], [rE], scale=SCALE)
                        TTo("dve", PTB.ap[:, par, :], EB_.ap[:, par, :],
                            EBs.ap[:, g, :, :, :].rearrange("p a b c -> p (a b c)"), ALU.mult, [rE, EBs.res()], [rPT])
                        st["par"], st["rPT"] = par, rPT

                    def B2(st=st, gp=gp, hf=hf, g=g, rows=rows, banks_=banks_):
                        bo, bro, bs, brs = banks_()
                        par, rPT = st["par"], st["rPT"]
                        MM([dict(out=bo[rows, j * 128 + 8 * b:j * 128 + 8 * b + 8], lhsT=VcB.ap[:, b, g * 64:(g + 1) * 64],
                                 rhs=PTB.ap[:, par, (b * 4 + j) * 8:(b * 4 + j) * 8 + 8], start=False,
                                 stop=(b == 15 and j == 3), tp=(0, hf * 64), sgc=True) for b in range(16) for j in range(4)],
                           [VcB.res(), rPT], [bro])
                        MM([dict(out=bs[rows, j * 128 + 8 * b:j * 128 + 8 * b + 8], lhsT=ones1[:, 0:64],
                                 rhs=PTB.ap[:, par, (b * 4 + j) * 8:(b * 4 + j) * 8 + 8], start=False,
                                 stop=(b == 15 and j == 3), tp=(0, hf * 64), sgc=True) for b in range(16) for j in range(4)],
                           [Rcb, rPT], [brs])

                    units.append([A2, B2])

            def N(gp=gp, banks_=banks_):
                bo, bro, bs, brs = banks_()
                rden = denB.res(gp)
                for j in range(4):
                    TS("dve", denB.ap[:, gp, j * 128:(j + 1) * 128], bs[:, j * 128:(j + 1) * 128],
                       esB.ap[:, 4 * gp + j:4 * gp + j + 1], None, ALU.add, None, [brs, Res_es], [rden])
                ACT(denB.ap[:, gp, :], denB.ap[:, gp, :], AF.Ln, [rden], [rden])
                ACT(denB.ap[:, gp, :], denB.ap[:, gp, :], AF.Exp, [rden], [rden], scale=-1.0)
                TTo("dve", xn[:, 4 * gp:4 * gp + 4, ocol:ocol + 128], bo[:, :].rearrange("p (a b) -> p a b", a=4),
                    denB.ap[:, gp, :].rearrange("p (a b) -> p a b", a=4), ALU.mult, [bro, rden], [Rxn[i]])
                release(bo)
                release(bs)

            units[-1].append(N)
        return units

    def run_units(units):
        if not units:
            return
        units[0][0]()
        for u in range(len(units)):
            if u + 1 < len(units):
                units[u + 1][0]()
            for f in units[u][1:]:
                f()

    qproj(4)
    run_units(attn_units(4, 0, 16, True))
    for b_ in (EBn, EBs, QTB, qfB, qsqB, qrsB):
        A.free(b_)
    QTB, qfB, qsqB, qrsB = alloc_q(512)

    def load_wo():
        wv_ = wview(0, 8, 1024)
        pairs = []
        for c in range(8):
            ha, hb_ = _chunk_heads(c)
            pairs.append((wv_[0:64, c, :], w_o[ha * 64:ha * 64 + 64, :]))
            pairs.append((wv_[64:128, c, :], w_o[hb_ * 64:hb_ * 64 + 64, :]))
        P.dma_many("pool", pairs, writes=[RW[0]])
        wqo["o"] = wv_

    load_wo()
    allu = []
    qproj(0)
    for i in range(4):
        for q4 in range(4):
            allu.extend(attn_units(i, q4 * 128, 4 * i + q4, False))
            if q4 == 1 and i + 1 < 4:
                allu.append([(lambda ii: (lambda: qproj(ii)))(i + 1), (lambda: None)])
    run_units(allu)
    for b_ in (QTB, qfB, qsqB, qrsB, EB_, PTB, denB, esB, KTB, VtB, EBc, EBp):
        A.free(b_)
    ffw[(1, 0)] = load_ffn(1, 0, 1)
    for i in [4, 0, 1, 2, 3]:
        proj_residual(wqo["o"], i, [RW[0]], None)
    if stop("attn"):
        return nc

    ffn(1, 1)
    ple_w[1] = load_ple(1, 1)
    ple(1, 1, ple_w[1][0], ple_w[1][1])

    youtB = A.alloc("yout", [2, D], F32)
    for blk in range(NBLK):
        i = min(blk // 4, 4)
        ry = youtB.res(blk % 2)
        for half in range(2):
            bk, br = bank()
            TR([(bk[:, q * 128:(q + 1) * 128], h[:, 4 * half + q, blk * 128:(blk + 1) * 128], identf) for q in range(4)],
               [Rh[i][c] for c in range(4 * half, 4 * half + 4)] + [Rc], [br])
            CP("act" if half == 0 else "dve", youtB.ap[:, blk % 2, half * 512:(half + 1) * 512], bk[:, :], [br], [ry])
        P.dma("sp", y[blk * 128:(blk + 1) * 128, :], youtB.ap[:, blk % 2, :], reads=[ry])

    P.finish()
    return nc


def _t5_bucket_np(d):
    d = np.asarray(d)
    df = np.maximum(d, 1).astype(np.float32)
    large = 16 + (np.log(df / np.float32(16)) / np.float32(math.log(128 / 16)) * np.float32(16)).astype(np.int32)
    large = np.minimum(large, 31)
    return np.where(d < 16, d, large)


def _host_consts():
    cst = np.zeros((128, 256), np.float32)
    cst[:, 0:128] = np.eye(128, dtype=np.float32)
    idx = np.arange(128)
    cst[:, 128:256] = (idx[:, None] // 8 == idx[None, :] // 8).astype(np.float32)
    ohu = np.zeros((33, LU), np.float32)
    for j in range(LU):
        d = j - 127
        if 0 <= d < 128:
            ohu[int(_t5_bucket_np(d)), j] = 1.0
        else:
            ohu[32, j] = 1.0
    return cst, ohu


_NC_CACHE = {}


def kernel(x_prompt, x_sample, state_conv, state_ffn, cache_k, cache_v, p_prompt, p_sample,
           g_mix, cm_w_pw1, cm_b_pw1, cm_w_dw, cm_b_dw, cm_ln_g, cm_ln_b, cm_w_pw2, cm_b_pw2,
           at_w_q, at_g_q, at_sinks, at_w_o, kv_g, kv_w_k, kv_w_v, kv_g_k, rel_bias,
           g_ffn, ffn_w_up, ffn_w_dw, ffn_b_dw, ffn_w_down, g_ple, ple_w_gate, ple_w_proj):
    f = lambda a: np.ascontiguousarray(np.asarray(a, dtype=np.float32))
    sinks = f(at_sinks).reshape(16)
    sink_rows = np.stack([np.concatenate([np.full(64, sinks[_chunk_heads(c)[0]], np.float32),
                                          np.full(64, sinks[_chunk_heads(c)[1]], np.float32)]) for c in range(8)])
    parts = {
        "g_mix": f(g_mix).ravel(), "b_pw1": f(cm_b_pw1).ravel(), "w_dw": f(cm_w_dw).ravel(), "b_dw": f(cm_b_dw).ravel(),
        "ln_g": f(cm_ln_g).ravel(), "ln_b": f(cm_ln_b).ravel(), "b_pw2": f(cm_b_pw2).ravel(), "kv_g": f(kv_g).ravel(),
        "g_ffn": f(g_ffn).ravel(), "f_w_dw": f(ffn_w_dw).ravel(), "f_b_dw": f(ffn_b_dw).ravel(), "g_ple": f(g_ple).ravel(),
        "gq2": np.tile(f(at_g_q).ravel(), 2), "gk2": np.tile(f(kv_g_k).ravel(), 2), "sinks": sink_rows.ravel(),
    }
    vecs = np.zeros((VROWS, 128), np.float32)
    for name, off in VOFF.items():
        v = parts[name].reshape(-1, 128)
        vecs[off:off + v.shape[0]] = v
    cst, ohu = _host_consts()
    shared = {
        "vecs": vecs, "cst": cst, "ohu": ohu, "rel_bias": f(rel_bias),
        "w_pw1": f(cm_w_pw1)[0], "w_pw2": f(cm_w_pw2)[0], "w_q": f(at_w_q)[0], "w_o": f(at_w_o)[0],
        "w_k": f(kv_w_k), "w_v": f(kv_w_v), "w_up": f(ffn_w_up), "w_down": f(ffn_w_down),
        "w_gate": f(ple_w_gate), "w_proj": f(ple_w_proj),
    }
    xp, xs = f(x_prompt), f(x_sample)
    pp, ps = f(p_prompt), f(p_sample)
    sc, sf = f(state_conv), f(state_ffn)
    ck, cv = f(cache_k), f(cache_v)
    in_maps = []
    for c in range(NCORES):
        sl = slice(16 * c, 16 * c + 16)
        m = dict(shared)
        m["x_all"] = np.concatenate([xp[c], xs[sl].reshape(128, D)], axis=0)
        m["p_all"] = np.concatenate([pp[:, c], ps[:, sl].reshape(2, 128, 256)], axis=1)
        m["st_conv"] = np.ascontiguousarray(sc[0, sl])
        m["st_ffn"] = np.ascontiguousarray(sf[:, sl].reshape(2, 32, 2 * DFF))
        m["cache_k"] = np.ascontiguousarray(ck[sl].reshape(16, 128, 256))
        m["cache_v"] = np.ascontiguousarray(cv[sl].reshape(16, 128, 256))
        in_maps.append(m)
    if "nc" not in _NC_CACHE:
        _NC_CACHE["nc"] = build_program()
    nc = _NC_CACHE["nc"]
    res = run_bass_kernel_spmd(nc, in_maps, core_ids=list(range(NCORES)))
    R = res.results
    y_prompt = np.stack([R[c]["y"][0:2048] for c in range(NCORES)])
    y_sample = np.concatenate([R[c]["y"][2048:].reshape(16, 8, D) for c in range(NCORES)], axis=0)
    conv_p = np.stack([R[c]["conv_p"] for c in range(NCORES)])[None]
    conv_s = np.concatenate([R[c]["conv_s"] for c in range(NCORES)], axis=0)[None]
    ffn_p = np.stack([R[c]["ffn_p"] for c in range(NCORES)], axis=1)
    ffn_s = np.concatenate([R[c]["ffn_s"].reshape(2, 16, 2, 2 * DFF) for c in range(NCORES)], axis=1)
    k_p = np.stack([R[c]["k_p"].reshape(128, 4, 64) for c in range(NCORES)])
    k_s = np.concatenate([R[c]["k_s"].reshape(16, 128, 4, 64) for c in range(NCORES)], axis=0)
    v_p = np.stack([R[c]["v_p"].reshape(128, 4, 64) for c in range(NCORES)])
    v_s = np.concatenate([R[c]["v_s"].reshape(16, 128, 4, 64) for c in range(NCORES)], axis=0)
    outs = (y_prompt, y_sample, conv_p, conv_s, ffn_p, ffn_s, k_p, k_s, v_p, v_s)
    return tuple(np.ascontiguousarray(o, dtype=np.float32) for o in outs)
```

```python
import math
import os
import numpy as np
import concourse.bass as bass
import concourse.mybir as mybir
from concourse.bass_utils import run_bass_kernel_spmd

F32 = mybir.dt.float32
BF16 = mybir.dt.bfloat16
U8 = mybir.dt.uint8
AF = mybir.ActivationFunctionType
ALU = mybir.AluOpType

NCORES = 8
D = 1024
DFF = 2816
NF = 44
T = 2176
TT = [(0, 512), (512, 512), (1024, 512), (1536, 512), (2048, 128)]
NBLK = 17
EPS = 1e-6
SCALE = 0.125
LU = 383
GROUPS = [(0, 3), (3, 3), (6, 3), (9, 3), (12, 3), (15, 3), (18, 2), (20, 2)]
SLOT = 9216
UW = 30 + 2048 + 16 * 38
SB0 = 2078


def _vec_layout():
    ents = [("g_mix", 16), ("b_pw1", 16), ("w_dw", 248), ("b_dw", 8), ("ln_g", 8), ("ln_b", 8), ("b_pw2", 8),
            ("kv_g", 8), ("g_ffn", 16), ("f_w_dw", 264), ("f_b_dw", 88), ("g_ple", 16), ("gq2", 1), ("gk2", 1),
            ("sinks", 8)]
    off = {}
    o = 0
    for n, c in ents:
        off[n] = o
        o += c
    return off, o


VOFF, VN = _vec_layout()
VROWS = 768


def _chunk_heads(c):
    gp, j = c // 4, c % 4
    return 8 * gp + j, 8 * gp + 4 + j


class Res:
    __slots__ = ("w", "r", "name")

    def __init__(self, name=""):
        self.w = []
        self.r = []
        self.name = name


class Prog:
    def __init__(self, nc, n_dma_sems=10):
        self.nc = nc
        self.names = ["pe", "act", "dve", "pool", "sp"]
        self.ops = {n: [] for n in self.names}
        self.cnt = {n: 0 for n in self.names}
        self.seen = {n: {} for n in self.names}
        self.sem = {}
        for n in ["pe", "act", "dve", "pool"]:
            self.sem[n] = nc.alloc_semaphore("s_" + n)
        self.dq = ["sp", "pool"]
        self.dsem = {q: [nc.alloc_semaphore(f"d_{q}{i}") for i in range(n_dma_sems)] for q in self.dq}
        self.dcnt = {q: [0] * n_dma_sems for q in self.dq}
        self.dnext = {q: 0 for q in self.dq}

    def _waits(self, eng, evs):
        best = {}
        for ev in evs:
            if ev is None:
                continue
            s, v = ev
            k = id(s)
            if k not in best or best[k][1] < v:
                best[k] = (s, v)
        out = []
        seen = self.seen[eng]
        for k, (s, v) in best.items():
            if seen.get(k, 0) >= v:
                continue
            seen[k] = v
            out.append((s, v))
        return out

    @staticmethod
    def _deps(reads, writes, extra):
        evs = list(extra)
        for r in reads:
            evs.extend(r.w)
        for w in writes:
            evs.extend(w.w)
            evs.extend(w.r)
        return evs

    @staticmethod
    def _commit(ev, reads, writes):
        evl = ev if isinstance(ev, list) else [ev]
        for r in reads:
            r.r.extend(evl)
        for w in writes:
            w.w = list(evl)
            w.r = []

    def op(self, eng, fn, reads=(), writes=(), extra=()):
        waits = self._waits(eng, self._deps(reads, writes, extra))
        self.cnt[eng] += 1
        sem = self.sem[eng]
        ev = (sem, self.cnt[eng])

        def run(e):
            for s, v in waits:
                e.wait_ge(s, v)
            fn(e).then_inc(sem, 1)

        self.ops[eng].append(run)
        self._commit(ev, reads, writes)
        return ev

    def dma(self, q, out, in_, reads=(), writes=(), extra=()):
        i = self.dnext[q]
        self.dnext[q] = (i + 1) % len(self.dsem[q])
        s = self.dsem[q][i]
        prev = self.dcnt[q][i]
        evs = self._deps(reads, writes, extra)
        if prev > 0:
            evs.append((s, prev))
        waits = self._waits(q, evs)
        self.dcnt[q][i] = prev + 16
        ev = (s, prev + 16)

        def run(e):
            for s2, v in waits:
                e.wait_ge(s2, v)
            e.dma_start(out=out, in_=in_).then_inc(s, 16)

        self.ops[q].append(run)
        self._commit(ev, reads, writes)
        return ev

    def dma_many(self, q, pairs, reads=(), writes=()):
        evs0 = self._deps(reads, writes, ())
        out_evs = []
        for (out, in_) in pairs:
            i = self.dnext[q]
            self.dnext[q] = (i + 1) % len(self.dsem[q])
            s = self.dsem[q][i]
            prev = self.dcnt[q][i]
            evs = list(evs0) + ([(s, prev)] if prev > 0 else [])
            waits = self._waits(q, evs)
            self.dcnt[q][i] = prev + 16
            out_evs.append((s, prev + 16))

            def run(e, waits=waits, out=out, in_=in_, s=s):
                for s2, v in waits:
                    e.wait_ge(s2, v)
                e.dma_start(out=out, in_=in_).then_inc(s, 16)

            self.ops[q].append(run)
        self._commit(out_evs, reads, writes)
        return out_evs

    def finish(self):
        nc = self.nc
        finals = [(s, c) for q in self.dq for s, c in zip(self.dsem[q], self.dcnt[q]) if c > 0]
        waits = self._waits("sp", finals)

        def fin(e):
            for s, v in waits:
                e.wait_ge(s, v)

        self.ops["sp"].append(fin)
        with nc.Block() as block:
            @block.tensor
            def _(e):
                for f in self.ops["pe"]:
                    f(e)

            @block.scalar
            def _(e):
                for f in self.ops["act"]:
                    f(e)

            @block.vector
            def _(e):
                for f in self.ops["dve"]:
                    f(e)

            @block.gpsimd
            def _(e):
                for f in self.ops["pool"]:
                    f(e)

            @block.sync
            def _(e):
                for f in self.ops["sp"]:
                    f(e)


class Buf:
    def __init__(self, ap, off, nbytes, inherit, name):
        self.ap = ap
        self.off = off
        self.nbytes = nbytes
        self.inherit = inherit
        self.name = name
        self.R = {}

    def res(self, key=None):
        if key not in self.R:
            r = Res(f"{self.name}:{key}")
            r.r = list(self.inherit)
            self.R[key] = r
        return self.R[key]

    def all_events(self):
        evs = list(self.inherit)
        for r in self.R.values():
            evs.extend(r.w)
            evs.extend(r.r)
        return evs


class Arena:
    def __init__(self, nc, nbytes):
        self.t = nc.alloc_sbuf_tensor("arena", [128, nbytes], U8)
        self.n = nbytes
        self.live = []
        self.dead = []
        self.peak = 0

    def alloc(self, name, shape, dtype):
        esz = 4 if dtype == F32 else 2
        n = 1
        for s in shape:
            n *= s
        nbytes = ((n * esz + 63) // 64) * 64
        spans = sorted((b.off, b.off + b.nbytes) for b in self.live)
        off = 0
        for a, b in spans:
            if off + nbytes <= a:
                break
            off = max(off, b)
        if off + nbytes > self.n:
            raise RuntimeError(f"arena OOM allocating {name} ({nbytes}); live={[(b.name, b.nbytes) for b in self.live]}")
        self.peak = max(self.peak, off + nbytes)
        inherit = []
        keep = []
        for (o2, n2, evs) in self.dead:
            if o2 < off + nbytes and off < o2 + n2:
                inherit.extend(evs)
                if not (off <= o2 and o2 + n2 <= off + nbytes):
                    keep.append((o2, n2, evs))
            else:
                keep.append((o2, n2, evs))
        self.dead = keep
        v = self.t[:, off:off + n * esz].bitcast(dtype)
        if len(shape) == 2:
            v = v.rearrange("p (a b) -> p a b", a=shape[0])
        elif len(shape) == 3:
            v = v.rearrange("p (a b c) -> p a b c", a=shape[0], b=shape[1])
        elif len(shape) == 4:
            v = v.rearrange("p (a b c d) -> p a b c d", a=shape[0], b=shape[1], c=shape[2])
        buf = Buf(v, off, nbytes, inherit, name)
        self.live.append(buf)
        return buf

    def free(self, buf):
        self.live.remove(buf)
        self.dead.append((buf.off, buf.nbytes, buf.all_events()))


def build_program():
    nc = bass.Bass("TRN2", target_bir_lowering=False)
    P = Prog(nc)

    def din(name, shape):
        return nc.dram_tensor(name, list(shape), F32, kind="ExternalInput").ap()

    def dout(name, shape):
        return nc.dram_tensor(name, list(shape), F32, kind="ExternalOutput").ap()

    x_all = din("x_all", [T, D])
    p_all = din("p_all", [2, T, 256])
    st_conv = din("st_conv", [16, 30, D])
    st_ffn = din("st_ffn", [2, 32, 2 * DFF])
    cache_k = din("cache_k", [16, 128, 256])
    cache_v = din("cache_v", [16, 128, 256])
    vecs = din("vecs", [VROWS, 128])
    cst = din("cst", [128, 256])
    ohu = din("ohu", [33, LU])
    rel_bias = din("rel_bias", [32, 16])
    w_pw1 = din("w_pw1", [D, 2 * D])
    w_pw2 = din("w_pw2", [D, D])
    w_q = din("w_q", [D, D])
    w_o = din("w_o", [D, D])
    w_k = din("w_k", [D, 256])
    w_v = din("w_v", [D, 256])
    w_up = din("w_up", [2, D, 2 * DFF])
    w_down = din("w_down", [2, DFF, D])
    w_gate = din("w_gate", [2, D, D])
    w_proj = din("w_proj", [2, 256, D])

    y = dout("y", [T, D])
    conv_p = dout("conv_p", [30, D])
    conv_s = dout("conv_s", [16, 30, D])
    ffn_p = dout("ffn_p", [2, 2, 2 * DFF])
    ffn_s = dout("ffn_s", [2, 32, 2 * DFF])
    k_p = dout("k_p", [128, 256])
    k_s = dout("k_s", [16, 128, 256])
    v_p = dout("v_p", [128, 256])
    v_s = dout("v_s", [16, 128, 256])
    sc2 = nc.dram_tensor("sc2", [16, 128 * (LU + 1)], F32, kind="Internal")

    A = Arena(nc, 211968)
    _stop = os.environ.get("KSTOP", "")

    def stop(tag):
        if _stop == tag:
            P.finish()
            return True
        return False
    banks = [nc.alloc_psum_tensor(f"bank{i}", [128, 512], F32) for i in range(8)]
    bank_res = [Res(f"bank{i}") for i in range(8)]
    bstate = {"n": 0, "held": set()}

    def bank(hold=False):
        while True:
            i = bstate["n"] % 8
            bstate["n"] += 1
            if i not in bstate["held"]:
                break
        if hold:
            bstate["held"].add(i)
        return banks[i], bank_res[i]

    def release(bk):
        bstate["held"].discard(banks.index(bk))

    def ACT(out, in_, func, reads, writes, bias=0.0, scale=1.0):
        return P.op("act", lambda e: e.activation(out=out, in_=in_, func=func, bias=bias, scale=scale), reads, writes)

    def TTo(eng, out, in0, in1, op, reads, writes):
        return P.op(eng, lambda e: e.tensor_tensor(out=out, in0=in0, in1=in1, op=op), reads, writes)

    def TS(eng, out, in0, s1, s2, op0, op1, reads, writes):
        if s2 is None:
            return P.op(eng, lambda e: e.tensor_scalar(out=out, in0=in0, scalar1=s1, scalar2=None, op0=op0), reads, writes)
        return P.op(eng, lambda e: e.tensor_scalar(out=out, in0=in0, scalar1=s1, scalar2=s2, op0=op0, op1=op1), reads, writes)

    def STT(eng, out, in0, scalar, in1, op0, op1, reads, writes, extra=()):
        return P.op(eng, lambda e: e.scalar_tensor_tensor(out=out, in0=in0, scalar=scalar, in1=in1, op0=op0, op1=op1),
                    reads, writes, extra)

    def RSTD(out, in_, reads, writes):
        ACT(out, in_, AF.Ln, list(reads) + [Rcb], writes, bias=epsc)
        return ACT(out, out, AF.Exp, writes, writes, scale=-0.5)

    def CP(eng, out, in_, reads, writes):
        if eng == "act":
            return P.op("act", lambda e: e.activation(out=out, in_=in_, func=AF.Copy), reads, writes)
        return P.op(eng, lambda e: e.tensor_copy(out=out, in_=in_), reads, writes)

    def MEMSET(eng, out, val, writes):
        return P.op(eng, lambda e: e.memset(out, val), (), writes)

    def MM(mms, reads, writes):
        def fn(e):
            last = None
            for m in mms:
                kw = {}
                if m.get("tp") is not None:
                    kw["tile_position"] = m["tp"]
                if m.get("sgc"):
                    kw["skip_group_check"] = True
                last = e.matmul(m["out"], lhsT=m["lhsT"], rhs=m["rhs"], start=m["start"], stop=m["stop"], **kw)
            return last
        return P.op("pe", fn, reads, writes)

    def TR(items, reads, writes):
        def fn(e):
            last = None
            for o, i, idn in items:
                last = e.transpose(o, i, idn)
            return last
        return P.op("pe", fn, reads, writes)

    def acc(out, pairs, tp=None):
        n = len(pairs)
        return [dict(out=out, lhsT=l, rhs=r, start=(k == 0), stop=(k == n - 1), tp=tp) for k, (l, r) in enumerate(pairs)]

    hB = A.alloc("h", [8, T], F32)
    h = hB.ap
    Rh = [[hB.res((i, c)) for c in range(8)] for i in range(5)]
    XN = {}

    def xn_alloc():
        XN["B"] = A.alloc("xn", [8, T], BF16)
        return XN["B"].ap, [XN["B"].res(i) for i in range(5)]

    def xn_free():
        A.free(XN["B"])

    xn, Rxn = xn_alloc()
    WB = A.alloc("W", [2 * SLOT], BF16)
    W = WB.ap
    RW = [WB.res(0), WB.res(1)]
    vtB = A.alloc("vt", [VROWS], F32)
    vt = vtB.ap
    Rvt = vtB.res()
    cB = A.alloc("consts", [2, 128], F32)
    identf = cB.ap[:, 0, :]
    bmask = cB.ap[:, 1, :]
    Rc = cB.res()
    cbB = A.alloc("constsb", [4, 128], BF16)
    identb = cbB.ap[:, 0, :]
    onesN = cbB.ap[:, 1, :]
    ones64 = cbB.ap[:, 2, :]
    ones1 = cbB.ap[:, 3, :]
    Rcb = cbB.res()

    def vcol(name, idx=0):
        c = VOFF[name] + idx
        return vt[:, c:c + 1]

    P.dma("sp", cB.ap[:, 0:2, :], cst.rearrange("p (a b) -> p a b", a=2), writes=[Rc])
    CP("dve", identb, identf, [Rc], [Rcb])
    MEMSET("dve", onesN, 1.0 / 1024.0, [Rcb])
    MEMSET("dve", ones64, 0.0, [Rcb])
    MEMSET("dve", cbB.ap[0:64, 2, 0:64], 1.0 / 64.0, [Rcb])
    MEMSET("dve", cbB.ap[64:128, 2, 64:128], 1.0 / 64.0, [Rcb])
    MEMSET("dve", ones1, 1.0, [Rcb])
    epsB = A.alloc("epsc", [1], F32)
    epsc = epsB.ap[:, 0:1]
    MEMSET("dve", epsc, EPS, [Rcb])

    vinB = A.alloc("vin", [6, 128], F32)
    Rvin = vinB.res()
    P.dma("sp", vinB.ap, vecs.rearrange("(a p) f -> p a f", p=128), writes=[Rvin])
    for half in range(2):
        bk, br = bank()
        TR([(bk[:, q * 128:(q + 1) * 128], vinB.ap[:, half * 3 + q, :], identf) for q in range(3)], [Rvin, Rc], [br])
        CP("dve", vt[:, half * 384:(half + 1) * 384], bk[:, 0:384], [br], [Rvt])
    A.free(vinB)

    def load_rows(dst3, src2, nk, slots):
        P.dma_many("pool", [(dst3[:, kc, :], src2[kc * 128:(kc + 1) * 128, :]) for kc in range(nk)], writes=slots)

    def wview(off, a, b):
        return W[:, off:off + a * b].rearrange("p (a b) -> p a b", a=a)

    def load_pw1():
        load_rows(wview(0, 8, 2048), w_pw1, 8, [RW[0], RW[1]])

    def load_pw2():
        load_rows(wview(0, 8, 1024), w_pw2, 8, [RW[0]])

    def load_ffn(l, gi, slot):
        j0, nj = GROUPS[gi]
        base = slot * SLOT
        wu = wview(base, 8, 2 * nj * 128)
        wd = wview(base + 16 * nj * 128, nj, 1024)
        pairs = []
        for kc in range(8):
            pairs.append((wu[:, kc, 0:nj * 128], w_up[l, kc * 128:(kc + 1) * 128, j0 * 128:(j0 + nj) * 128]))
            pairs.append((wu[:, kc, nj * 128:2 * nj * 128],
                          w_up[l, kc * 128:(kc + 1) * 128, DFF + j0 * 128:DFF + (j0 + nj) * 128]))
        for jj in range(nj):
            pairs.append((wd[:, jj, :], w_down[l, (j0 + jj) * 128:(j0 + jj + 1) * 128, :]))
        P.dma_many("pool", pairs, writes=[RW[slot]])
        return wu, wd

    def build_eb():
        tbB = A.alloc("tbl", [16], F32)
        Rtb = tbB.res()
        ohB = A.alloc("ohu", [LU], F32)
        Roh = ohB.res()
        UB = A.alloc("U", [LU], F32)
        RU = UB.res()
        MEMSET("dve", tbB.ap[32:33, :], -30000.0, [Rtb])
        P.dma("sp", tbB.ap[0:32, :], rel_bias, writes=[Rtb])
        P.dma("sp", ohB.ap[0:33, :], ohu, writes=[Roh])
        bk, br = bank()
        MM([dict(out=bk[0:16, 0:LU], lhsT=tbB.ap[0:33, :], rhs=ohB.ap[0:33, :], start=True, stop=True)], [Rtb, Roh], [br])
        ACT(UB.ap[0:16, :], bk[0:16, 0:LU], AF.Exp, [br], [RU])
        Rsc = Res("sc2")
        pst = UB.ap.ap[0][0]
        P.dma("sp", bass.AP(sc2, 0, [[128 * (LU + 1), 16], [LU + 1, 128], [1, LU]]),
              bass.AP(UB.ap.tensor, UB.ap.offset, [[pst, 16], [0, 128], [1, LU]]), reads=[RU], writes=[Rsc])
        EBc = A.alloc("EBc", [16, 128], BF16)
        EBp = A.alloc("EBp", [16, 128], BF16)
        P.dma("pool", EBc.ap, bass.AP(sc2, 127, [[LU, 128], [128 * (LU + 1), 16], [1, 128]]), reads=[Rsc], writes=[EBc.res()])
        P.dma("pool", EBp.ap, bass.AP(sc2, 255, [[LU, 128], [128 * (LU + 1), 16], [1, 128]]), reads=[Rsc], writes=[EBp.res()])
        A.free(tbB)
        A.free(ohB)
        A.free(UB)
        return EBc, EBp

    load_pw1()

    xinB = A.alloc("xin", [2, D], F32)
    for blk in range(NBLK):
        i = min(blk // 4, 4)
        rx = xinB.res(blk % 2)
        P.dma("sp", xinB.ap[:, blk % 2, :], x_all[blk * 128:(blk + 1) * 128, :], writes=[rx])
        for half in range(2):
            bk, br = bank()
            TR([(bk[:, q * 128:(q + 1) * 128], xinB.ap[:, blk % 2, (4 * half + q) * 128:(4 * half + q + 1) * 128], identf)
                for q in range(4)], [rx, Rc], [br])
            CP("act" if half == 0 else "dve", h[:, 4 * half:4 * half + 4, blk * 128:(blk + 1) * 128],
               bk[:, :].rearrange("p (a b) -> p a b", a=4), [br], [Rh[i][c] for c in range(4 * half, 4 * half + 4)])
    A.free(xinB)
    if stop("p0"):
        return nc

    def rmsnorm(i, gname, gidx):
        c0, n = TT[i]
        sq = A.alloc("sq", [8, n], BF16)
        rs = A.alloc("rstd", [n], F32)
        ACT(sq.ap, h[:, :, c0:c0 + n], AF.Square, Rh[i], [sq.res()])
        bk, br = bank()
        MM(acc(bk[:, 0:n], [(onesN, sq.ap[:, c, :]) for c in range(8)]), [sq.res(), Rcb], [br])
        RSTD(rs.ap, bk[:, 0:n], [br], [rs.res()])
        for c in range(8):
            STT("dve", xn[:, c, c0:c0 + n], h[:, c, c0:c0 + n], vcol(gname, gidx * 8 + c), rs.ap,
                ALU.mult, ALU.mult, [Rh[i][c], rs.res(), Rvt], [Rxn[i]])
        A.free(sq)
        A.free(rs)

    uhB = A.alloc("uh", [8, UW], BF16)
    uh = uhB.ap
    Ru = [uhB.res(i) for i in range(5)]
    Rhist = uhB.res("hist")
    ustB = A.alloc("ust", [8, 158], F32)
    Rust = ustB.res()
    MEMSET("pool", uh[:, :, 0:30], 0.0, [Ru[0]])

    P.dma("sp", conv_s[:, 0:22, :], st_conv[:, 8:30, :])
    scB = A.alloc("scin", [D], F32)
    for blk in range(4):
        rsb = scB.res()
        P.dma("sp", scB.ap[0:120, :], st_conv[4 * blk:4 * blk + 4, :, :].rearrange("b r d -> (b r) d"), writes=[rsb])
        for half in range(2):
            bk, br = bank()
            TR([(bk[:, q * 120:(q + 1) * 120], scB.ap[0:120, (4 * half + q) * 128:(4 * half + q + 1) * 128], identf[0:120, 0:120])
                for q in range(4)], [rsb, Rc], [br])
            for q in range(4):
                c = 4 * half + q
                dst = uh[:, c, SB0 + 38 * 4 * blk:SB0 + 38 * 4 * (blk + 1)].rearrange("p (b w) -> p b w", w=38)[:, :, 0:30]
                CP("dve" if q % 2 == 0 else "act", dst, bk[:, q * 120:(q + 1) * 120].rearrange("p (b r) -> p b r", r=30),
                   [br], [Rhist])
    A.free(scB)

    for i in range(5):
        rmsnorm(i, "g_mix", 0)

    wp1 = wview(0, 8, 2048)
    sgB = A.alloc("sg", [2, 512], F32)
    for i in range(5):
        c0, n = TT[i]
        for m in range(8):
            ba, bra = bank()
            bg, brg = bank()
            MM(acc(ba[:, 0:n], [(wp1[:, kc, m * 128:(m + 1) * 128], xn[:, kc, c0:c0 + n]) for kc in range(8)]),
               [RW[0], RW[1], Rxn[i]], [bra])
            MM(acc(bg[:, 0:n], [(wp1[:, kc, (8 + m) * 128:(9 + m) * 128], xn[:, kc, c0:c0 + n]) for kc in range(8)]),
               [RW[0], RW[1], Rxn[i]], [brg])
            rsg = sgB.res(m % 2)
            sg = sgB.ap[:, m % 2, 0:n]
            ACT(sg, bg[:, 0:n], AF.Sigmoid, [brg, Rvt], [rsg], bias=vcol("b_pw1", 8 + m))
            if i < 4:
                dst = uh[:, m, 30 + c0:30 + c0 + n]
                STT("dve", dst, ba[:, 0:n], vcol("b_pw1", m), sg, ALU.add, ALU.mult, [bra, rsg, Rvt], [Ru[i]])
                if i == 3:
                    STT("dve", ustB.ap[:, m, 0:30], ba[:, 482:512], vcol("b_pw1", m), sg[:, 482:512], ALU.add, ALU.mult,
                        [bra, rsg, Rvt], [Rust])
            else:
                dst = uh[:, m, SB0:UW].rearrange("p (b w) -> p b w", w=38)[:, :, 30:38]
                STT("dve", dst, ba[:, 0:128].rearrange("p (b t) -> p b t", t=8), vcol("b_pw1", m),
                    sg.rearrange("p (b t) -> p b t", t=8), ALU.add, ALU.mult, [bra, rsg, Rvt], [Ru[4]])
                STT("dve", ustB.ap[:, m, 30:158], ba[:, 0:128], vcol("b_pw1", m), sg, ALU.add, ALU.mult,
                    [bra, rsg, Rvt], [Rust])
    A.free(sgB)
    if stop("p1a"):
        return nc
    load_pw2()

    cpoB = A.alloc("cpo", [D], F32)
    for half in range(2):
        bk, br = bank()
        TR([(bk[0:30, q * 128:(q + 1) * 128], ustB.ap[:, 4 * half + q, 0:30], identf) for q in range(4)], [Rust, Rc], [br])
        CP("act", cpoB.ap[0:30, half * 512:(half + 1) * 512], bk[0:30, :], [br], [cpoB.res()])
    P.dma("sp", conv_p, cpoB.ap[0:30, :], reads=[cpoB.res()])
    csoB = A.alloc("cso", [D], F32)
    for half in range(2):
        bk, br = bank()
        TR([(bk[:, q * 128:(q + 1) * 128], ustB.ap[:, 4 * half + q, 30:158], identf) for q in range(4)], [Rust, Rc], [br])
        CP("act", csoB.ap[:, half * 512:(half + 1) * 512], bk[:, :], [br], [csoB.res()])
    for b in range(16):
        P.dma("sp", conv_s[b, 22:30, :], csoB.ap[8 * b:8 * b + 8, :], reads=[csoB.res()])
    A.free(cpoB)
    A.free(csoB)
    A.free(ustB)
    if stop("p1"):
        return nc

    xn_free()
    wp2 = wview(0, 8, 1024)
    ffw = {}
    c2B = A.alloc("c2t", [8, 512], BF16)
    Rc2 = c2B.res()
    DgB = A.alloc("Dg", [2, 31, 128], BF16)
    ccB = A.alloc("cc", [8, 512], F32)
    cbfB = A.alloc("cbf", [2, 512], BF16)
    sqcB = A.alloc("sqc", [2, 512], BF16)
    lnB = A.alloc("lnst", [3, 512], F32)
    wdw_pst = vt.ap[0][0]
    for i in range(5):
        c0, n = TT[i]
        bm, brm = bank(hold=True)
        bq, brq = bank(hold=True)
        for c in range(8):
            rdg = DgB.res(c % 2)
            NDV = 25
            for (eng_, k0, k1) in (("dve", 0, NDV), ("pool", NDV, 31)):
                in0 = bass.AP(identb.tensor, identb.offset, [[identb.ap[0][0], 128], [0, k1 - k0], [1, 128]])
                wsrc = vcol("w_dw", k0 * 8 + c)
                in1 = bass.AP(wsrc.tensor, wsrc.offset, [[wdw_pst, 128], [8, k1 - k0], [0, 128]])
                TTo(eng_, DgB.ap[:, c % 2, k0:k1, :], in0, in1, ALU.mult, [Rcb, Rvt], [DgB.res((c % 2, eng_))])
            bk, br = bank()
            if i < 4:
                rhs = [uh[:, c, c0 + k:c0 + k + n] for k in range(31)]
                rds = [Ru[i]] + ([Ru[i - 1]] if i > 0 else [])
            else:
                sv = uh[:, c, SB0:UW].rearrange("p (b w) -> p b w", w=38)
                rhs = [sv[:, :, k:k + 8] for k in range(31)]
                rds = [Ru[4], Rhist]
            outv = bk[:, 0:n] if i < 4 else bk[:, 0:128].rearrange("p (b t) -> p b t", t=8)
            MM(acc(outv, [(DgB.ap[:, c % 2, k, :], rhs[k]) for k in range(31)]),
               [DgB.res((c % 2, "dve")), DgB.res((c % 2, "pool"))] + rds, [br])
            rcc = ccB.res(c)
            ACT(ccB.ap[:, c, 0:n], bk[:, 0:n], AF.Identity, [br, Rvt], [rcc], bias=vcol("b_dw", c))
            ACT(cbfB.ap[:, c % 2, 0:n], bk[:, 0:n], AF.Identity, [br, Rvt], [cbfB.res(c % 2)], bias=vcol("b_dw", c))
            ACT(sqcB.ap[:, c % 2, 0:n], ccB.ap[:, c, 0:n], AF.Square, [rcc], [sqcB.res(c % 2)])
            MM([dict(out=bm[:, 0:n], lhsT=onesN, rhs=cbfB.ap[:, c % 2, 0:n], start=(c == 0), stop=(c == 7))],
               [cbfB.res(c % 2), Rcb], [brm])
            MM([dict(out=bq[:, 0:n], lhsT=onesN, rhs=sqcB.ap[:, c % 2, 0:n], start=(c == 0), stop=(c == 7))],
               [sqcB.res(c % 2), Rcb], [brq])
        mean = lnB.ap[:, 0, 0:n]
        m2 = lnB.ap[:, 1, 0:n]
        rstd = lnB.ap[:, 2, 0:n]
        rln = lnB.res()
        CP("act", mean, bm[:, 0:n], [brm], [rln])
        TTo("dve", m2, mean, mean, ALU.mult, [rln], [rln])
        TTo("dve", m2, bq[:, 0:n], m2, ALU.subtract, [brq, rln], [rln])
        RSTD(rstd, m2, [rln], [rln])
        release(bm)
        release(bq)
        for c in range(8):
            rcc = ccB.res(c)
            TTo("dve", ccB.ap[:, c, 0:n], ccB.ap[:, c, 0:n], mean, ALU.subtract, [rcc, rln], [rcc])
            TTo("dve", ccB.ap[:, c, 0:n], ccB.ap[:, c, 0:n], rstd, ALU.mult, [rcc, rln], [rcc])
            ACT(c2B.ap[:, c, 0:n], ccB.ap[:, c, 0:n], AF.Silu, [rcc, Rvt], [Rc2], bias=vcol("ln_b", c),
                scale=vcol("ln_g", c))
        if i == 3:
            ffw[(0, 0)] = load_ffn(0, 0, 1)
        for m in range(8):
            bk, br = bank()
            MM(acc(bk[:, 0:n], [(wp2[:, kc, m * 128:(m + 1) * 128], c2B.ap[:, kc, 0:n]) for kc in range(8)]),
               [RW[0], Rc2], [br])
            STT("dve", h[:, m, c0:c0 + n], bk[:, 0:n], vcol("b_pw2", m), h[:, m, c0:c0 + n], ALU.add, ALU.add,
                [br, Rvt], [Rh[i][m]])
    for b_ in (DgB, ccB, cbfB, sqcB, lnB, c2B):
        A.free(b_)
    A.free(uhB)
    xn, Rxn = xn_alloc()
    if stop("p3"):
        return nc

    def proj_residual(wv, i, rslots, bias_name=None):
        c0, n = TT[i]
        for m in range(8):
            bk, br = bank()
            MM(acc(bk[:, 0:n], [(wv[:, kc, m * 128:(m + 1) * 128], xn[:, kc, c0:c0 + n]) for kc in range(8)]),
               rslots + [Rxn[i]], [br])
            if bias_name is not None:
                STT("dve", h[:, m, c0:c0 + n], bk[:, 0:n], vcol(bias_name, m), h[:, m, c0:c0 + n], ALU.add, ALU.add,
                    [br, Rvt], [Rh[i][m]])
            else:
                TTo("dve", h[:, m, c0:c0 + n], bk[:, 0:n], h[:, m, c0:c0 + n], ALU.add, [br], [Rh[i][m]])


    def ffn(l, slot0, after_first_group=None):
        for i in range(5):
            rmsnorm(i, "g_ffn", l)
        fstB = A.alloc("fst", [NF, 34], F32)
        Rfst = fstB.res()
        upB = A.alloc("uprev", [3, NF, 2], F32)
        hsB = A.alloc("hs", [NF, 32], F32)
        Rhs = hsB.res()
        sfB = A.alloc("sfin", [1408], F32)
        for q4 in range(4):
            rsf = sfB.res()
            P.dma("sp", sfB.ap[0:32, :], st_ffn[l, :, q4 * 1408:(q4 + 1) * 1408], writes=[rsf])
            bk, br = bank()
            TR([(bk[:, q * 32:(q + 1) * 32], sfB.ap[0:32, q * 128:(q + 1) * 128], identf[0:32, 0:32]) for q in range(11)],
               [rsf, Rc], [br])
            CP("dve", hsB.ap[:, q4 * 11:(q4 + 1) * 11, :], bk[:, 0:352].rearrange("p (a b) -> p a b", a=11), [br], [Rhs])
        A.free(sfB)
        cgB = A.alloc("cg", [2, 2, 512], F32)
        sgtB = A.alloc("sgt", [2, 512], F32)
        actB = A.alloc("act", [2, 4, 512], BF16)
        units = [(gi, i) for gi in range(len(GROUPS)) for i in range(5)]

        def unit_ctx(k):
            gi, i = units[k]
            j0, nj = GROUPS[gi]
            slot = (slot0 + gi) % 2
            wu, wd = ffw[(l, gi)]
            c0, n = TT[i]
            return gi, i, j0, nj, slot, wu, wd, c0, n, actB.res(k % 2)

        def emit_up(k):
            gi, i, j0, nj, slot, wu, wd, c0, n, ract = unit_ctx(k)
            for jj in range(nj):
                j = j0 + jj
                par = (jj + i) % 2
                cvals = []
                for gv in range(2):
                    f = j + 22 * gv
                    bk, br = bank()
                    MM(acc(bk[:, 0:n], [(wu[:, kc, (gv * nj + jj) * 128:(gv * nj + jj + 1) * 128], xn[:, kc, c0:c0 + n])
                                        for kc in range(8)]), [RW[slot], Rxn[i]], [br])
                    rcg = cgB.res((par, gv))
                    cg = cgB.ap[:, par, gv, 0:n]
                    w0 = vcol("f_w_dw", (l * 3 + 0) * NF + f)
                    w1 = vcol("f_w_dw", (l * 3 + 1) * NF + f)
                    w2 = vcol("f_w_dw", (l * 3 + 2) * NF + f)
                    bb = vcol("f_b_dw", l * NF + f)
                    ACT(cg, bk[:, 0:n], AF.Identity, [br, Rvt], [rcg], bias=bb, scale=w2)
                    if i < 4:
                        if i < 3:
                            rup = upB.res((i, f))
                            evc = CP("act", upB.ap[:, i, f, :], bk[:, n - 2:n], [br], [rup])
                        else:
                            evc = CP("act", fstB.ap[:, f, 0:2], bk[:, n - 2:n], [br], [Rfst])
                        STT("dve", cg[:, 1:n], bk[:, 0:n - 1], w1, cg[:, 1:n], ALU.mult, ALU.add, [br, Rvt], [rcg], extra=[evc])
                        STT("dve", cg[:, 2:n], bk[:, 0:n - 2], w0, cg[:, 2:n], ALU.mult, ALU.add, [br, Rvt], [rcg])
                        if i > 0:
                            rprev = upB.res((i - 1, f))
                            ext = upB.ap[:, i - 1, f, :]
                            STT("dve", cg[:, 0:2], ext, w0, cg[:, 0:2], ALU.mult, ALU.add, [rprev, Rvt], [rcg])
                            STT("dve", cg[:, 0:1], ext[:, 1:2], w1, cg[:, 0:1], ALU.mult, ALU.add, [rprev, Rvt], [rcg])
                    else:
                        cg3 = cg.rearrange("p (b t) -> p b t", t=8)
                        bk3 = bk[:, 0:128].rearrange("p (b t) -> p b t", t=8)
                        hs3 = hsB.ap[:, f, :].rearrange("p (b r) -> p b r", r=2)
                        evc = CP("act", fstB.ap[:, f, 2:34].rearrange("p (b r) -> p b r", r=2), bk3[:, :, 6:8], [br], [Rfst])
                        STT("dve", cg3[:, :, 1:8], bk3[:, :, 0:7], w1, cg3[:, :, 1:8], ALU.mult, ALU.add, [br, Rvt], [rcg], extra=[evc])
                        STT("dve", cg3[:, :, 2:8], bk3[:, :, 0:6], w0, cg3[:, :, 2:8], ALU.mult, ALU.add, [br, Rvt], [rcg])
                        STT("dve", cg3[:, :, 0:2], hs3, w0, cg3[:, :, 0:2], ALU.mult, ALU.add, [Rhs, Rvt], [rcg])
                        STT("dve", cg3[:, :, 0:1], hs3[:, :, 1:2], w1, cg3[:, :, 0:1], ALU.mult, ALU.add, [Rhs, Rvt], [rcg])
                    cvals.append((cg, rcg))
                rsg = sgtB.res(par)
                sgt = sgtB.ap[:, par, 0:n]
                ACT(sgt, cvals[0][0], AF.Silu, [cvals[0][1]], [rsg])
                TTo("pool", actB.ap[:, k % 2, jj, 0:n], sgt, cvals[1][0], ALU.mult, [rsg, cvals[1][1]], [ract])

        def emit_down(k):
            gi, i, j0, nj, slot, wu, wd, c0, n, ract = unit_ctx(k)
            for m in range(8):
                bk, br = bank()
                MM(acc(bk[:, 0:n], [(wd[:, jj, m * 128:(m + 1) * 128], actB.ap[:, k % 2, jj, 0:n]) for jj in range(nj)]),
                   [RW[slot], ract], [br])
                TTo("dve", h[:, m, c0:c0 + n], bk[:, 0:n], h[:, m, c0:c0 + n], ALU.add, [br], [Rh[i][m]])

        if len(GROUPS) > 1:
            ffw[(l, 1)] = load_ffn(l, 1, (slot0 + 1) % 2)
        emit_up(0)
        for k in range(len(units)):
            if k + 1 < len(units):
                emit_up(k + 1)
            emit_down(k)
            gi, i = units[k]
            if i == 4 and gi + 2 < len(GROUPS):
                ffw[(l, gi + 2)] = load_ffn(l, gi + 2, (slot0 + gi) % 2)
        for b_ in (cgB, sgtB, actB, upB, hsB):
            A.free(b_)
        ostB = A.alloc("ost", [1408], F32)
        for q4 in range(4):
            rost = ostB.res()
            for t3 in range(3):
                nq = 4 if t3 < 2 else 3
                bk, br = bank()
                TR([(bk[0:34, q * 128:(q + 1) * 128], fstB.ap[:, q4 * 11 + t3 * 4 + q, :], identf) for q in range(nq)],
                   [Rfst, Rc], [br])
                CP("act", ostB.ap[0:34, t3 * 512:t3 * 512 + nq * 128], bk[0:34, 0:nq * 128], [br], [rost])
            P.dma("sp", ffn_p[l, :, q4 * 1408:(q4 + 1) * 1408], ostB.ap[0:2, :], reads=[rost])
            P.dma("sp", ffn_s[l, :, q4 * 1408:(q4 + 1) * 1408], ostB.ap[2:34, :], reads=[rost])
        A.free(ostB)
        A.free(fstB)

    def load_ple(l, slot):
        wg = wview(slot * SLOT, 8, 1024)
        wp = wview((1 - slot) * SLOT + 4096, 2, 1024)
        load_rows(wg, w_gate[l], 8, [RW[slot]])
        load_rows(wp, w_proj[l], 2, [RW[1 - slot]])
        return wg, wp

    def ple(l, slot, wg, wp, prefetch=None):
        pTB = A.alloc("pT", [2, T], BF16)
        pinB = A.alloc("pin", [2, 256], F32)
        RpT = [pTB.res(i) for i in range(5)]
        for blk in range(NBLK):
            i = min(blk // 4, 4)
            rp = pinB.res(blk % 2)
            P.dma("sp", pinB.ap[:, blk % 2, :], p_all[l, blk * 128:(blk + 1) * 128, :], writes=[rp])
            bk, br = bank()
            TR([(bk[:, q * 128:(q + 1) * 128], pinB.ap[:, blk % 2, q * 128:(q + 1) * 128], identf) for q in range(2)],
               [rp, Rc], [br])
            CP("act", pTB.ap[:, :, blk * 128:(blk + 1) * 128], bk[:, 0:256].rearrange("p (a b) -> p a b", a=2), [br], [RpT[i]])
        A.free(pinB)
        for i in range(5):
            rmsnorm(i, "g_ple", l)
        if prefetch is not None:
            prefetch()
        gtB = A.alloc("gt", [2, 512], F32)
        tmB = A.alloc("tm", [2, 512], F32)
        for i in range(5):
            c0, n = TT[i]
            for m in range(8):
                bg, brg = bank()
                bp, brp = bank()
                MM(acc(bg[:, 0:n], [(wg[:, kc, m * 128:(m + 1) * 128], xn[:, kc, c0:c0 + n]) for kc in range(8)]),
                   [RW[slot], Rxn[i]], [brg])
                MM(acc(bp[:, 0:n], [(wp[:, kc, m * 128:(m + 1) * 128], pTB.ap[:, kc, c0:c0 + n]) for kc in range(2)]),
                   [RW[1 - slot], RpT[i]], [brp])
                rgt = gtB.res(m % 2)
                rtm = tmB.res(m % 2)
                ACT(gtB.ap[:, m % 2, 0:n], bg[:, 0:n], AF.Sigmoid, [brg], [rgt])
                TTo("dve", tmB.ap[:, m % 2, 0:n], bp[:, 0:n], gtB.ap[:, m % 2, 0:n], ALU.mult, [brp, rgt], [rtm])
                TTo("pool", h[:, m, c0:c0 + n], h[:, m, c0:c0 + n], tmB.ap[:, m % 2, 0:n], ALU.add, [rtm], [Rh[i][m]])
        for b_ in (gtB, tmB, pTB):
            A.free(b_)

    ple_w = {}

    def pf_ple0():
        ple_w[0] = load_ple(0, 1)

    ffn(0, 1, after_first_group=None)
    EBc, EBp = build_eb()
    if stop("ffn0"):
        return nc
    pf_ple0()

    kvw = {}

    def load_kv():
        base = 0
        wk = wview(base, 8, 256)
        wv = wview(base + 2048, 8, 256)
        P.dma_many("pool", [(wk[:, kc, :], w_k[kc * 128:(kc + 1) * 128, :]) for kc in range(8)] +
                   [(wv[:, kc, :], w_v[kc * 128:(kc + 1) * 128, :]) for kc in range(8)], writes=[RW[0]])
        kvw["k"], kvw["v"] = wk, wv

    ple(0, 1, ple_w[0][0], ple_w[0][1], prefetch=load_kv)
    if stop("ple0"):
        return nc

    KTB = A.alloc("KT", [2, T], BF16)
    KT = KTB.ap
    RKT = [KTB.res(i) for i in range(5)]
    VtB = A.alloc("Vtok", [NBLK, 256], BF16)
    Vt = VtB.ap
    RVt = [VtB.res(b) for b in range(NBLK)]
    for i in range(5):
        rmsnorm(i, "kv_g", 0)

    wqo = {}

    def load_wq():
        wqo["q"] = wview(SLOT, 8, 1024)
        stg = A.alloc("wqstg", [8, 1024], BF16)
        load_rows(stg.ap, w_q, 8, [stg.res()])
        for gp in range(2):
            for hf in range(2):
                dst = wqo["q"][:, :, gp * 512:(gp + 1) * 512].rearrange("p k (j hf d) -> p k j hf d", j=4, hf=2)[:, :, :, hf, :]
                src = stg.ap[:, :, gp * 512:(gp + 1) * 512].rearrange("p k (hf j d) -> p k hf j d", hf=2, j=4)[:, :, hf, :, :]
                CP("pool", dst, src, [stg.res()], [RW[1]])
        A.free(stg)

    load_wq()
    if stop("kv0"):
        return nc
    knoB = A.alloc("kno", [2, 256], F32)
    Rkno = knoB.res()
    kfB = A.alloc("kf", [2, 512], F32)
    ksqB = A.alloc("ksq", [2, 512], BF16)
    krsB = A.alloc("krs", [2, 512], F32)
    for i in range(5):
        c0, n = TT[i]
        for gp in range(2):
            bk, br = bank()
            MM(acc(bk[:, 0:n], [(kvw["k"][:, kc, gp * 128:(gp + 1) * 128], xn[:, kc, c0:c0 + n]) for kc in range(8)]),
               [RW[0], Rxn[i]], [br])
            rkf, rsq, rrs = kfB.res(gp), ksqB.res(gp), krsB.res(gp)
            kf = kfB.ap[:, gp, 0:n]
            CP("act", kf, bk[:, 0:n], [br], [rkf])
            ACT(ksqB.ap[:, gp, 0:n], bk[:, 0:n], AF.Square, [br], [rsq])
            b2, br2 = bank()
            MM([dict(out=b2[:, 0:n], lhsT=ones64, rhs=ksqB.ap[:, gp, 0:n], start=True, stop=True)], [rsq, Rcb], [br2])
            RSTD(krsB.ap[:, gp, 0:n], b2[:, 0:n], [br2], [rrs])
            STT("dve", KT[:, gp, c0:c0 + n], kf, vcol("gk2"), krsB.ap[:, gp, 0:n], ALU.mult, ALU.mult, [rkf, rrs, Rvt], [RKT[i]])
            if i == 3:
                STT("dve", knoB.ap[:, gp, 0:128], kf[:, 384:512], vcol("gk2"), krsB.ap[:, gp, 384:512], ALU.mult, ALU.mult,
                    [rkf, rrs, Rvt], [Rkno])
            if i == 4:
                STT("dve", knoB.ap[:, gp, 128:256], kf, vcol("gk2"), krsB.ap[:, gp, 0:128], ALU.mult, ALU.mult,
                    [rkf, rrs, Rvt], [Rkno])
    for b_ in (kfB, ksqB, krsB):
        A.free(b_)
    if stop("kv1"):
        return nc
    voB = A.alloc("vo", [2, 256], F32)
    for blk in range(NBLK):
        i = min(blk // 4, 4)
        bk, br = bank()
        MM(acc(bk[:, 0:256], [(xn[:, kc, blk * 128:(blk + 1) * 128], kvw["v"][:, kc, :]) for kc in range(8)]),
           [RW[0], Rxn[i]], [br])
        CP("act", Vt[:, blk, :], bk[:, 0:256], [br], [RVt[blk]])
        if blk >= 15 and os.environ.get("KSUB", "") != "a":
            rvo = voB.res(blk - 15)
            CP("act", voB.ap[:, blk - 15, :], bk[:, 0:256], [br], [rvo])
            if os.environ.get("KSUB", "") == "c":
                pass
            elif blk == 15:
                P.dma("sp", v_p, voB.ap[:, 0, :], reads=[rvo])
            elif os.environ.get("KSUB", "") != "b":
                for b in range(16):
                    P.dma("sp", v_s[b, 120:128, :], voB.ap[8 * b:8 * b + 8, 1, :], reads=[rvo])
    A.free(voB)
    if stop("kv2"):
        return nc
    koB = A.alloc("ko", [2, 256], F32)
    for w in range(2):
        bk, br = bank()
        TR([(bk[:, gp * 128:(gp + 1) * 128], knoB.ap[:, gp, w * 128:(w + 1) * 128], identf) for gp in range(2)], [Rkno, Rc], [br])
        rko = koB.res(w)
        CP("act", koB.ap[:, w, :], bk[:, 0:256], [br], [rko])
        if w == 0:
            P.dma("sp", k_p, koB.ap[:, 0, :], reads=[rko])
        else:
            for b in range(16):
                P.dma("sp", k_s[b, 120:128, :], koB.ap[8 * b:8 * b + 8, 1, :], reads=[rko])
    A.free(koB)
    A.free(knoB)
    P.dma("sp", k_s[:, 0:120, :], cache_k[:, 8:128, :])
    P.dma("sp", v_s[:, 0:120, :], cache_v[:, 8:128, :])
    if stop("kv"):
        return nc

    for i in range(5):
        rmsnorm(i, "g_mix", 1)
    esB = A.alloc("es", [8], F32)
    Res_es = esB.res()
    ACT(esB.ap, vt[:, VOFF["sinks"]:VOFF["sinks"] + 8], AF.Exp, [Rvt], [Res_es])

    EB_ = A.alloc("E", [2, 512], F32)
    PTB = A.alloc("PT", [2, 512], BF16)
    denB = A.alloc("den", [2, 512], F32)
    EBn = A.alloc("EBn", [16, 128], BF16)
    for hh in range(16):
        TTo("pool", EBn.ap[:, hh, :], EBc.ap[:, hh, :], bmask, ALU.mult, [EBc.res(), Rc], [EBn.res()])
    EBs = A.alloc("EBs", [4, 16, 4, 8], BF16)
    for b in range(16):
        CP("pool", EBs.ap[:, :, b, :, :], EBp.ap[:, :, 0:8].rearrange("p (g j) q -> p g j q", g=4), [EBp.res()], [EBs.res()])
    class _View:
        def __init__(self, ap, r):
            self.ap = ap
            self._r = r

        def res(self, key=None):
            return self._r

    KcT = _View(W[:, 0:4096].rearrange("p (b g s) -> p b g s", b=16, g=2), RW[0])
    VcB = _View(W[:, 4096:8192].rearrange("p (b f) -> p b f", b=16), RW[0])
    P.dma("pool", VcB.ap, cache_v.rearrange("b s f -> s b f"), writes=[VcB.res()])
    ckB = A.alloc("ck", [8, 256], F32)
    for hb in range(2):
        rck = ckB.res()
        P.dma("sp", ckB.ap, cache_k[8 * hb:8 * hb + 8].rearrange("b s f -> s b f"), writes=[rck])
        for b2 in range(0, 8, 2):
            bk, br = bank()
            TR([(bk[:, (bb * 2 + gp) * 128:(bb * 2 + gp + 1) * 128], ckB.ap[:, b2 + bb, gp * 128:(gp + 1) * 128], identf)
                for bb in range(2) for gp in range(2)], [rck, Rc], [br])
            CP("act", KcT.ap[:, 8 * hb + b2:8 * hb + b2 + 2, :, :], bk[:, :].rearrange("p (b g s) -> p b g s", b=2, g=2),
               [br], [KcT.res()])
    A.free(ckB)

    def alloc_q(n):
        return (A.alloc("QT", [2, 8, n], BF16), A.alloc("qf", [2, n], F32), A.alloc("qsq", [2, n], BF16),
                A.alloc("qrs", [2, n], F32))

    QTB, qfB, qsqB, qrsB = alloc_q(128)
    est = {"n": 0}

    def qproj(i):
        c0, n = TT[i]
        for m in range(8):
            ha, hb_ = _chunk_heads(m)
            bk, br = bank()
            lw = wqo["q"]
            MM(acc(bk[:, 0:n], [(lw[:, kc, m * 128:(m + 1) * 128], xn[:, kc, c0:c0 + n]) for kc in range(8)]),
               [RW[1], Rxn[i]], [br])
            par = m % 2
            rqf, rsq, rrs = qfB.res(par), qsqB.res(par), qrsB.res(par)
            CP("act", qfB.ap[:, par, 0:n], bk[:, 0:n], [br], [rqf])
            ACT(qsqB.ap[:, par, 0:n], bk[:, 0:n], AF.Square, [br], [rsq])
            b2, br2 = bank()
            MM([dict(out=b2[:, 0:n], lhsT=ones64, rhs=qsqB.ap[:, par, 0:n], start=True, stop=True)], [rsq, Rcb], [br2])
            RSTD(qrsB.ap[:, par, 0:n], b2[:, 0:n], [br2], [rrs])
            STT("dve", QTB.ap[:, i % 2, m, 0:n], qfB.ap[:, par, 0:n], vcol("gq2"), qrsB.ap[:, par, 0:n], ALU.mult, ALU.mult,
                [rqf, rrs, Rvt], [QTB.res(i % 2)])

    def attn_units(i, qoff, nb, sample):
        qp = i % 2
        RQ = QTB.res(qp)
        ocol = nb * 128
        if sample:
            parts = [(16, EBn)]
        else:
            parts = ([(nb - 1, EBp)] if nb > 0 else []) + [(nb, EBc)]
        units = []
        for gp in range(2):
            ctx = {}

            def banks_(ctx=ctx):
                if "bo" not in ctx:
                    ctx["bo"], ctx["bro"] = bank(hold=True)
                    ctx["bs"], ctx["brs"] = bank(hold=True)
                return ctx["bo"], ctx["bro"], ctx["bs"], ctx["brs"]

            for hf in range(2):
                g = 2 * gp + hf
                rows = slice(hf * 64, (hf + 1) * 64)
                for pi, (kblk, EBt) in enumerate(parts):
                    st = {}

                    def A(st=st, gp=gp, hf=hf, g=g, rows=rows, kblk=kblk, EBt=EBt):
                        ki = min(kblk // 4, 4)
                        bk, br = bank()
                        MM([dict(out=bk[:, j * 128:(j + 1) * 128], lhsT=KT[rows, gp, kblk * 128:(kblk + 1) * 128],
                                 rhs=QTB.ap[rows, qp, 4 * gp + j, qoff:qoff + 128], start=True, stop=True, tp=(hf * 64, 0))
                            for j in range(4)], [RKT[ki], RQ], [br])
                        par = est["n"] % 2
                        est["n"] += 1
                        rE, rPT = EB_.res(par), PTB.res(par)
                        ACT(EB_.ap[:, par, :], bk[:, :], AF.Exp, [br], [rE], scale=SCALE)
                        TTo("dve", PTB.ap[:, par, :], EB_.ap[:, par, :],
                            EBt.ap[:, 4 * g:4 * g + 4, :].rearrange("p a b -> p (a b)"), ALU.mult, [rE, EBt.res()], [rPT])
                        st["par"], st["rPT"] = par, rPT

                    def B(st=st, gp=gp, hf=hf, g=g, rows=rows, kblk=kblk, pi=pi, banks_=banks_):
                        bo, bro, bs, brs = banks_()
                        par, rPT = st["par"], st["rPT"]
                        lastp = (pi == len(parts) - 1) and not sample
                        MM([dict(out=bo[rows, :], lhsT=Vt[:, kblk, g * 64:(g + 1) * 64], rhs=PTB.ap[:, par, :],
                                 start=(pi == 0), stop=lastp, tp=(0, hf * 64), sgc=True)], [RVt[kblk], rPT], [bro])
                        MM([dict(out=bs[rows, :], lhsT=ones1[:, 0:64], rhs=PTB.ap[:, par, :],
                                 start=(pi == 0), stop=lastp, tp=(0, hf * 64), sgc=True)], [Rcb, rPT], [brs])

                    units.append([A, B])
                if sample:
                    st = {}

                    def A2(st=st, gp=gp, hf=hf, g=g, rows=rows):
                        bk, br = bank()
                        MM([dict(out=bk[:, (b * 4 + j) * 8:(b * 4 + j) * 8 + 8], lhsT=KcT.ap[rows, b, gp, :],
                                 rhs=QTB.ap[rows, qp, 4 * gp + j, 8 * b:8 * b + 8], start=True, stop=True, tp=(hf * 64, 0))
                            for b in range(16) for j in range(4)], [KcT.res(), RQ], [br])
                        par = est["n"] % 2
                        est["n"] += 1
                        rE, rPT = EB_.res(par), PTB.res(par)
                        ACT(EB_.ap[:, par, :], bk[:, :], AF.Exp, [br], [rE], scale=SCALE)
                        TTo("dve", PTB.ap[:, par, :], EB_.ap[:, par, :],
                            EBs.ap[:, g, :, :, :].rearrange("p a b c -> p (a b c)"), ALU.mult, [rE, EBs.res()], [rPT])
                        st["par"], st["rPT"] = par, rPT

                    def B2(st=st, gp=gp, hf=hf, g=g, rows=rows, banks_=banks_):
                        bo, bro, bs, brs = banks_()
                        par, rPT = st["par"], st["rPT"]
                        MM([dict(out=bo[rows, j * 128 + 8 * b:j * 128 + 8 * b + 8], lhsT=VcB.ap[:, b, g * 64:(g + 1) * 64],
                                 rhs=PTB.ap[:, par, (b * 4 + j) * 8:(b * 4 + j) * 8 + 8], start=False,
                                 stop=(b == 15 and j == 3), tp=(0, hf * 64), sgc=True) for b in range(16) for j in range(4)],
                           [VcB.res(), rPT], [bro])
                        MM([dict(out=bs[rows, j * 128 + 8 * b:j * 128 + 8 * b + 8], lhsT=ones1[:, 0:64],
                                 rhs=PTB.ap[:, par, (b * 4 + j) * 8:(b * 4 + j) * 8 + 8], start=False,
                                 stop=(b == 15 and j == 3), tp=(0, hf * 64), sgc=True) for b in range(16) for j in range(4)],
                           [Rcb, rPT], [brs])

                    units.append([A2, B2])

            def N(gp=gp, banks_=banks_):
                bo, bro, bs, brs = banks_()
                rden = denB.res(gp)
                for j in range(4):
                    TS("dve", denB.ap[:, gp, j * 128:(j + 1) * 128], bs[:, j * 128:(j + 1) * 128],
                       esB.ap[:, 4 * gp + j:4 * gp + j + 1], None, ALU.add, None, [brs, Res_es], [rden])
                ACT(denB.ap[:, gp, :], denB.ap[:, gp, :], AF.Ln, [rden], [rden])
                ACT(denB.ap[:, gp, :], denB.ap[:, gp, :], AF.Exp, [rden], [rden], scale=-1.0)
                TTo("dve", xn[:, 4 * gp:4 * gp + 4, ocol:ocol + 128], bo[:, :].rearrange("p (a b) -> p a b", a=4),
                    denB.ap[:, gp, :].rearrange("p (a b) -> p a b", a=4), ALU.mult, [bro, rden], [Rxn[i]])
                release(bo)
                release(bs)

            units[-1].append(N)
        return units

    def run_units(units):
        if not units:
            return
        units[0][0]()
        for u in range(len(units)):
            if u + 1 < len(units):
                units[u + 1][0]()
            for f in units[u][1:]:
                f()

    qproj(4)
    run_units(attn_units(4, 0, 16, True))
    for b_ in (EBn, EBs, QTB, qfB, qsqB, qrsB):
        A.free(b_)
    QTB, qfB, qsqB, qrsB = alloc_q(512)

    def load_wo():
        wv_ = wview(0, 8, 1024)
        pairs = []
        for c in range(8):
            ha, hb_ = _chunk_heads(c)
            pairs.append((wv_[0:64, c, :], w_o[ha * 64:ha * 64 + 64, :]))
            pairs.append((wv_[64:128, c, :], w_o[hb_ * 64:hb_ * 64 + 64, :]))
        P.dma_many("pool", pairs, writes=[RW[0]])
        wqo["o"] = wv_

    load_wo()
    allu = []
    qproj(0)
    for i in range(4):
        for q4 in range(4):
            allu.extend(attn_units(i, q4 * 128, 4 * i + q4, False))
            if q4 == 1 and i + 1 < 4:
                allu.append([(lambda ii: (lambda: qproj(ii)))(i + 1), (lambda: None)])
    run_units(allu)
    for b_ in (QTB, qfB, qsqB, qrsB, EB_, PTB, denB, esB, KTB, VtB, EBc, EBp):
        A.free(b_)
    ffw[(1, 0)] = load_ffn(1, 0, 1)
    for i in [4, 0, 1, 2, 3]:
        proj_residual(wqo["o"], i, [RW[0]], None)
    if stop("attn"):
        return nc

    ffn(1, 1)
    ple_w[1] = load_ple(1, 1)
    ple(1, 1, ple_w[1][0], ple_w[1][1])

    youtB = A.alloc("yout", [2, D], F32)
    for blk in range(NBLK):
        i = min(blk // 4, 4)
        ry = youtB.res(blk % 2)
        for half in range(2):
            bk, br = bank()
            TR([(bk[:, q * 128:(q + 1) * 128], h[:, 4 * half + q, blk * 128:(blk + 1) * 128], identf) for q in range(4)],
               [Rh[i][c] for c in range(4 * half, 4 * half + 4)] + [Rc], [br])
            CP("act" if half == 0 else "dve", youtB.ap[:, blk % 2, half * 512:(half + 1) * 512], bk[:, :], [br], [ry])
        P.dma("sp", y[blk * 128:(blk + 1) * 128, :], youtB.ap[:, blk % 2, :], reads=[ry])

    P.finish()
    return nc


def _t5_bucket_np(d):
    d = np.asarray(d)
    df = np.maximum(d, 1).astype(np.float32)
    large = 16 + (np.log(df / np.float32(16)) / np.float32(math.log(128 / 16)) * np.float32(16)).astype(np.int32)
    large = np.minimum(large, 31)
    return np.where(d < 16, d, large)


def _host_consts():
    cst = np.zeros((128, 256), np.float32)
    cst[:, 0:128] = np.eye(128, dtype=np.float32)
    idx = np.arange(128)
    cst[:, 128:256] = (idx[:, None] // 8 == idx[None, :] // 8).astype(np.float32)
    ohu = np.zeros((33, LU), np.float32)
    for j in range(LU):
        d = j - 127
        if 0 <= d < 128:
            ohu[int(_t5_bucket_np(d)), j] = 1.0
        else:
            ohu[32, j] = 1.0
    return cst, ohu


_NC_CACHE = {}


def kernel(x_prompt, x_sample, state_conv, state_ffn, cache_k, cache_v, p_prompt, p_sample,
           g_mix, cm_w_pw1, cm_b_pw1, cm_w_dw, cm_b_dw, cm_ln_g, cm_ln_b, cm_w_pw2, cm_b_pw2,
           at_w_q, at_g_q, at_sinks, at_w_o, kv_g, kv_w_k, kv_w_v, kv_g_k, rel_bias,
           g_ffn, ffn_w_up, ffn_w_dw, ffn_b_dw, ffn_w_down, g_ple, ple_w_gate, ple_w_proj):
    f = lambda a: np.ascontiguousarray(np.asarray(a, dtype=np.float32))
    sinks = f(at_sinks).reshape(16)
    sink_rows = np.stack([np.concatenate([np.full(64, sinks[_chunk_heads(c)[0]], np.float32),
                                          np.full(64, sinks[_chunk_heads(c)[1]], np.float32)]) for c in range(8)])
    parts = {
        "g_mix": f(g_mix).ravel(), "b_pw1": f(cm_b_pw1).ravel(), "w_dw": f(cm_w_dw).ravel(), "b_dw": f(cm_b_dw).ravel(),
        "ln_g": f(cm_ln_g).ravel(), "ln_b": f(cm_ln_b).ravel(), "b_pw2": f(cm_b_pw2).ravel(), "kv_g": f(kv_g).ravel(),
        "g_ffn": f(g_ffn).ravel(), "f_w_dw": f(ffn_w_dw).ravel(), "f_b_dw": f(ffn_b_dw).ravel(), "g_ple": f(g_ple).ravel(),
        "gq2": np.tile(f(at_g_q).ravel(), 2), "gk2": np.tile(f(kv_g_k).ravel(), 2), "sinks": sink_rows.ravel(),
    }
    vecs = np.zeros((VROWS, 128), np.float32)
    for name, off in VOFF.items():
        v = parts[name].reshape(-1, 128)
        vecs[off:off + v.shape[0]] = v
    cst, ohu = _host_consts()
    shared = {
        "vecs": vecs, "cst": cst, "ohu": ohu, "rel_bias": f(rel_bias),
        "w_pw1": f(cm_w_pw1)[0], "w_pw2": f(cm_w_pw2)[0], "w_q": f(at_w_q)[0], "w_o": f(at_w_o)[0],
        "w_k": f(kv_w_k), "w_v": f(kv_w_v), "w_up": f(ffn_w_up), "w_down": f(ffn_w_down),
        "w_gate": f(ple_w_gate), "w_proj": f(ple_w_proj),
    }
    xp, xs = f(x_prompt), f(x_sample)
    pp, ps = f(p_prompt), f(p_sample)
    sc, sf = f(state_conv), f(state_ffn)
    ck, cv = f(cache_k), f(cache_v)
    in_maps = []
    for c in range(NCORES):
        sl = slice(16 * c, 16 * c + 16)
        m = dict(shared)
        m["x_all"] = np.concatenate([xp[c], xs[sl].reshape(128, D)], axis=0)
        m["p_all"] = np.concatenate([pp[:, c], ps[:, sl].reshape(2, 128, 256)], axis=1)
        m["st_conv"] = np.ascontiguousarray(sc[0, sl])
        m["st_ffn"] = np.ascontiguousarray(sf[:, sl].reshape(2, 32, 2 * DFF))
        m["cache_k"] = np.ascontiguousarray(ck[sl].reshape(16, 128, 256))
        m["cache_v"] = np.ascontiguousarray(cv[sl].reshape(16, 128, 256))
        in_maps.append(m)
    if "nc" not in _NC_CACHE:
        _NC_CACHE["nc"] = build_program()
    nc = _NC_CACHE["nc"]
    res = run_bass_kernel_spmd(nc, in_maps, core_ids=list(range(NCORES)))
    R = res.results
    y_prompt = np.stack([R[c]["y"][0:2048] for c in range(NCORES)])
    y_sample = np.concatenate([R[c]["y"][2048:].reshape(16, 8, D) for c in range(NCORES)], axis=0)
    conv_p = np.stack([R[c]["conv_p"] for c in range(NCORES)])[None]
    conv_s = np.concatenate([R[c]["conv_s"] for c in range(NCORES)], axis=0)[None]
    ffn_p = np.stack([R[c]["ffn_p"] for c in range(NCORES)], axis=1)
    ffn_s = np.concatenate([R[c]["ffn_s"].reshape(2, 16, 2, 2 * DFF) for c in range(NCORES)], axis=1)
    k_p = np.stack([R[c]["k_p"].reshape(128, 4, 64) for c in range(NCORES)])
    k_s = np.concatenate([R[c]["k_s"].reshape(16, 128, 4, 64) for c in range(NCORES)], axis=0)
    v_p = np.stack([R[c]["v_p"].reshape(128, 4, 64) for c in range(NCORES)])
    v_s = np.concatenate([R[c]["v_s"].reshape(16, 128, 4, 64) for c in range(NCORES)], axis=0)
    outs = (y_prompt, y_sample, conv_p, conv_s, ffn_p, ffn_s, k_p, k_s, v_p, v_s)
    return tuple(np.ascontiguousarray(o, dtype=np.float32) for o in outs)
```

```python
import math
import os
import numpy as np
import concourse.bass as bass
import concourse.mybir as mybir
from concourse.bass_utils import run_bass_kernel_spmd

F32 = mybir.dt.float32
BF16 = mybir.dt.bfloat16
U8 = mybir.dt.uint8
AF = mybir.ActivationFunctionType
ALU = mybir.AluOpType

NCORES = 8
D = 1024
DFF = 2816
NF = 44
T = 2176
TT = [(0, 512), (512, 512), (1024, 512), (1536, 512), (2048, 128)]
NBLK = 17
EPS = 1e-6
SCALE = 0.125
LU = 383
GROUPS = [(0, 3), (3, 3), (6, 3), (9, 3), (12, 3), (15, 3), (18, 2), (20, 2)]
SLOT = 9216
UW = 30 + 2048 + 16 * 38
SB0 = 2078


def _vec_layout():
    ents = [("g_mix", 16), ("b_pw1", 16), ("w_dw", 248), ("b_dw", 8), ("ln_g", 8), ("ln_b", 8), ("b_pw2", 8),
            ("kv_g", 8), ("g_ffn", 16), ("f_w_dw", 264), ("f_b_dw", 88), ("g_ple", 16), ("gq2", 1), ("gk2", 1),
            ("sinks", 8)]
    off = {}
    o = 0
    for n, c in ents:
        off[n] = o
        o += c
    return off, o


VOFF, VN = _vec_layout()
VROWS = 768


def _chunk_heads(c):
    gp, j = c // 4, c % 4
    return 8 * gp + j, 8 * gp + 4 + j


class Res:
    __slots__ = ("w", "r", "name")

    def __init__(self, name=""):
        self.w = []
        self.r = []
        self.name = name


class Prog:
    def __init__(self, nc, n_dma_sems=10):
        self.nc = nc
        self.names = ["pe", "act", "dve", "pool", "sp"]
        self.ops = {n: [] for n in self.names}
        self.cnt = {n: 0 for n in self.names}
        self.seen = {n: {} for n in self.names}
        self.sem = {}
        for n in ["pe", "act", "dve", "pool"]:
            self.sem[n] = nc.alloc_semaphore("s_" + n)
        self.dq = ["sp", "pool"]
        self.dsem = {q: [nc.alloc_semaphore(f"d_{q}{i}") for i in range(n_dma_sems)] for q in self.dq}
        self.dcnt = {q: [0] * n_dma_sems for q in self.dq}
        self.dnext = {q: 0 for q in self.dq}

    def _waits(self, eng, evs):
        best = {}
        for ev in evs:
            if ev is None:
                continue
            s, v = ev
            k = id(s)
            if k not in best or best[k][1] < v:
                best[k] = (s, v)
        out = []
        seen = self.seen[eng]
        for k, (s, v) in best.items():
            if seen.get(k, 0) >= v:
                continue
            seen[k] = v
            out.append((s, v))
        return out

    @staticmethod
    def _deps(reads, writes, extra):
        evs = list(extra)
        for r in reads:
            evs.extend(r.w)
        for w in writes:
            evs.extend(w.w)
            evs.extend(w.r)
        return evs

    @staticmethod
    def _commit(ev, reads, writes):
        evl = ev if isinstance(ev, list) else [ev]
        for r in reads:
            r.r.extend(evl)
        for w in writes:
            w.w = list(evl)
            w.r = []

    def op(self, eng, fn, reads=(), writes=(), extra=()):
        waits = self._waits(eng, self._deps(reads, writes, extra))
        self.cnt[eng] += 1
        sem = self.sem[eng]
        ev = (sem, self.cnt[eng])

        def run(e):
            for s, v in waits:
                e.wait_ge(s, v)
            fn(e).then_inc(sem, 1)

        self.ops[eng].append(run)
        self._commit(ev, reads, writes)
        return ev

    def dma(self, q, out, in_, reads=(), writes=(), extra=()):
        i = self.dnext[q]
        self.dnext[q] = (i + 1) % len(self.dsem[q])
        s = self.dsem[q][i]
        prev = self.dcnt[q][i]
        evs = self._deps(reads, writes, extra)
        if prev > 0:
            evs.append((s, prev))
        waits = self._waits(q, evs)
        self.dcnt[q][i] = prev + 16
        ev = (s, prev + 16)

        def run(e):
            for s2, v in waits:
                e.wait_ge(s2, v)
            e.dma_start(out=out, in_=in_).then_inc(s, 16)

        self.ops[q].append(run)
        self._commit(ev, reads, writes)
        return ev

    def dma_many(self, q, pairs, reads=(), writes=()):
        evs0 = self._deps(reads, writes, ())
        out_evs = []
        for (out, in_) in pairs:
            i = self.dnext[q]
            self.dnext[q] = (i + 1) % len(self.dsem[q])
            s = self.dsem[q][i]
            prev = self.dcnt[q][i]
            evs = list(evs0) + ([(s, prev)] if prev > 0 else [])
            waits = self._waits(q, evs)
            self.dcnt[q][i] = prev + 16
            out_evs.append((s, prev + 16))

            def run(e, waits=waits, out=out, in_=in_, s=s):
                for s2, v in waits:
                    e.wait_ge(s2, v)
                e.dma_start(out=out, in_=in_).then_inc(s, 16)

            self.ops[q].append(run)
        self._commit(out_evs, reads, writes)
        return out_evs

    def finish(self):
        nc = self.nc
        finals = [(s, c) for q in self.dq for s, c in zip(self.dsem[q], self.dcnt[q]) if c > 0]
        waits = self._waits("sp", finals)

        def fin(e):
            for s, v in waits:
                e.wait_ge(s, v)

        self.ops["sp"].append(fin)
        with nc.Block() as block:
            @block.tensor
            def _(e):
                for f in self.ops["pe"]:
                    f(e)

            @block.scalar
            def _(e):
                for f in self.ops["act"]:
                    f(e)

            @block.vector
            def _(e):
                for f in self.ops["dve"]:
                    f(e)

            @block.gpsimd
            def _(e):
                for f in self.ops["pool"]:
                    f(e)

            @block.sync
            def _(e):
                for f in self.ops["sp"]:
                    f(e)


class Buf:
    def __init__(self, ap, off, nbytes, inherit, name):
        self.ap = ap
        self.off = off
        self.nbytes = nbytes
        self.inherit = inherit
        self.name = name
        self.R = {}

    def res(self, key=None):
        if key not in self.R:
            r = Res(f"{self.name}:{key}")
            r.r = list(self.inherit)
            self.R[key] = r
        return self.R[key]

    def all_events(self):
        evs = list(self.inherit)
        for r in self.R.values():
            evs.extend(r.w)
            evs.extend(r.r)
        return evs


class Arena:
    def __init__(self, nc, nbytes):
        self.t = nc.alloc_sbuf_tensor("arena", [128, nbytes], U8)
        self.n = nbytes
        self.live = []
        self.dead = []
        self.peak = 0

    def alloc(self, name, shape, dtype):
        esz = 4 if dtype == F32 else 2
        n = 1
        for s in shape:
            n *= s
        nbytes = ((n * esz + 63) // 64) * 64
        spans = sorted((b.off, b.off + b.nbytes) for b in self.live)
        off = 0
        for a, b in spans:
            if off + nbytes <= a:
                break
            off = max(off, b)
        if off + nbytes > self.n:
            raise RuntimeError(f"arena OOM allocating {name} ({nbytes}); live={[(b.name, b.nbytes) for b in self.live]}")
        self.peak = max(self.peak, off + nbytes)
        inherit = []
        keep = []
        for (o2, n2, evs) in self.dead:
            if o2 < off + nbytes and off < o2 + n2:
                inherit.extend(evs)
                if not (off <= o2 and o2 + n2 <= off + nbytes):
                    keep.append((o2, n2, evs))
            else:
                keep.append((o2, n2, evs))
        self.dead = keep
        v = self.t[:, off:off + n * esz].bitcast(dtype)
        if len(shape) == 2:
            v = v.rearrange("p (a b) -> p a b", a=shape[0])
        elif len(shape) == 3:
            v = v.rearrange("p (a b c) -> p a b c", a=shape[0], b=shape[1])
        elif len(shape) == 4:
            v = v.rearrange("p (a b c d) -> p a b c d", a=shape[0], b=shape[1], c=shape[2])
        buf = Buf(v, off, nbytes, inherit, name)
        self.live.append(buf)
        return buf

    def free(self, buf):
        self.live.remove(buf)
        self.dead.append((buf.off, buf.nbytes, buf.all_events()))


def build_program():
    nc = bass.Bass("TRN2", target_bir_lowering=False)
    P = Prog(nc)

    def din(name, shape):
        return nc.dram_tensor(name, list(shape), F32, kind="ExternalInput").ap()

    def dout(name, shape):
        return nc.dram_tensor(name, list(shape), F32, kind="ExternalOutput").ap()

    x_all = din("x_all", [T, D])
    p_all = din("p_all", [2, T, 256])
    st_conv = din("st_conv", [16, 30, D])
    st_ffn = din("st_ffn", [2, 32, 2 * DFF])
    cache_k = din("cache_k", [16, 128, 256])
    cache_v = din("cache_v", [16, 128, 256])
    vecs = din("vecs", [VROWS, 128])
    cst = din("cst", [128, 256])
    ohu = din("ohu", [33, LU])
    rel_bias = din("rel_bias", [32, 16])
    w_pw1 = din("w_pw1", [D, 2 * D])
    w_pw2 = din("w_pw2", [D, D])
    w_q = din("w_q", [D, D])
    w_o = din("w_o", [D, D])
    w_k = din("w_k", [D, 256])
    w_v = din("w_v", [D, 256])
    w_up = din("w_up", [2, D, 2 * DFF])
    w_down = din("w_down", [2, DFF, D])
    w_gate = din("w_gate", [2, D, D])
    w_proj = din("w_proj", [2, 256, D])

    y = dout("y", [T, D])
    conv_p = dout("conv_p", [30, D])
    conv_s = dout("conv_s", [16, 30, D])
    ffn_p = dout("ffn_p", [2, 2, 2 * DFF])
    ffn_s = dout("ffn_s", [2, 32, 2 * DFF])
    k_p = dout("k_p", [128, 256])
    k_s = dout("k_s", [16, 128, 256])
    v_p = dout("v_p", [128, 256])
    v_s = dout("v_s", [16, 128, 256])
    sc2 = nc.dram_tensor("sc2", [16, 128 * (LU + 1)], F32, kind="Internal")

    A = Arena(nc, 211968)
    _stop = os.environ.get("KSTOP", "")

    def stop(tag):
        if _stop == tag:
            P.finish()
            return True
        return False
    banks = [nc.alloc_psum_tensor(f"bank{i}", [128, 512], F32) for i in range(8)]
    bank_res = [Res(f"bank{i}") for i in range(8)]
    bstate = {"n": 0, "held": set()}

    def bank(hold=False):
        while True:
            i = bstate["n"] % 8
            bstate["n"] += 1
            if i not in bstate["held"]:
                break
        if hold:
            bstate["held"].add(i)
        return banks[i], bank_res[i]

    def release(bk):
        bstate["held"].discard(banks.index(bk))

    def ACT(out, in_, func, reads, writes, bias=0.0, scale=1.0):
        return P.op("act", lambda e: e.activation(out=out, in_=in_, func=func, bias=bias, scale=scale), reads, writes)

    def TTo(eng, out, in0, in1, op, reads, writes):
        return P.op(eng, lambda e: e.tensor_tensor(out=out, in0=in0, in1=in1, op=op), reads, writes)

    def TS(eng, out, in0, s1, s2, op0, op1, reads, writes):
        if s2 is None:
            return P.op(eng, lambda e: e.tensor_scalar(out=out, in0=in0, scalar1=s1, scalar2=None, op0=op0), reads, writes)
        return P.op(eng, lambda e: e.tensor_scalar(out=out, in0=in0, scalar1=s1, scalar2=s2, op0=op0, op1=op1), reads, writes)

    def STT(eng, out, in0, scalar, in1, op0, op1, reads, writes, extra=()):
        return P.op(eng, lambda e: e.scalar_tensor_tensor(out=out, in0=in0, scalar=scalar, in1=in1, op0=op0, op1=op1),
                    reads, writes, extra)

    def RSTD(out, in_, reads, writes):
        ACT(out, in_, AF.Ln, list(reads) + [Rcb], writes, bias=epsc)
        return ACT(out, out, AF.Exp, writes, writes, scale=-0.5)

    def CP(eng, out, in_, reads, writes):
        if eng == "act":
            return P.op("act", lambda e: e.activation(out=out, in_=in_, func=AF.Copy), reads, writes)
        return P.op(eng, lambda e: e.tensor_copy(out=out, in_=in_), reads, writes)

    def MEMSET(eng, out, val, writes):
        return P.op(eng, lambda e: e.memset(out, val), (), writes)

    def MM(mms, reads, writes):
        def fn(e):
            last = None
            for m in mms:
                kw = {}
                if m.get("tp") is not None:
                    kw["tile_position"] = m["tp"]
                if m.get("sgc"):
                    kw["skip_group_check"] = True
                last = e.matmul(m["out"], lhsT=m["lhsT"], rhs=m["rhs"], start=m["start"], stop=m["stop"], **kw)
            return last
        return P.op("pe", fn, reads, writes)

    def TR(items, reads, writes):
        def fn(e):
            last = None
            for o, i, idn in items:
                last = e.transpose(o, i, idn)
            return last
        return P.op("pe", fn, reads, writes)

    def acc(out, pairs, tp=None):
        n = len(pairs)
        return [dict(out=out, lhsT=l, rhs=r, start=(k == 0), stop=(k == n - 1), tp=tp) for k, (l, r) in enumerate(pairs)]

    hB = A.alloc("h", [8, T], F32)
    h = hB.ap
    Rh = [[hB.res((i, c)) for c in range(8)] for i in range(5)]
    XN = {}

    def xn_alloc():
        XN["B"] = A.alloc("xn", [8, T], BF16)
        return XN["B"].ap, [XN["B"].res(i) for i in range(5)]

    def xn_free():
        A.free(XN["B"])

    xn, Rxn = xn_alloc()
    WB = A.alloc("W", [2 * SLOT], BF16)
    W = WB.ap
    RW = [WB.res(0), WB.res(1)]
    vtB = A.alloc("vt", [VROWS], F32)
    vt = vtB.ap
    Rvt = vtB.res()
    cB = A.alloc("consts", [2, 128], F32)
    identf = cB.ap[:, 0, :]
    bmask = cB.ap[:, 1, :]
    Rc = cB.res()
    cbB = A.alloc("constsb", [4, 128], BF16)
    identb = cbB.ap[:, 0, :]
    onesN = cbB.ap[:, 1, :]
    ones64 = cbB.ap[:, 2, :]
    ones1 = cbB.ap[:, 3, :]
    Rcb = cbB.res()

    def vcol(name, idx=0):
        c = VOFF[name] + idx
        return vt[:, c:c + 1]

    P.dma("sp", cB.ap[:, 0:2, :], cst.rearrange("p (a b) -> p a b", a=2), writes=[Rc])
    CP("dve", identb, identf, [Rc], [Rcb])
    MEMSET("dve", onesN, 1.0 / 1024.0, [Rcb])
    MEMSET("dve", ones64, 0.0, [Rcb])
    MEMSET("dve", cbB.ap[0:64, 2, 0:64], 1.0 / 64.0, [Rcb])
    MEMSET("dve", cbB.ap[64:128, 2, 64:128], 1.0 / 64.0, [Rcb])
    MEMSET("dve", ones1, 1.0, [Rcb])
    epsB = A.alloc("epsc", [1], F32)
    epsc = epsB.ap[:, 0:1]
    MEMSET("dve", epsc, EPS, [Rcb])

    vinB = A.alloc("vin", [6, 128], F32)
    Rvin = vinB.res()
    P.dma("sp", vinB.ap, vecs.rearrange("(a p) f -> p a f", p=128), writes=[Rvin])
    for half in range(2):
        bk, br = bank()
        TR([(bk[:, q * 128:(q + 1) * 128], vinB.ap[:, half * 3 + q, :], identf) for q in range(3)], [Rvin, Rc], [br])
        CP("dve", vt[:, half * 384:(half + 1) * 384], bk[:, 0:384], [br], [Rvt])
    A.free(vinB)

    def load_rows(dst3, src2, nk, slots):
        P.dma_many("pool", [(dst3[:, kc, :], src2[kc * 128:(kc + 1) * 128, :]) for kc in range(nk)], writes=slots)

    def wview(off, a, b):
        return W[:, off:off + a * b].rearrange("p (a b) -> p a b", a=a)

    def load_pw1():
        load_rows(wview(0, 8, 2048), w_pw1, 8, [RW[0], RW[1]])

    def load_pw2():
        load_rows(wview(0, 8, 1024), w_pw2, 8, [RW[0]])

    def load_ffn(l, gi, slot):
        j0, nj = GROUPS[gi]
        base = slot * SLOT
        wu = wview(base, 8, 2 * nj * 128)
        wd = wview(base + 16 * nj * 128, nj, 1024)
        pairs = []
        for kc in range(8):
            pairs.append((wu[:, kc, 0:nj * 128], w_up[l, kc * 128:(kc + 1) * 128, j0 * 128:(j0 + nj) * 128]))
            pairs.append((wu[:, kc, nj * 128:2 * nj * 128],
                          w_up[l, kc * 128:(kc + 1) * 128, DFF + j0 * 128:DFF + (j0 + nj) * 128]))
        for jj in range(nj):
            pairs.append((wd[:, jj, :], w_down[l, (j0 + jj) * 128:(j0 + jj + 1) * 128, :]))
        P.dma_many("pool", pairs, writes=[RW[slot]])
        return wu, wd

    def build_eb():
        tbB = A.alloc("tbl", [16], F32)
        Rtb = tbB.res()
        ohB = A.alloc("ohu", [LU], F32)
        Roh = ohB.res()
        UB = A.alloc("U", [LU], F32)
        RU = UB.res()
        MEMSET("dve", tbB.ap[32:33, :], -30000.0, [Rtb])
        P.dma("sp", tbB.ap[0:32, :], rel_bias, writes=[Rtb])
        P.dma("sp", ohB.ap[0:33, :], ohu, writes=[Roh])
        bk, br = bank()
        MM([dict(out=bk[0:16, 0:LU], lhsT=tbB.ap[0:33, :], rhs=ohB.ap[0:33, :], start=True, stop=True)], [Rtb, Roh], [br])
        ACT(UB.ap[0:16, :], bk[0:16, 0:LU], AF.Exp, [br], [RU])
        Rsc = Res("sc2")
        pst = UB.ap.ap[0][0]
        P.dma("sp", bass.AP(sc2, 0, [[128 * (LU + 1), 16], [LU + 1, 128], [1, LU]]),
              bass.AP(UB.ap.tensor, UB.ap.offset, [[pst, 16], [0, 128], [1, LU]]), reads=[RU], writes=[Rsc])
        EBc = A.alloc("EBc", [16, 128], BF16)
        EBp = A.alloc("EBp", [16, 128], BF16)
        P.dma("pool", EBc.ap, bass.AP(sc2, 127, [[LU, 128], [128 * (LU + 1), 16], [1, 128]]), reads=[Rsc], writes=[EBc.res()])
        P.dma("pool", EBp.ap, bass.AP(sc2, 255, [[LU, 128], [128 * (LU + 1), 16], [1, 128]]), reads=[Rsc], writes=[EBp.res()])
        A.free(tbB)
        A.free(ohB)
        A.free(UB)
        return EBc, EBp

    load_pw1()

    xinB = A.alloc("xin", [2, D], F32)
    for blk in range(NBLK):
        i = min(blk // 4, 4)
        rx = xinB.res(blk % 2)
        P.dma("sp", xinB.ap[:, blk % 2, :], x_all[blk * 128:(blk + 1) * 128, :], writes=[rx])
        for half in range(2):
            bk, br = bank()
            TR([(bk[:, q * 128:(q + 1) * 128], xinB.ap[:, blk % 2, (4 * half + q) * 128:(4 * half + q + 1) * 128], identf)
                for q in range(4)], [rx, Rc], [br])
            CP("act" if half == 0 else "dve", h[:, 4 * half:4 * half + 4, blk * 128:(blk + 1) * 128],
               bk[:, :].rearrange("p (a b) -> p a b", a=4), [br], [Rh[i][c] for c in range(4 * half, 4 * half + 4)])
    A.free(xinB)
    if stop("p0"):
        return nc

    def rmsnorm(i, gname, gidx):
        c0, n = TT[i]
        sq = A.alloc("sq", [8, n], BF16)
        rs = A.alloc("rstd", [n], F32)
        ACT(sq.ap, h[:, :, c0:c0 + n], AF.Square, Rh[i], [sq.res()])
        bk, br = bank()
        MM(acc(bk[:, 0:n], [(onesN, sq.ap[:, c, :]) for c in range(8)]), [sq.res(), Rcb], [br])
        RSTD(rs.ap, bk[:, 0:n], [br], [rs.res()])
        for c in range(8):
            STT("dve", xn[:, c, c0:c0 + n], h[:, c, c0:c0 + n], vcol(gname, gidx * 8 + c), rs.ap,
                ALU.mult, ALU.mult, [Rh[i][c], rs.res(), Rvt], [Rxn[i]])
        A.free(sq)
        A.free(rs)

    uhB = A.alloc("uh", [8, UW], BF16)
    uh = uhB.ap
    Ru = [uhB.res(i) for i in range(5)]
    Rhist = uhB.res("hist")
    ustB = A.alloc("ust", [8, 158], F32)
    Rust = ustB.res()
    MEMSET("pool", uh[:, :, 0:30], 0.0, [Ru[0]])

    P.dma("sp", conv_s[:, 0:22, :], st_conv[:, 8:30, :])
    scB = A.alloc("scin", [D], F32)
    for blk in range(4):
        rsb = scB.res()
        P.dma("sp", scB.ap[0:120, :], st_conv[4 * blk:4 * blk + 4, :, :].rearrange("b r d -> (b r) d"), writes=[rsb])
        for half in range(2):
            bk, br = bank()
            TR([(bk[:, q * 120:(q + 1) * 120], scB.ap[0:120, (4 * half + q) * 128:(4 * half + q + 1) * 128], identf[0:120, 0:120])
                for q in range(4)], [rsb, Rc], [br])
            for q in range(4):
                c = 4 * half + q
                dst = uh[:, c, SB0 + 38 * 4 * blk:SB0 + 38 * 4 * (blk + 1)].rearrange("p (b w) -> p b w", w=38)[:, :, 0:30]
                CP("dve" if q % 2 == 0 else "act", dst, bk[:, q * 120:(q + 1) * 120].rearrange("p (b r) -> p b r", r=30),
                   [br], [Rhist])
    A.free(scB)

    for i in range(5):
        rmsnorm(i, "g_mix", 0)

    wp1 = wview(0, 8, 2048)
    sgB = A.alloc("sg", [2, 512], F32)
    for i in range(5):
        c0, n = TT[i]
        for m in range(8):
            ba, bra = bank()
            bg, brg = bank()
            MM(acc(ba[:, 0:n], [(wp1[:, kc, m * 128:(m + 1) * 128], xn[:, kc, c0:c0 + n]) for kc in range(8)]),
               [RW[0], RW[1], Rxn[i]], [bra])
            MM(acc(bg[:, 0:n], [(wp1[:, kc, (8 + m) * 128:(9 + m) * 128], xn[:, kc, c0:c0 + n]) for kc in range(8)]),
               [RW[0], RW[1], Rxn[i]], [brg])
            rsg = sgB.res(m % 2)
            sg = sgB.ap[:, m % 2, 0:n]
            ACT(sg, bg[:, 0:n], AF.Sigmoid, [brg, Rvt], [rsg], bias=vcol("b_pw1", 8 + m))
            if i < 4:
                dst = uh[:, m, 30 + c0:30 + c0 + n]
                STT("dve", dst, ba[:, 0:n], vcol("b_pw1", m), sg, ALU.add, ALU.mult, [bra, rsg, Rvt], [Ru[i]])
                if i == 3:
                    STT("dve", ustB.ap[:, m, 0:30], ba[:, 482:512], vcol("b_pw1", m), sg[:, 482:512], ALU.add, ALU.mult,
                        [bra, rsg, Rvt], [Rust])
            else:
                dst = uh[:, m, SB0:UW].rearrange("p (b w) -> p b w", w=38)[:, :, 30:38]
                STT("dve", dst, ba[:, 0:128].rearrange("p (b t) -> p b t", t=8), vcol("b_pw1", m),
                    sg.rearrange("p (b t) -> p b t", t=8), ALU.add, ALU.mult, [bra, rsg, Rvt], [Ru[4]])
                STT("dve", ustB.ap[:, m, 30:158], ba[:, 0:128], vcol("b_pw1", m), sg, ALU.add, ALU.mult,
                    [bra, rsg, Rvt], [Rust])
    A.free(sgB)
    if stop("p1a"):
        return nc
    load_pw2()

    cpoB = A.alloc("cpo", [D], F32)
    for half in range(2):
        bk, br = bank()
        TR([(bk[0:30, q * 128:(q + 1) * 128], ustB.ap[:, 4 * half + q, 0:30], identf) for q in range(4)], [Rust, Rc], [br])
        CP("act", cpoB.ap[0:30, half * 512:(half + 1) * 512], bk[0:30, :], [br], [cpoB.res()])
    P.dma("sp", conv_p, cpoB.ap[0:30, :], reads=[cpoB.res()])
    csoB = A.alloc("cso", [D], F32)
    for half in range(2):
        bk, br = bank()
        TR([(bk[:, q * 128:(q + 1) * 128], ustB.ap[:, 4 * half + q, 30:158], identf) for q in range(4)], [Rust, Rc], [br])
        CP("act", csoB.ap[:, half * 512:(half + 1) * 512], bk[:, :], [br], [csoB.res()])
    for b in range(16):
        P.dma("sp", conv_s[b, 22:30, :], csoB.ap[8 * b:8 * b + 8, :], reads=[csoB.res()])
    A.free(cpoB)
    A.free(csoB)
    A.free(ustB)
    if stop("p1"):
        return nc

    xn_free()
    wp2 = wview(0, 8, 1024)
    ffw = {}
    c2B = A.alloc("c2t", [8, 512], BF16)
    Rc2 = c2B.res()
    DgB = A.alloc("Dg", [2, 31, 128], BF16)
    ccB = A.alloc("cc", [8, 512], F32)
    cbfB = A.alloc("cbf", [2, 512], BF16)
    sqcB = A.alloc("sqc", [2, 512], BF16)
    lnB = A.alloc("lnst", [3, 512], F32)
    wdw_pst = vt.ap[0][0]
    for i in range(5):
        c0, n = TT[i]
        bm, brm = bank(hold=True)
        bq, brq = bank(hold=True)
        for c in range(8):
            rdg = DgB.res(c % 2)
            NDV = 25
            for (eng_, k0, k1) in (("dve", 0, NDV), ("pool", NDV, 31)):
                in0 = bass.AP(identb.tensor, identb.offset, [[identb.ap[0][0], 128], [0, k1 - k0], [1, 128]])
                wsrc = vcol("w_dw", k0 * 8 + c)
                in1 = bass.AP(wsrc.tensor, wsrc.offset, [[wdw_pst, 128], [8, k1 - k0], [0, 128]])
                TTo(eng_, DgB.ap[:, c % 2, k0:k1, :], in0, in1, ALU.mult, [Rcb, Rvt], [DgB.res((c % 2, eng_))])
            bk, br = bank()
            if i < 4:
                rhs = [uh[:, c, c0 + k:c0 + k + n] for k in range(31)]
                rds = [Ru[i]] + ([Ru[i - 1]] if i > 0 else [])
            else:
                sv = uh[:, c, SB0:UW].rearrange("p (b w) -> p b w", w=38)
                rhs = [sv[:, :, k:k + 8] for k in range(31)]
                rds = [Ru[4], Rhist]
            outv = bk[:, 0:n] if i < 4 else bk[:, 0:128].rearrange("p (b t) -> p b t", t=8)
            MM(acc(outv, [(DgB.ap[:, c % 2, k, :], rhs[k]) for k in range(31)]),
               [DgB.res((c % 2, "dve")), DgB.res((c % 2, "pool"))] + rds, [br])
            rcc = ccB.res(c)
            ACT(ccB.ap[:, c, 0:n], bk[:, 0:n], AF.Identity, [br, Rvt], [rcc], bias=vcol("b_dw", c))
            ACT(cbfB.ap[:, c % 2, 0:n], bk[:, 0:n], AF.Identity, [br, Rvt], [cbfB.res(c % 2)], bias=vcol("b_dw", c))
            ACT(sqcB.ap[:, c % 2, 0:n], ccB.ap[:, c, 0:n], AF.Square, [rcc], [sqcB.res(c % 2)])
            MM([dict(out=bm[:, 0:n], lhsT=onesN, rhs=cbfB.ap[:, c % 2, 0:n], start=(c == 0), stop=(c == 7))],
               [cbfB.res(c % 2), Rcb], [brm])
            MM([dict(out=bq[:, 0:n], lhsT=onesN, rhs=sqcB.ap[:, c % 2, 0:n], start=(c == 0), stop=(c == 7))],
               [sqcB.res(c % 2), Rcb], [brq])
        mean = lnB.ap[:, 0, 0:n]
        m2 = lnB.ap[:, 1, 0:n]
        rstd = lnB.ap[:, 2, 0:n]
        rln = lnB.res()
        CP("act", mean, bm[:, 0:n], [brm], [rln])
        TTo("dve", m2, mean, mean, ALU.mult, [rln], [rln])
        TTo("dve", m2, bq[:, 0:n], m2, ALU.subtract, [brq, rln], [rln])
        RSTD(rstd, m2, [rln], [rln])
        release(bm)
        release(bq)
        for c in range(8):
            rcc = ccB.res(c)
            TTo("dve", ccB.ap[:, c, 0:n], ccB.ap[:, c, 0:n], mean, ALU.subtract, [rcc, rln], [rcc])
            TTo("dve", ccB.ap[:, c, 0:n], ccB.ap[:, c, 0:n], rstd, ALU.mult, [rcc, rln], [rcc])
            ACT(c2B.ap[:, c, 0:n], ccB.ap[:, c, 0:n], AF.Silu, [rcc, Rvt], [Rc2], bias=vcol("ln_b", c),
                scale=vcol("ln_g", c))
        if i == 3:
            ffw[(0, 0)] = load_ffn(0, 0, 1)
        for m in range(8):
            bk, br = bank()
            MM(acc(bk[:, 0:n], [(wp2[:, kc, m * 128:(m + 1) * 128], c2B.ap[:, kc, 0:n]) for kc in range(8)]),
               [RW[0], Rc2], [br])
            STT("dve", h[:, m, c0:c0 + n], bk[:, 0:n], vcol("b_pw2", m), h[:, m, c0:c0 + n], ALU.add, ALU.add,
                [br, Rvt], [Rh[i][m]])
    for b_ in (DgB, ccB, cbfB, sqcB, lnB, c2B):
        A.free(b_)
    A.free(uhB)
    xn, Rxn = xn_alloc()
    if stop("p3"):
        return nc

    def proj_residual(wv, i, rslots, bias_name=None):
        c0, n = TT[i]
        for m in range(8):
            bk, br = bank()
            MM(acc(bk[:, 0:n], [(wv[:, kc, m * 128:(m + 1) * 128], xn[:, kc, c0:c0 + n]) for kc in range(8)]),
               rslots + [Rxn[i]], [br])
            if bias_name is not None:
                STT("dve", h[:, m, c0:c0 + n], bk[:, 0:n], vcol(bias_name, m), h[:, m, c0:c0 + n], ALU.add, ALU.add,
                    [br, Rvt], [Rh[i][m]])
            else:
                TTo("dve", h[:, m, c0:c0 + n], bk[:, 0:n], h[:, m, c0:c0 + n], ALU.add, [br], [Rh[i][m]])


    def ffn(l, slot0, after_first_group=None):
        for i in range(5):
            rmsnorm(i, "g_ffn", l)
        fstB = A.alloc("fst", [NF, 34], F32)
        Rfst = fstB.res()
        upB = A.alloc("uprev", [3, NF, 2], F32)
        hsB = A.alloc("hs", [NF, 32], F32)
        Rhs = hsB.res()
        sfB = A.alloc("sfin", [1408], F32)
        for q4 in range(4):
            rsf = sfB.res()
            P.dma("sp", sfB.ap[0:32, :], st_ffn[l, :, q4 * 1408:(q4 + 1) * 1408], writes=[rsf])
            bk, br = bank()
            TR([(bk[:, q * 32:(q + 1) * 32], sfB.ap[0:32, q * 128:(q + 1) * 128], identf[0:32, 0:32]) for q in range(11)],
               [rsf, Rc], [br])
            CP("dve", hsB.ap[:, q4 * 11:(q4 + 1) * 11, :], bk[:, 0:352].rearrange("p (a b) -> p a b", a=11), [br], [Rhs])
        A.free(sfB)
        cgB = A.alloc("cg", [2, 2, 512], F32)
        sgtB = A.alloc("sgt", [2, 512], F32)
        actB = A.alloc("act", [2, 4, 512], BF16)
        units = [(gi, i) for gi in range(len(GROUPS)) for i in range(5)]

        def unit_ctx(k):
            gi, i = units[k]
            j0, nj = GROUPS[gi]
            slot = (slot0 + gi) % 2
            wu, wd = ffw[(l, gi)]
            c0, n = TT[i]
            return gi, i, j0, nj, slot, wu, wd, c0, n, actB.res(k % 2)

        def emit_up(k):
            gi, i, j0, nj, slot, wu, wd, c0, n, ract = unit_ctx(k)
            for jj in range(nj):
                j = j0 + jj
                par = (jj + i) % 2
                cvals = []
                for gv in range(2):
                    f = j + 22 * gv
                    bk, br = bank()
                    MM(acc(bk[:, 0:n], [(wu[:, kc, (gv * nj + jj) * 128:(gv * nj + jj + 1) * 128], xn[:, kc, c0:c0 + n])
                                        for kc in range(8)]), [RW[slot], Rxn[i]], [br])
                    rcg = cgB.res((par, gv))
                    cg = cgB.ap[:, par, gv, 0:n]
                    w0 = vcol("f_w_dw", (l * 3 + 0) * NF + f)
                    w1 = vcol("f_w_dw", (l * 3 + 1) * NF + f)
                    w2 = vcol("f_w_dw", (l * 3 + 2) * NF + f)
                    bb = vcol("f_b_dw", l * NF + f)
                    ACT(cg, bk[:, 0:n], AF.Identity, [br, Rvt], [rcg], bias=bb, scale=w2)
                    if i < 4:
                        if i < 3:
                            rup = upB.res((i, f))
                            evc = CP("act", upB.ap[:, i, f, :], bk[:, n - 2:n], [br], [rup])
                        else:
                            evc = CP("act", fstB.ap[:, f, 0:2], bk[:, n - 2:n], [br], [Rfst])
                        STT("dve", cg[:, 1:n], bk[:, 0:n - 1], w1, cg[:, 1:n], ALU.mult, ALU.add, [br, Rvt], [rcg], extra=[evc])
                        STT("dve", cg[:, 2:n], bk[:, 0:n - 2], w0, cg[:, 2:n], ALU.mult, ALU.add, [br, Rvt], [rcg])
                        if i > 0:
                            rprev = upB.res((i - 1, f))
                            ext = upB.ap[:, i - 1, f, :]
                            STT("dve", cg[:, 0:2], ext, w0, cg[:, 0:2], ALU.mult, ALU.add, [rprev, Rvt], [rcg])
                            STT("dve", cg[:, 0:1], ext[:, 1:2], w1, cg[:, 0:1], ALU.mult, ALU.add, [rprev, Rvt], [rcg])
                    else:
                        cg3 = cg.rearrange("p (b t) -> p b t", t=8)
                        bk3 = bk[:, 0:128].rearrange("p (b t) -> p b t", t=8)
                        hs3 = hsB.ap[:, f, :].rearrange("p (b r) -> p b r", r=2)
                        evc = CP("act", fstB.ap[:, f, 2:34].rearrange("p (b r) -> p b r", r=2), bk3[:, :, 6:8], [br], [Rfst])
                        STT("dve", cg3[:, :, 1:8], bk3[:, :, 0:7], w1, cg3[:, :, 1:8], ALU.mult, ALU.add, [br, Rvt], [rcg], extra=[evc])
                        STT("dve", cg3[:, :, 2:8], bk3[:, :, 0:6], w0, cg3[:, :, 2:8], ALU.mult, ALU.add, [br, Rvt], [rcg])
                        STT("dve", cg3[:, :, 0:2], hs3, w0, cg3[:, :, 0:2], ALU.mult, ALU.add, [Rhs, Rvt], [rcg])
                        STT("dve", cg3[:, :, 0:1], hs3[:, :, 1:2], w1, cg3[:, :, 0:1], ALU.mult, ALU.add, [Rhs, Rvt], [rcg])
                    cvals.append((cg, rcg))
                rsg = sgtB.res(par)
                sgt = sgtB.ap[:, par, 0:n]
                ACT(sgt, cvals[0][0], AF.Silu, [cvals[0][1]], [rsg])
                TTo("pool", actB.ap[:, k % 2, jj, 0:n], sgt, cvals[1][0], ALU.mult, [rsg, cvals[1][1]], [ract])

        def emit_down(k):
            gi, i, j0, nj, slot, wu, wd, c0, n, ract = unit_ctx(k)
            for m in range(8):
                bk, br = bank()
                MM(acc(bk[:, 0:n], [(wd[:, jj, m * 128:(m + 1) * 128], actB.ap[:, k % 2, jj, 0:n]) for jj in range(nj)]),
                   [RW[slot], ract], [br])
                TTo("dve", h[:, m, c0:c0 + n], bk[:, 0:n], h[:, m, c0:c0 + n], ALU.add, [br], [Rh[i][m]])

        if len(GROUPS) > 1:
            ffw[(l, 1)] = load_ffn(l, 1, (slot0 + 1) % 2)
        emit_up(0)
        for k in range(len(units)):
            if k + 1 < len(units):
                emit_up(k + 1)
            emit_down(k)
            gi, i = units[k]
            if i == 4 and gi + 2 < len(GROUPS):
                ffw[(l, gi + 2)] = load_ffn(l, gi + 2, (slot0 + gi) % 2)
        for b_ in (cgB, sgtB, actB, upB, hsB):
            A.free(b_)
        ostB = A.alloc("ost", [1408], F32)
        for q4 in range(4):
            rost = ostB.res()
            for t3 in range(3):
                nq = 4 if t3 < 2 else 3
                bk, br = bank()
                TR([(bk[0:34, q * 128:(q + 1) * 128], fstB.ap[:, q4 * 11 + t3 * 4 + q, :], identf) for q in range(nq)],
                   [Rfst, Rc], [br])
                CP("act", ostB.ap[0:34, t3 * 512:t3 * 512 + nq * 128], bk[0:34, 0:nq * 128], [br], [rost])
            P.dma("sp", ffn_p[l, :, q4 * 1408:(q4 + 1) * 1408], ostB.ap[0:2, :], reads=[rost])
            P.dma("sp", ffn_s[l, :, q4 * 1408:(q4 + 1) * 1408], ostB.ap[2:34, :], reads=[rost])
        A.free(ostB)
        A.free(fstB)

    def load_ple(l, slot):
        wg = wview(slot * SLOT, 8, 1024)
        wp = wview((1 - slot) * SLOT + 4096, 2, 1024)
        load_rows(wg, w_gate[l], 8, [RW[slot]])
        load_rows(wp, w_proj[l], 2, [RW[1 - slot]])
        return wg, wp

    def ple(l, slot, wg, wp, prefetch=None):
        pTB = A.alloc("pT", [2, T], BF16)
        pinB = A.alloc("pin", [2, 256], F32)
        RpT = [pTB.res(i) for i in range(5)]
        for blk in range(NBLK):
            i = min(blk // 4, 4)
            rp = pinB.res(blk % 2)
            P.dma("sp", pinB.ap[:, blk % 2, :], p_all[l, blk * 128:(blk + 1) * 128, :], writes=[rp])
            bk, br = bank()
            TR([(bk[:, q * 128:(q + 1) * 128], pinB.ap[:, blk % 2, q * 128:(q + 1) * 128], identf) for q in range(2)],
               [rp, Rc], [br])
            CP("act", pTB.ap[:, :, blk * 128:(blk + 1) * 128], bk[:, 0:256].rearrange("p (a b) -> p a b", a=2), [br], [RpT[i]])
        A.free(pinB)
        for i in range(5):
            rmsnorm(i, "g_ple", l)
        if prefetch is not None:
            prefetch()
        gtB = A.alloc("gt", [2, 512], F32)
        tmB = A.alloc("tm", [2, 512], F32)
        for i in range(5):
            c0, n = TT[i]
            for m in range(8):
                bg, brg = bank()
                bp, brp = bank()
                MM(acc(bg[:, 0:n], [(wg[:, kc, m * 128:(m + 1) * 128], xn[:, kc, c0:c0 + n]) for kc in range(8)]),
                   [RW[slot], Rxn[i]], [brg])
                MM(acc(bp[:, 0:n], [(wp[:, kc, m * 128:(m + 1) * 128], pTB.ap[:, kc, c0:c0 + n]) for kc in range(2)]),
                   [RW[1 - slot], RpT[i]], [brp])
                rgt = gtB.res(m % 2)
                rtm = tmB.res(m % 2)
                ACT(gtB.ap[:, m % 2, 0:n], bg[:, 0:n], AF.Sigmoid, [brg], [rgt])
                TTo("dve", tmB.ap[:, m % 2, 0:n], bp[:, 0:n], gtB.ap[:, m % 2, 0:n], ALU.mult, [brp, rgt], [rtm])
                TTo("pool", h[:, m, c0:c0 + n], h[:, m, c0:c0 + n], tmB.ap[:, m % 2, 0:n], ALU.add, [rtm], [Rh[i][m]])
        for b_ in (gtB, tmB, pTB):
            A.free(b_)

    ple_w = {}

    def pf_ple0():
        ple_w[0] = load_ple(0, 1)

    ffn(0, 1, after_first_group=None)
    EBc, EBp = build_eb()
    if stop("ffn0"):
        return nc
    pf_ple0()

    kvw = {}

    def load_kv():
        base = 0
        wk = wview(base, 8, 256)
        wv = wview(base + 2048, 8, 256)
        P.dma_many("pool", [(wk[:, kc, :], w_k[kc * 128:(kc + 1) * 128, :]) for kc in range(8)] +
                   [(wv[:, kc, :], w_v[kc * 128:(kc + 1) * 128, :]) for kc in range(8)], writes=[RW[0]])
        kvw["k"], kvw["v"] = wk, wv

    ple(0, 1, ple_w[0][0], ple_w[0][1], prefetch=load_kv)
    if stop("ple0"):
        return nc

    KTB = A.alloc("KT", [2, T], BF16)
    KT = KTB.ap
    RKT = [KTB.res(i) for i in range(5)]
    VtB = A.alloc("Vtok", [NBLK, 256], BF16)
    Vt = VtB.ap
    RVt = [VtB.res(b) for b in range(NBLK)]
    for i in range(5):
        rmsnorm(i, "kv_g", 0)

    wqo = {}

    def load_wq():
        wqo["q"] = wview(SLOT, 8, 1024)
        stg = A.alloc("wqstg", [8, 1024], BF16)
        load_rows(stg.ap, w_q, 8, [stg.res()])
        for gp in range(2):
            for hf in range(2):
                dst = wqo["q"][:, :, gp * 512:(gp + 1) * 512].rearrange("p k (j hf d) -> p k j hf d", j=4, hf=2)[:, :, :, hf, :]
                src = stg.ap[:, :, gp * 512:(gp + 1) * 512].rearrange("p k (hf j d) -> p k hf j d", hf=2, j=4)[:, :, hf, :, :]
                CP("pool", dst, src, [stg.res()], [RW[1]])
        A.free(stg)

    load_wq()
    if stop("kv0"):
        return nc
    knoB = A.alloc("kno", [2, 256], F32)
    Rkno = knoB.res()
    kfB = A.alloc("kf", [2, 512], F32)
    ksqB = A.alloc("ksq", [2, 512], BF16)
    krsB = A.alloc("krs", [2, 512], F32)
    for i in range(5):
        c0, n = TT[i]
        for gp in range(2):
            bk, br = bank()
            MM(acc(bk[:, 0:n], [(kvw["k"][:, kc, gp * 128:(gp + 1) * 128], xn[:, kc, c0:c0 + n]) for kc in range(8)]),
               [RW[0], Rxn[i]], [br])
            rkf, rsq, rrs = kfB.res(gp), ksqB.res(gp), krsB.res(gp)
            kf = kfB.ap[:, gp, 0:n]
            CP("act", kf, bk[:, 0:n], [br], [rkf])
            ACT(ksqB.ap[:, gp, 0:n], bk[:, 0:n], AF.Square, [br], [rsq])
            b2, br2 = bank()
            MM([dict(out=b2[:, 0:n], lhsT=ones64, rhs=ksqB.ap[:, gp, 0:n], start=True, stop=True)], [rsq, Rcb], [br2])
            RSTD(krsB.ap[:, gp, 0:n], b2[:, 0:n], [br2], [rrs])
            STT("dve", KT[:, gp, c0:c0 + n], kf, vcol("gk2"), krsB.ap[:, gp, 0:n], ALU.mult, ALU.mult, [rkf, rrs, Rvt], [RKT[i]])
            if i == 3:
                STT("dve", knoB.ap[:, gp, 0:128], kf[:, 384:512], vcol("gk2"), krsB.ap[:, gp, 384:512], ALU.mult, ALU.mult,
                    [rkf, rrs, Rvt], [Rkno])
            if i == 4:
                STT("dve", knoB.ap[:, gp, 128:256], kf, vcol("gk2"), krsB.ap[:, gp, 0:128], ALU.mult, ALU.mult,
                    [rkf, rrs, Rvt], [Rkno])
    for b_ in (kfB, ksqB, krsB):
        A.free(b_)
    if stop("kv1"):
        return nc
    voB = A.alloc("vo", [2, 256], F32)
    for blk in range(NBLK):
        i = min(blk // 4, 4)
        bk, br = bank()
        MM(acc(bk[:, 0:256], [(xn[:, kc, blk * 128:(blk + 1) * 128], kvw["v"][:, kc, :]) for kc in range(8)]),
           [RW[0], Rxn[i]], [br])
        CP("act", Vt[:, blk, :], bk[:, 0:256], [br], [RVt[blk]])
        if blk >= 15 and os.environ.get("KSUB", "") != "a":
            rvo = voB.res(blk - 15)
            CP("act", voB.ap[:, blk - 15, :], bk[:, 0:256], [br], [rvo])
            if os.environ.get("KSUB", "") == "c":
                pass
            elif blk == 15:
                P.dma("sp", v_p, voB.ap[:, 0, :], reads=[rvo])
            elif os.environ.get("KSUB", "") != "b":
                for b in range(16):
                    P.dma("sp", v_s[b, 120:128, :], voB.ap[8 * b:8 * b + 8, 1, :], reads=[rvo])
    A.free(voB)
    if stop("kv2"):
        return nc
    koB = A.alloc("ko", [2, 256], F32)
    for w in range(2):
        bk, br = bank()
        TR([(bk[:, gp * 128:(gp + 1) * 128], knoB.ap[:, gp, w * 128:(w + 1) * 128], identf) for gp in range(2)], [Rkno, Rc], [br])
        rko = koB.res(w)
        CP("act", koB.ap[:, w, :], bk[:, 0:256], [br], [rko])
        if w == 0:
            P.dma("sp", k_p, koB.ap[:, 0, :], reads=[rko])
        else:
            for b in range(16):
                P.dma("sp", k_s[b, 120:128, :], koB.ap[8 * b:8 * b + 8, 1, :], reads=[rko])
    A.free(koB)
    A.free(knoB)
    P.dma("sp", k_s[:, 0:120, :], cache_k[:, 8:128, :])
    P.dma("sp", v_s[:, 0:120, :], cache_v[:, 8:128, :])
    if stop("kv"):
        return nc

    for i in range(5):
        rmsnorm(i, "g_mix", 1)
    esB = A.alloc("es", [8], F32)
    Res_es = esB.res()
    ACT(esB.ap, vt[:, VOFF["sinks"]:VOFF["sinks"] + 8], AF.Exp, [Rvt], [Res_es])

    EB_ = A.alloc("E", [2, 512], F32)
    PTB = A.alloc("PT", [3, 512], BF16)
    denB = A.alloc("den", [2, 512], F32)
    EBn = A.alloc("EBn", [16, 128], BF16)
    for hh in range(16):
        TTo("pool", EBn.ap[:, hh, :], EBc.ap[:, hh, :], bmask, ALU.mult, [EBc.res(), Rc], [EBn.res()])
    EBs = A.alloc("EBs", [4, 16, 4, 8], BF16)
    for b in range(16):
        CP("pool", EBs.ap[:, :, b, :, :], EBp.ap[:, :, 0:8].rearrange("p (g j) q -> p g j q", g=4), [EBp.res()], [EBs.res()])
    class _View:
        def __init__(self, ap, r):
            self.ap = ap
            self._r = r

        def res(self, key=None):
            return self._r

    KcT = _View(W[:, 0:4096].rearrange("p (b g s) -> p b g s", b=16, g=2), RW[0])
    VcB = _View(W[:, 4096:8192].rearrange("p (b f) -> p b f", b=16), RW[0])
    P.dma("pool", VcB.ap, cache_v.rearrange("b s f -> s b f"), writes=[VcB.res()])
    ckB = A.alloc("ck", [8, 256], F32)
    for hb in range(2):
        rck = ckB.res()
        P.dma("sp", ckB.ap, cache_k[8 * hb:8 * hb + 8].rearrange("b s f -> s b f"), writes=[rck])
        for b2 in range(0, 8, 2):
            bk, br = bank()
            TR([(bk[:, (bb * 2 + gp) * 128:(bb * 2 + gp + 1) * 128], ckB.ap[:, b2 + bb, gp * 128:(gp + 1) * 128], identf)
                for bb in range(2) for gp in range(2)], [rck, Rc], [br])
            CP("act", KcT.ap[:, 8 * hb + b2:8 * hb + b2 + 2, :, :], bk[:, :].rearrange("p (b g s) -> p b g s", b=2, g=2),
               [br], [KcT.res()])
    A.free(ckB)

    def alloc_q(n):
        return (A.alloc("QT", [2, 8, n], BF16), A.alloc("qf", [2, n], F32), A.alloc("qsq", [2, n], BF16),
                A.alloc("qrs", [2, n], F32))

    QTB, qfB, qsqB, qrsB = alloc_q(128)
    est = {"n": 0}

    def qproj(i):
        c0, n = TT[i]
        for m in range(8):
            ha, hb_ = _chunk_heads(m)
            bk, br = bank()
            lw = wqo["q"]
            MM(acc(bk[:, 0:n], [(lw[:, kc, m * 128:(m + 1) * 128], xn[:, kc, c0:c0 + n]) for kc in range(8)]),
               [RW[1], Rxn[i]], [br])
            par = m % 2
            rqf, rsq, rrs = qfB.res(par), qsqB.res(par), qrsB.res(par)
            CP("act", qfB.ap[:, par, 0:n], bk[:, 0:n], [br], [rqf])
            ACT(qsqB.ap[:, par, 0:n], bk[:, 0:n], AF.Square, [br], [rsq])
            b2, br2 = bank()
            MM([dict(out=b2[:, 0:n], lhsT=ones64, rhs=qsqB.ap[:, par, 0:n], start=True, stop=True)], [rsq, Rcb], [br2])
            RSTD(qrsB.ap[:, par, 0:n], b2[:, 0:n], [br2], [rrs])
            STT("dve", QTB.ap[:, i % 2, m, 0:n], qfB.ap[:, par, 0:n], vcol("gq2"), qrsB.ap[:, par, 0:n], ALU.mult, ALU.mult,
                [rqf, rrs, Rvt], [QTB.res(i % 2)])

    def attn_units(i, qoff, nb, sample):
        qp = i % 2
        RQ = QTB.res(qp)
        ocol = nb * 128
        if sample:
            parts = [(16, EBn)]
        else:
            parts = ([(nb - 1, EBp)] if nb > 0 else []) + [(nb, EBc)]
        units = []
        for gp in range(2):
            ctx = {}

            def banks_(ctx=ctx):
                if "bo" not in ctx:
                    ctx["bo"], ctx["bro"] = bank(hold=True)
                    ctx["bs"], ctx["brs"] = bank(hold=True)
                return ctx["bo"], ctx["bro"], ctx["bs"], ctx["brs"]

            for hf in range(2):
                g = 2 * gp + hf
                rows = slice(hf * 64, (hf + 1) * 64)
                for pi, (kblk, EBt) in enumerate(parts):
                    st = {}

                    def A(st=st, gp=gp, hf=hf, g=g, rows=rows, kblk=kblk, EBt=EBt):
                        ki = min(kblk // 4, 4)
                        bk, br = bank()
                        MM([dict(out=bk[:, j * 128:(j + 1) * 128], lhsT=KT[rows, gp, kblk * 128:(kblk + 1) * 128],
                                 rhs=QTB.ap[rows, qp, 4 * gp + j, qoff:qoff + 128], start=True, stop=True, tp=(hf * 64, 0))
                            for j in range(4)], [RKT[ki], RQ], [br])
                        par = est["n"] % 3
                        est["n"] += 1
                        pe2 = par % 2
                        rE, rPT = EB_.res(pe2), PTB.res(par)
                        ACT(EB_.ap[:, pe2, :], bk[:, :], AF.Exp, [br], [rE], scale=SCALE)
                        TTo("dve", PTB.ap[:, par, :], EB_.ap[:, pe2, :],
                            EBt.ap[:, 4 * g:4 * g + 4, :].rearrange("p a b -> p (a b)"), ALU.mult, [rE, EBt.res()], [rPT])
                        st["par"], st["rPT"] = par, rPT

                    def B(st=st, gp=gp, hf=hf, g=g, rows=rows, kblk=kblk, pi=pi, banks_=banks_):
                        bo, bro, bs, brs = banks_()
                        par, rPT = st["par"], st["rPT"]
                        lastp = (pi == len(parts) - 1) and not sample
                        MM([dict(out=bo[rows, :], lhsT=Vt[:, kblk, g * 64:(g + 1) * 64], rhs=PTB.ap[:, par, :],
                                 start=(pi == 0), stop=lastp, tp=(0, hf * 64), sgc=True)], [RVt[kblk], rPT], [bro])
                        MM([dict(out=bs[rows, :], lhsT=ones1[:, 0:64], rhs=PTB.ap[:, par, :],
                                 start=(pi == 0), stop=lastp, tp=(0, hf * 64), sgc=True)], [Rcb, rPT], [brs])

                    units.append([A, B])
                if sample:
                    st = {}

                    def A2(st=st, gp=gp, hf=hf, g=g, rows=rows):
                        bk, br = bank()
                        MM([dict(out=bk[:, (b * 4 + j) * 8:(b * 4 + j) * 8 + 8], lhsT=KcT.ap[rows, b, gp, :],
                                 rhs=QTB.ap[rows, qp, 4 * gp + j, 8 * b:8 * b + 8], start=True, stop=True, tp=(hf * 64, 0))
                            for b in range(16) for j in range(4)], [KcT.res(), RQ], [br])
                        par = est["n"] % 3
                        est["n"] += 1
                        pe2 = par % 2
                        rE, rPT = EB_.res(pe2), PTB.res(par)
                        ACT(EB_.ap[:, pe2, :], bk[:, :], AF.Exp, [br], [rE], scale=SCALE)
                        TTo("dve", PTB.ap[:, par, :], EB_.ap[:, pe2, :],
                            EBs.ap[:, g, :, :, :].rearrange("p a b c -> p (a b c)"), ALU.mult, [rE, EBs.res()], [rPT])
                        st["par"], st["rPT"] = par, rPT

                    def B2(st=st, gp=gp, hf=hf, g=g, rows=rows, banks_=banks_):
                        bo, bro, bs, brs = banks_()
                        par, rPT = st["par"], st["rPT"]
                        MM([dict(out=bo[rows, j * 128 + 8 * b:j * 128 + 8 * b + 8], lhsT=VcB.ap[:, b, g * 64:(g + 1) * 64],
                                 rhs=PTB.ap[:, par, (b * 4 + j) * 8:(b * 4 + j) * 8 + 8], start=False,
                                 stop=(b == 15 and j == 3), tp=(0, hf * 64), sgc=True) for b in range(16) for j in range(4)],
                           [VcB.res(), rPT], [bro])
                        MM([dict(out=bs[rows, j * 128 + 8 * b:j * 128 + 8 * b + 8], lhsT=ones1[:, 0:64],
                                 rhs=PTB.ap[:, par, (b * 4 + j) * 8:(b * 4 + j) * 8 + 8], start=False,
                                 stop=(b == 15 and j == 3), tp=(0, hf * 64), sgc=True) for b in range(16) for j in range(4)],
                           [Rcb, rPT], [brs])

                    units.append([A2, B2])

            def N(gp=gp, banks_=banks_):
                bo, bro, bs, brs = banks_()
                rden = denB.res(gp)
                for j in range(4):
                    TS("dve", denB.ap[:, gp, j * 128:(j + 1) * 128], bs[:, j * 128:(j + 1) * 128],
                       esB.ap[:, 4 * gp + j:4 * gp + j + 1], None, ALU.add, None, [brs, Res_es], [rden])
                ACT(denB.ap[:, gp, :], denB.ap[:, gp, :], AF.Ln, [rden], [rden])
                ACT(denB.ap[:, gp, :], denB.ap[:, gp, :], AF.Exp, [rden], [rden], scale=-1.0)
                TTo("dve", xn[:, 4 * gp:4 * gp + 4, ocol:ocol + 128], bo[:, :].rearrange("p (a b) -> p a b", a=4),
                    denB.ap[:, gp, :].rearrange("p (a b) -> p a b", a=4), ALU.mult, [bro, rden], [Rxn[i]])
                release(bo)
                release(bs)

            units[-1].append(N)
        return units

    def run_units(units):
        if not units:
            return
        LA = 2
        for u in range(min(LA, len(units))):
            units[u][0]()
        for u in range(len(units)):
            if u + LA < len(units):
                units[u + LA][0]()
            for f in units[u][1:]:
                f()

    qproj(4)
    run_units(attn_units(4, 0, 16, True))
    for b_ in (EBn, EBs, QTB, qfB, qsqB, qrsB):
        A.free(b_)
    QTB, qfB, qsqB, qrsB = alloc_q(512)

    def load_wo():
        wv_ = wview(0, 8, 1024)
        pairs = []
        for c in range(8):
            ha, hb_ = _chunk_heads(c)
            pairs.append((wv_[0:64, c, :], w_o[ha * 64:ha * 64 + 64, :]))
            pairs.append((wv_[64:128, c, :], w_o[hb_ * 64:hb_ * 64 + 64, :]))
        P.dma_many("pool", pairs, writes=[RW[0]])
        wqo["o"] = wv_

    load_wo()
    allu = []
    qproj(0)
    for i in range(4):
        for q4 in range(4):
            allu.extend(attn_units(i, q4 * 128, 4 * i + q4, False))
            if q4 == 1 and i + 1 < 4:
                allu.append([(lambda ii: (lambda: qproj(ii)))(i + 1), (lambda: None)])
    run_units(allu)
    for b_ in (QTB, qfB, qsqB, qrsB, EB_, PTB, denB, esB, KTB, VtB, EBc, EBp):
        A.free(b_)
    ffw[(1, 0)] = load_ffn(1, 0, 1)
    for i in [4, 0, 1, 2, 3]:
        proj_residual(wqo["o"], i, [RW[0]], None)
    if stop("attn"):
        return nc

    ffn(1, 1)
    ple_w[1] = load_ple(1, 1)
    ple(1, 1, ple_w[1][0], ple_w[1][1])

    youtB = A.alloc("yout", [2, D], F32)
    for blk in range(NBLK):
        i = min(blk // 4, 4)
        ry = youtB.res(blk % 2)
        for half in range(2):
            bk, br = bank()
            TR([(bk[:, q * 128:(q + 1) * 128], h[:, 4 * half + q, blk * 128:(blk + 1) * 128], identf) for q in range(4)],
               [Rh[i][c] for c in range(4 * half, 4 * half + 4)] + [Rc], [br])
            CP("act" if half == 0 else "dve", youtB.ap[:, blk % 2, half * 512:(half + 1) * 512], bk[:, :], [br], [ry])
        P.dma("sp", y[blk * 128:(blk + 1) * 128, :], youtB.ap[:, blk % 2, :], reads=[ry])

    P.finish()
    return nc


def _t5_bucket_np(d):
    d = np.asarray(d)
    df = np.maximum(d, 1).astype(np.float32)
    large = 16 + (np.log(df / np.float32(16)) / np.float32(math.log(128 / 16)) * np.float32(16)).astype(np.int32)
    large = np.minimum(large, 31)
    return np.where(d < 16, d, large)


def _host_consts():
    cst = np.zeros((128, 256), np.float32)
    cst[:, 0:128] = np.eye(128, dtype=np.float32)
    idx = np.arange(128)
    cst[:, 128:256] = (idx[:, None] // 8 == idx[None, :] // 8).astype(np.float32)
    ohu = np.zeros((33, LU), np.float32)
    for j in range(LU):
        d = j - 127
        if 0 <= d < 128:
            ohu[int(_t5_bucket_np(d)), j] = 1.0
        else:
            ohu[32, j] = 1.0
    return cst, ohu


_NC_CACHE = {}


def kernel(x_prompt, x_sample, state_conv, state_ffn, cache_k, cache_v, p_prompt, p_sample,
           g_mix, cm_w_pw1, cm_b_pw1, cm_w_dw, cm_b_dw, cm_ln_g, cm_ln_b, cm_w_pw2, cm_b_pw2,
           at_w_q, at_g_q, at_sinks, at_w_o, kv_g, kv_w_k, kv_w_v, kv_g_k, rel_bias,
           g_ffn, ffn_w_up, ffn_w_dw, ffn_b_dw, ffn_w_down, g_ple, ple_w_gate, ple_w_proj):
    f = lambda a: np.ascontiguousarray(np.asarray(a, dtype=np.float32))
    sinks = f(at_sinks).reshape(16)
    sink_rows = np.stack([np.concatenate([np.full(64, sinks[_chunk_heads(c)[0]], np.float32),
                                          np.full(64, sinks[_chunk_heads(c)[1]], np.float32)]) for c in range(8)])
    parts = {
        "g_mix": f(g_mix).ravel(), "b_pw1": f(cm_b_pw1).ravel(), "w_dw": f(cm_w_dw).ravel(), "b_dw": f(cm_b_dw).ravel(),
        "ln_g": f(cm_ln_g).ravel(), "ln_b": f(cm_ln_b).ravel(), "b_pw2": f(cm_b_pw2).ravel(), "kv_g": f(kv_g).ravel(),
        "g_ffn": f(g_ffn).ravel(), "f_w_dw": f(ffn_w_dw).ravel(), "f_b_dw": f(ffn_b_dw).ravel(), "g_ple": f(g_ple).ravel(),
        "gq2": np.tile(f(at_g_q).ravel(), 2), "gk2": np.tile(f(kv_g_k).ravel(), 2), "sinks": sink_rows.ravel(),
    }
    vecs = np.zeros((VROWS, 128), np.float32)
    for name, off in VOFF.items():
        v = parts[name].reshape(-1, 128)
        vecs[off:off + v.shape[0]] = v
    cst, ohu = _host_consts()
    shared = {
        "vecs": vecs, "cst": cst, "ohu": ohu, "rel_bias": f(rel_bias),
        "w_pw1": f(cm_w_pw1)[0], "w_pw2": f(cm_w_pw2)[0], "w_q": f(at_w_q)[0], "w_o": f(at_w_o)[0],
        "w_k": f(kv_w_k), "w_v": f(kv_w_v), "w_up": f(ffn_w_up), "w_down": f(ffn_w_down),
        "w_gate": f(ple_w_gate), "w_proj": f(ple_w_proj),
    }
    xp, xs = f(x_prompt), f(x_sample)
    pp, ps = f(p_prompt), f(p_sample)
    sc, sf = f(state_conv), f(state_ffn)
    ck, cv = f(cache_k), f(cache_v)
    in_maps = []
    for c in range(NCORES):
        sl = slice(16 * c, 16 * c + 16)
        m = dict(shared)
        m["x_all"] = np.concatenate([xp[c], xs[sl].reshape(128, D)], axis=0)
        m["p_all"] = np.concatenate([pp[:, c], ps[:, sl].reshape(2, 128, 256)], axis=1)
        m["st_conv"] = np.ascontiguousarray(sc[0, sl])
        m["st_ffn"] = np.ascontiguousarray(sf[:, sl].reshape(2, 32, 2 * DFF))
        m["cache_k"] = np.ascontiguousarray(ck[sl].reshape(16, 128, 256))
        m["cache_v"] = np.ascontiguousarray(cv[sl].reshape(16, 128, 256))
        in_maps.append(m)
    if "nc" not in _NC_CACHE:
        _NC_CACHE["nc"] = build_program()
    nc = _NC_CACHE["nc"]
    res = run_bass_kernel_spmd(nc, in_maps, core_ids=list(range(NCORES)))
    R = res.results
    y_prompt = np.stack([R[c]["y"][0:2048] for c in range(NCORES)])
    y_sample = np.concatenate([R[c]["y"][2048:].reshape(16, 8, D) for c in range(NCORES)], axis=0)
    conv_p = np.stack([R[c]["conv_p"] for c in range(NCORES)])[None]
    conv_s = np.concatenate([R[c]["conv_s"] for c in range(NCORES)], axis=0)[None]
    ffn_p = np.stack([R[c]["ffn_p"] for c in range(NCORES)], axis=1)
    ffn_s = np.concatenate([R[c]["ffn_s"].reshape(2, 16, 2, 2 * DFF) for c in range(NCORES)], axis=1)
    k_p = np.stack([R[c]["k_p"].reshape(128, 4, 64) for c in range(NCORES)])
    k_s = np.concatenate([R[c]["k_s"].reshape(16, 128, 4, 64) for c in range(NCORES)], axis=0)
    v_p = np.stack([R[c]["v_p"].reshape(128, 4, 64) for c in range(NCORES)])
    v_s = np.concatenate([R[c]["v_s"].reshape(16, 128, 4, 64) for c in range(NCORES)], axis=0)
    outs = (y_prompt, y_sample, conv_p, conv_s, ffn_p, ffn_s, k_p, k_s, v_p, v_s)
    return tuple(np.ascontiguousarray(o, dtype=np.float32) for o in outs)
```
